# Optimizing a Trainium2 kernel written in Bass

```python
import math
import jax, jax.numpy as jnp
from jax import lax
import numpy as np

D_MODEL = 1024
BATCH = 4
SEQ = 4096
DEPTH = 1
DEC_BATCH = 8
DEC_SEQ = 16
PAST_LEN = 2048

CHUNK = 64
Q_BLOCK = 128
N_HEADS = 8
QK_NOPE = 64
ROPE_DIM = 32
V_DIM = 64
Q_LORA = 384
KV_LORA = 256
QK_DIM = QK_NOPE + ROPE_DIM
MLA_WIDTH = N_HEADS * V_DIM
S5_GROUP_SIZE = 16
S5_WIDTH = 512
S5_GROUPS = S5_WIDTH // S5_GROUP_SIZE
S5_STATE = 64
D_MIX = MLA_WIDTH + S5_WIDTH
D_IN = Q_LORA + KV_LORA + ROPE_DIM + S5_WIDTH
D_FF = 2816
CONV_W = 3
ROPE_THETA = 10000.0
EPS = 1e-6
ATTN_SCALE = 1.0 / math.sqrt(QK_DIM)
NEG_INF = -1e30

kernel_name = "hybrid_mla_s5_convffn_stream_step"


def _rmsnorm(x, g):
    xf = x.astype(jnp.float32)
    xf = xf * lax.rsqrt(jnp.mean(xf * xf, axis=-1, keepdims=True) + EPS)
    return (xf * g.astype(jnp.float32)).astype(x.dtype)


def _rope_tables(pos):
    inv_freq = ROPE_THETA ** (-jnp.arange(0, ROPE_DIM, 2, dtype=jnp.float32) / ROPE_DIM)
    ang = pos.astype(jnp.float32)[:, None] * inv_freq[None, :]
    return jnp.cos(ang), jnp.sin(ang)


def _rope(x, cos, sin):
    half = ROPE_DIM // 2
    x1 = x[..., :half].astype(jnp.float32)
    x2 = x[..., half:].astype(jnp.float32)
    out = jnp.concatenate([x1 * cos - x2 * sin, x2 * cos + x1 * sin], axis=-1)
    return out.astype(x.dtype)


def _mla_keys(c_kv, k_pe, p):
    B, T, _ = c_kv.shape
    kv = (c_kv @ p["w_kv_b"]).reshape(B, T, N_HEADS, QK_NOPE + V_DIM)
    k_nope, v = kv[..., :QK_NOPE], kv[..., QK_NOPE:]
    k_rot = jnp.broadcast_to(k_pe[:, :, None, :], (B, T, N_HEADS, ROPE_DIM))
    k = _rmsnorm(jnp.concatenate([k_nope, k_rot], axis=-1), p["g_k_head"])
    return k, v


def _attn_prompt(q, k, v):
    B, S, H, Dq = q.shape
    nb = S // Q_BLOCK
    qb = q.reshape(B, nb, Q_BLOCK, H, Dq).transpose(1, 0, 2, 3, 4)
    key_chunk = jnp.arange(S) // CHUNK

    def one(args):
        qi, i = args
        q_chunk = (i * Q_BLOCK + jnp.arange(Q_BLOCK)) // CHUNK
        s = jnp.einsum('bqhd,bkhd->bhqk', qi, k).astype(jnp.float32) * ATTN_SCALE
        mask = key_chunk[None, :] <= q_chunk[:, None]
        s = jnp.where(mask[None, None], s, NEG_INF)
        pr = jax.nn.softmax(s, axis=-1).astype(v.dtype)
        return jnp.einsum('bhqk,bkhd->bqhd', pr, v)

    o = lax.map(one, (qb, jnp.arange(nb)))
    return o.transpose(1, 0, 2, 3, 4).reshape(B, S, H * V_DIM)


def _attn_sample(q, k, v):
    B, S, H, _ = q.shape
    s = jnp.einsum('bqhd,bkhd->bhqk', q, k).astype(jnp.float32) * ATTN_SCALE
    pr = jax.nn.softmax(s, axis=-1).astype(v.dtype)
    return jnp.einsum('bhqk,bkhd->bqhd', pr, v).reshape(B, S, H * V_DIM)


def _s5(u, h0_re, h0_im, p):
    B, S, _ = u.shape
    ug = u.reshape(B, S, S5_GROUPS, S5_GROUP_SIZE)
    a_re, a_im = p["s5_a_re"], p["s5_a_im"]
    dt = jnp.exp(p["s5_log_dt"])[:, None]
    mag = jnp.exp(a_re * dt)
    ang = a_im * dt
    lb_re, lb_im = mag * jnp.cos(ang), mag * jnp.sin(ang)
    nr, ni = lb_re - 1.0, lb_im
    den = a_re * a_re + a_im * a_im
    f_re = (nr * a_re + ni * a_im) / den
    f_im = (ni * a_re - nr * a_im) / den
    b_re, b_im = p["s5_b_re"], p["s5_b_im"]
    bb_re = f_re[..., None] * b_re - f_im[..., None] * b_im
    bb_im = f_re[..., None] * b_im + f_im[..., None] * b_re
    bu_re = jnp.einsum('gpc,bsgc->bsgp', bb_re, ug)
    bu_im = jnp.einsum('gpc,bsgc->bsgp', bb_im, ug)
    la_re = jnp.broadcast_to(lb_re, bu_re.shape)
    la_im = jnp.broadcast_to(lb_im, bu_re.shape)

    def comb(e1, e2):
        a1r, a1i, b1r, b1i = e1
        a2r, a2i, b2r, b2i = e2
        return (a2r * a1r - a2i * a1i,
                a2r * a1i + a2i * a1r,
                a2r * b1r - a2i * b1i + b2r,
                a2r * b1i + a2i * b1r + b2i)

    ar, ai, sr, si = lax.associative_scan(comb, (la_re, la_im, bu_re, bu_im), axis=1)
    if h0_re is not None:
        h_re, h_im = h0_re[:, None], h0_im[:, None]
        sr, si = sr + ar * h_re - ai * h_im, si + ar * h_im + ai * h_re
    y = (jnp.einsum('gcp,bsgp->bsgc', p["s5_c_re"], sr)
         - jnp.einsum('gcp,bsgp->bsgc', p["s5_c_im"], si))
    y = y.reshape(B, S, S5_WIDTH) + p["s5_d"] * u
    g = jax.nn.gelu(y)
    out = g * jax.nn.sigmoid(g @ p["w_s5_glu"] + p["b_s5_glu"])
    return out, sr[:, -1], si[:, -1]


def _conv_ffn(xn, conv_hist, p):
    up = xn @ p["w_up"]
    B, S, C = up.shape
    if conv_hist is None:
        conv_hist = jnp.zeros((B, CONV_W - 1, C), up.dtype)
    full = jnp.concatenate([conv_hist.astype(up.dtype), up], axis=1)
    w = p["w_dw"]
    conv = w[0] * full[:, 0:S] + w[1] * full[:, 1:S + 1] + w[2] * full[:, 2:S + 2] + p["b_dw"]
    gate, val = conv[..., :D_FF], conv[..., D_FF:]
    y = (jax.nn.silu(gate) * val) @ p["w_down"]
    return y, full[:, -(CONV_W - 1):]


def _block(x, past_c_kv, past_k_pe, h0_re, h0_im, conv_hist, p):
    B, S, _ = x.shape
    past = 0 if past_c_kv is None else past_c_kv.shape[1]
    pos = past + jnp.arange(S)
    cos, sin = _rope_tables(pos)
    xn = _rmsnorm(x, p["g_mix_norm"])
    hin = xn @ p["w_in"]
    q_lat = hin[..., :Q_LORA]
    kv_lat = hin[..., Q_LORA:Q_LORA + KV_LORA]
    k_pe = hin[..., Q_LORA + KV_LORA:Q_LORA + KV_LORA + ROPE_DIM]
    u = hin[..., Q_LORA + KV_LORA + ROPE_DIM:]
    c_q = _rmsnorm(q_lat, p["g_q_a"])
    q = (c_q @ p["w_q_b"]).reshape(B, S, N_HEADS, QK_DIM)
    q = jnp.concatenate([q[..., :QK_NOPE], _rope(q[..., QK_NOPE:], cos[:, None], sin[:, None])], axis=-1)
    q = _rmsnorm(q, p["g_q_head"])
    c_kv = _rmsnorm(kv_lat, p["g_kv_a"])
    k_pe = _rope(k_pe, cos, sin)
    if past_c_kv is None:
        k, v = _mla_keys(c_kv, k_pe, p)
        o_a = _attn_prompt(q, k, v)
    else:
        k, v = _mla_keys(jnp.concatenate([past_c_kv.astype(c_kv.dtype), c_kv], axis=1),
                         jnp.concatenate([past_k_pe.astype(k_pe.dtype), k_pe], axis=1), p)
        o_a = _attn_sample(q, k, v)
    o_b, s_re, s_im = _s5(u, h0_re, h0_im, p)
    h = x + jnp.concatenate([o_a, o_b], axis=-1) @ p["w_out"]
    f, conv_state = _conv_ffn(_rmsnorm(h, p["g_ffn_norm"]), conv_hist, p)
    return h + f, c_kv, k_pe, s_re, s_im, conv_state


def setup_inputs(seed: int = 0) -> dict:
    key = jax.random.key(seed)
    ks = iter(jax.random.split(key, 40))
    nrm = lambda shape, s: jax.random.normal(next(ks), shape, jnp.float32) * s
    L = DEPTH
    n_idx = jnp.arange(S5_STATE, dtype=jnp.float32)
    return {
        "x_prompt": nrm((BATCH, SEQ, D_MODEL), 1.0),
        "x_sample": nrm((DEC_BATCH, DEC_SEQ, D_MODEL), 1.0),
        "cache_kv_latent": nrm((L, DEC_BATCH, PAST_LEN, KV_LORA), 1.0),
        "cache_k_rope": nrm((L, DEC_BATCH, PAST_LEN, ROPE_DIM), 1.0),
        "state_s5_re": nrm((L, DEC_BATCH, S5_GROUPS, S5_STATE), 0.5),
        "state_s5_im": nrm((L, DEC_BATCH, S5_GROUPS, S5_STATE), 0.5),
        "state_ffn_conv": nrm((L, DEC_BATCH, CONV_W - 1, 2 * D_FF), 0.5),
        "g_mix_norm": 1.0 + nrm((L, D_MODEL), 0.02),
        "w_in": nrm((L, D_MODEL, D_IN), D_MODEL ** -0.5),
        "g_q_a": 1.0 + nrm((L, Q_LORA), 0.02),
        "w_q_b": nrm((L, Q_LORA, N_HEADS * QK_DIM), Q_LORA ** -0.5),
        "g_kv_a": 1.0 + nrm((L, KV_LORA), 0.02),
        "w_kv_b": nrm((L, KV_LORA, N_HEADS * (QK_NOPE + V_DIM)), KV_LORA ** -0.5),
        "g_q_head": 1.0 + nrm((L, QK_DIM), 0.02),
        "g_k_head": 1.0 + nrm((L, QK_DIM), 0.02),
        "s5_a_re": -0.5 + nrm((L, S5_GROUPS, S5_STATE), 0.01),
        "s5_a_im": math.pi * n_idx + nrm((L, S5_GROUPS, S5_STATE), 0.01),
        "s5_log_dt": jax.random.uniform(next(ks), (L, S5_GROUPS), jnp.float32, math.log(1e-3), math.log(1e-1)),
        "s5_b_re": nrm((L, S5_GROUPS, S5_STATE, S5_GROUP_SIZE), (2 * S5_GROUP_SIZE) ** -0.5),
        "s5_b_im": nrm((L, S5_GROUPS, S5_STATE, S5_GROUP_SIZE), (2 * S5_GROUP_SIZE) ** -0.5),
        "s5_c_re": nrm((L, S5_GROUPS, S5_GROUP_SIZE, S5_STATE), (2 * S5_STATE) ** -0.5),
        "s5_c_im": nrm((L, S5_GROUPS, S5_GROUP_SIZE, S5_STATE), (2 * S5_STATE) ** -0.5),
        "s5_d": nrm((L, S5_WIDTH), 1.0),
        "w_s5_glu": nrm((L, S5_WIDTH, S5_WIDTH), S5_WIDTH ** -0.5),
        "b_s5_glu": nrm((L, S5_WIDTH), 0.01),
        "w_out": nrm((L, D_MIX, D_MODEL), D_MIX ** -0.5),
        "g_ffn_norm": 1.0 + nrm((L, D_MODEL), 0.02),
        "w_up": nrm((L, D_MODEL, 2 * D_FF), D_MODEL ** -0.5),
        "w_dw": nrm((L, CONV_W, 2 * D_FF), CONV_W ** -0.5),
        "b_dw": nrm((L, 2 * D_FF), 0.01),
        "w_down": nrm((L, D_FF, D_MODEL), D_FF ** -0.5),
    }


def reference(x_prompt, x_sample, cache_kv_latent, cache_k_rope, state_s5_re, state_s5_im, state_ffn_conv,
              g_mix_norm, w_in, g_q_a, w_q_b, g_kv_a, w_kv_b, g_q_head, g_k_head,
              s5_a_re, s5_a_im, s5_log_dt, s5_b_re, s5_b_im, s5_c_re, s5_c_im, s5_d, w_s5_glu, b_s5_glu,
              w_out, g_ffn_norm, w_up, w_dw, b_dw, w_down):
    yp, ys = x_prompt, x_sample
    per_p, per_s = [], []
    for l in range(DEPTH):
        p = dict(g_mix_norm=g_mix_norm[l], w_in=w_in[l], g_q_a=g_q_a[l], w_q_b=w_q_b[l],
                 g_kv_a=g_kv_a[l], w_kv_b=w_kv_b[l], g_q_head=g_q_head[l], g_k_head=g_k_head[l],
                 s5_a_re=s5_a_re[l], s5_a_im=s5_a_im[l], s5_log_dt=s5_log_dt[l],
                 s5_b_re=s5_b_re[l], s5_b_im=s5_b_im[l], s5_c_re=s5_c_re[l], s5_c_im=s5_c_im[l],
                 s5_d=s5_d[l], w_s5_glu=w_s5_glu[l], b_s5_glu=b_s5_glu[l], w_out=w_out[l],
                 g_ffn_norm=g_ffn_norm[l], w_up=w_up[l], w_dw=w_dw[l], b_dw=b_dw[l], w_down=w_down[l])
        yp, ckv_p, kpe_p, sre_p, sim_p, conv_p = _block(yp, None, None, None, None, None, p)
        ys, ckv_s, kpe_s, sre_s, sim_s, conv_s = _block(
            ys, cache_kv_latent[l], cache_k_rope[l], state_s5_re[l], state_s5_im[l], state_ffn_conv[l], p)
        per_p.append((ckv_p, kpe_p, sre_p, sim_p, conv_p))
        per_s.append((ckv_s, kpe_s, sre_s, sim_s, conv_s))
    np_ = [jnp.stack(t, axis=0) for t in zip(*per_p)]
    ns_ = [jnp.stack(t, axis=0) for t in zip(*per_s)]
    return (yp, ys, np_[0], np_[1], np_[2], np_[3], np_[4], ns_[0], ns_[1], ns_[2], ns_[3], ns_[4])
```

```python
import math
import os
import numpy as np
from contextlib import ExitStack
import ml_dtypes
import concourse.bass as bass
import concourse.mybir as mybir
from concourse.bass_utils import run_bass_kernel_spmd
from concourse.alu_op_type import AluOpType as ALU

F32 = mybir.dt.float32
BF16 = mybir.dt.bfloat16
AF = mybir.ActivationFunctionType
AX = mybir.AxisListType

D = 1024
NH = 8
QKD = 96
DIN_Q = 384
DKV = 256
ROPE = 32
S5W = 512
NG = 32
DFF = 2816
NFF = DFF // 128
EPS = 1e-6
ATTN_SCALE = 1.0 / math.sqrt(96.0)
GELU_C = math.sqrt(2.0 / math.pi)
SEQ = 4096
HALF = 2048
PAST = 2048
DEC = 16


NO_SELF_RAW = set()


class Buf:
    __slots__ = ("name", "w", "r")

    def __init__(self, name):
        self.name = name
        self.w = None
        self.r = []


class Sched:
    def __init__(self, nc, es, n_dma_sems=40):
        self.nc = nc
        self.eng = {"pe": nc.tensor, "act": nc.scalar, "dve": nc.vector,
                    "pool": nc.gpsimd, "sp": nc.sync}
        self.sem = {}
        self.cnt = {}
        for k in self.eng:
            self.sem[k] = es.enter_context(nc.semaphore("s_" + k))
            self.cnt[k] = 0
        self.dsem = [es.enter_context(nc.semaphore("d%d" % i)) for i in range(n_dma_sems)]
        self.dcnt = [0] * n_dma_sems
        self.dnext = 0
        self.seen = {k: {} for k in self.eng}
        self.nwait = 0
        self.nops = 0

    def _semof(self, key):
        if isinstance(key, tuple):
            return self.dsem[key[1]]
        return self.sem[key]

    def _need(self, e, deps):
        best = {}
        for d in deps:
            k, v = d
            if v > best.get(k, 0):
                best[k] = v
        for k, v in best.items():
            if self.seen[e].get(k, 0) >= v:
                continue
            self.eng[e].wait_ge(self._semof(k), v)
            self.nwait += 1
            self.seen[e][k] = v

    def op(self, e, reads, writes, fn):
        deps = []
        for b in reads:
            if b.w is not None and (b.w[0] != e or e not in NO_SELF_RAW):
                deps.append(b.w)
        for b in writes:
            if b.w is not None and b.w[0] != e:
                deps.append(b.w)
            for r in b.r:
                if r[0] != e:
                    deps.append(r)
        self._need(e, deps)
        ins = fn(self.eng[e])
        self.cnt[e] += 1
        v = self.cnt[e]
        ins.then_inc(self.sem[e], 1)
        self.nops += 1
        for b in writes:
            b.w = (e, v)
            b.r = []
        for b in reads:
            if b not in writes:
                b.r.append((e, v))
        return ins

    def dma(self, q, out, in_, reads, writes, **kw):
        slot = self.dnext
        self.dnext = (self.dnext + 1) % len(self.dsem)
        key = ("dma", slot)
        deps = []
        if self.dcnt[slot] > 0:
            deps.append((key, self.dcnt[slot]))
        for b in reads:
            if b.w is not None:
                deps.append(b.w)
        for b in writes:
            if b.w is not None:
                deps.append(b.w)
            deps.extend(b.r)
        self._need(q, deps)
        ins = self.eng[q].dma_start(out=out, in_=in_, **kw)
        self.dcnt[slot] += 16
        v = self.dcnt[slot]
        ins.then_inc(self.dsem[slot], 16)
        for b in writes:
            b.w = (key, v)
            b.r = []
        for b in reads:
            if b not in writes:
                b.r.append((key, v))
        return ins

    def barrier(self, dma=True):
        deps = [(k, self.cnt[k]) for k in self.eng if self.cnt[k] > 0]
        if dma:
            deps += [(("dma", i), c) for i, c in enumerate(self.dcnt) if c > 0]
        for e in self.eng:
            self._need(e, [d for d in deps if d[0] != e])


class Inst:
    pass


def build_program(do_sample=True, stop_after=99, dbg=False):
    nc = bass.Bass("TRN2", target_bir_lowering=False)
    di = {}

    def inp(name, shape, dt=F32):
        di[name] = nc.dram_tensor(name, list(shape), dt, kind="ExternalInput").ap()
        return di[name]

    def outp(name, shape, dt=F32):
        di[name] = nc.dram_tensor(name, list(shape), dt, kind="ExternalOutput").ap()
        return di[name]

    xw = inp("xw", [SEQ, D])
    xsw = inp("xsw", [128, D])
    cache_ckv = inp("cache_ckv", [PAST, DKV])
    cache_kpe = inp("cache_kpe", [PAST, ROPE])
    valid_p = inp("valid_p", [128, 32])
    valid_s = inp("valid_s", [128, 17])
    cos_p = inp("cos_p", [128, 32, 16])
    sin_p = inp("sin_p", [128, 32, 16])
    cos_s = inp("cos_s", [128, 1, 16])
    sin_s = inp("sin_s", [128, 1, 16])
    masks_d = inp("masks", [128, 4, 512])
    hval_d = inp("hval", [128, 1])
    s5init_s = inp("s5init_s", [128, NG])
    convh_s = inp("convh_s", [128, 2 * NFF, 2])
    mask8_d = inp("mask8", [128, 8])
    sgn_d = inp("sgn", [128, 2])
    w_in = inp("w_in", [D, 1184])
    w_q_b = inp("w_q_b", [DIN_Q, NH * QKD])
    w_kv_b = inp("w_kv_b", [DKV, 1024])
    w_glu_d = inp("w_s5_glu", [S5W, S5W])
    w_out_d = inp("w_out", [D, D])
    w_up_d = inp("w_up", [D, 2 * DFF])
    w_down_d = inp("w_down", [DFF, D])
    g_mix_d = inp("g_mix_b", [128, D])
    g_qa_d = inp("g_qa_b", [128, DIN_Q])
    g_kva_d = inp("g_kva_b", [128, DKV])
    g_ffn_d = inp("g_ffn_b", [128, D])
    g_qh_d = inp("g_qh_b", [128, QKD])
    g_kh_d = inp("g_kh_b", [128, QKD])
    are_d = inp("s5_are", [128, NG])
    aim_d = inp("s5_aim", [128, NG])
    ldt_d = inp("s5_ldt", [128, NG])
    bst_d = inp("s5_bst", [128, NG, 16])
    bsw_d = inp("s5_bsw", [128, NG, 16])
    cn_d = inp("s5_cn", [128, 4, 128])
    cns_d = inp("s5_cns", [128, 4, 128])
    d_d = inp("s5_dl", [128, 4])
    bglu_d = inp("s5_bglu", [128, 4])
    wdw_d = inp("w_dw_l", [128, 2 * NFF, 3])
    bdw_d = inp("b_dw_l", [128, 2 * NFF])

    y_p = outp("y_p", [HALF, D])
    nkv_p = outp("nkv_p", [HALF, DKV])
    nkr_p = outp("nkr_p", [HALF, ROPE])
    s5_p = outp("s5_p", [128, NG])
    conv_p = outp("conv_p", [128, 2 * NFF, 2])
    y_s = outp("y_s", [DEC, D])
    nkv_s = outp("nkv_s", [DEC, DKV])
    nkr_s = outp("nkr_s", [DEC, ROPE])
    s5_s = outp("s5_s", [128, NG])
    conv_s = outp("conv_s", [128, 2 * NFF, 2])
    if dbg:
        dbg_ob = outp("dbg_ob", [128, 4, 17 * 128], BF16)
        dbg_oa = outp("dbg_oa", [128, 4, 17 * 128], BF16)
        dbg_q = outp("dbg_q", [96, 8, 17 * 128], BF16)
    scr_up = nc.dram_tensor("scr_up", [D, 2 * DFF], BF16, kind="Internal").ap()
    sc5 = {
        "cos": nc.dram_tensor("sc_cos", [128, NG * 128], F32, kind="Internal").ap(),
        "sin": nc.dram_tensor("sc_sin", [128, NG * 128], F32, kind="Internal").ap(),
        "bpad": nc.dram_tensor("sc_bpad", [128, NG * 128], BF16, kind="Internal").ap(),
        "bpsw": nc.dram_tensor("sc_bpsw", [128, NG * 128], BF16, kind="Internal").ap(),
        "cw1": nc.dram_tensor("sc_cw1", [128, NG * 128], BF16, kind="Internal").ap(),
        "cw2": nc.dram_tensor("sc_cw2", [128, NG * 128], BF16, kind="Internal").ap(),
        "diag": nc.dram_tensor("sc_diag", [128, 512], BF16, kind="Internal").ap(),
        "rm": nc.dram_tensor("sc_rm", [128, NG], F32, kind="Internal").ap(),
    }
    Bsc5 = {k: Buf("sc5_" + k) for k in sc5}

    es = ExitStack()
    with es:
        cnt = [0]

        def sb(shape, dt, stack=None, name=None):
            cnt[0] += 1
            return (stack or es).enter_context(
                nc.sbuf_tensor((name or "t") + "_%d" % cnt[0], list(shape), dt))

        banks = [es.enter_context(nc.psum_tensor("bank%d" % i, [128, 512], F32)) for i in range(8)]
        Bbank = [Buf("bank%d" % i) for i in range(8)]
        bankb = [b[:].bitcast(BF16) for b in banks]

        ident = sb([128, 128], BF16); Bident = Buf("ident")
        identf = sb([128, 128], F32)
        swp = sb([128, 128], F32); Bswp = Buf("swp")
        mh = sb([128, 8], F32); Bmh = Buf("mh")
        Bg = Buf("gconst")
        obT_p = sb([128, 4, 17 * 128], BF16)
        oaT_p = sb([128, 4, 17 * 128], BF16)
        obT_s = sb([128, 4, 128], BF16)
        oaT_s = sb([128, 4, 128], BF16)
        P4 = {}
        hval = sb([128, 1], F32); Bhval = Buf("hval")
        sgn = sb([128, 2], F32); Bsgn = Buf("sgn")
        junk = sb([128, 1024], F32); Bjunk = Buf("junk")

        blk = es.enter_context(nc.Block())
        S = Sched(nc, es)
        A = lambda r, w, f: S.op("act", r, w, f)
        V = lambda r, w, f: S.op("dve", r, w, f)
        G = lambda r, w, f: S.op("pool", r, w, f)
        T = lambda r, w, f: S.op("pe", r, w, f)

        def bc_mid(ap, n):
            return ap.unsqueeze(1).to_broadcast([ap.shape[0], n, ap.shape[1]])

        def bc_last(ap, n):
            return ap.unsqueeze(2).to_broadcast([ap.shape[0], ap.shape[1], n])

        G([], [Bident], lambda e: e.memset(identf[:], 0.0))
        G([Bident], [Bident], lambda e: e.affine_select(
            out=identf[:], in_=identf[:], pattern=[[-1, 128]], compare_op=ALU.not_equal,
            fill=1.0, base=0, channel_multiplier=1))
        G([Bident], [Bident], lambda e: e.tensor_copy(out=ident[:], in_=identf[:]))
        G([Bident], [Bswp], lambda e: e.tensor_copy(out=swp[:, 0:64], in_=identf[:, 64:128]))
        G([Bident, Bswp], [Bswp], lambda e: e.tensor_copy(out=swp[:, 64:128], in_=identf[:, 0:64]))
        G([], [Bmh], lambda e: e.memset(mh[:], -0.5))
        S.dma("sp", hval[:], hval_d, [], [Bhval])
        S.dma("sp", sgn[:], sgn_d, [], [Bsgn])

        def rstd_from(ssq, Bssq, n, dim, out, Bout):
            V([Bssq], [Bssq], lambda e: e.tensor_scalar(out=ssq, in0=ssq, scalar1=1.0 / dim, scalar2=EPS,
                                                        op0=ALU.mult, op1=ALU.add))
            G([Bssq, Bmh], [Bout], lambda e: e.tensor_tensor(out=out, in0=ssq, in1=mh[:, 0:n], op=ALU.pow))

        def run_instance(I):
            NW, OWN0 = I.NW, I.OWN0
            NOWN = NW - OWN0
            oaT, obT, BoaT, BobT = I.oaT, I.obT, Buf("oaT"), Buf("obT")
            pk = ExitStack()
            w_kvb = sb([128, 2, 1024], BF16, pk); Bwkvb = Buf("wkvb")
            ckvT = sb([128, 2, NW * 128], BF16, pk); BckvT = [Buf("ckvT%d" % i) for i in range(NW)]
            krotT = sb([96, NW * 128], BF16, pk); BkrotT = [Buf("krotT%d" % i) for i in range(NW)]
            kscale = sb([128, NW, NH], F32, pk); Bkscale = [Buf("kscale%d" % i) for i in range(NW)]
            g_mix = sb([128, D], F32, pk)
            S.dma("sp", g_mix[:], g_mix_d, [], [Bg])
            S.dma("pool", w_kvb[:], w_kv_b.rearrange("(k p) n -> p k n", p=128), [], [Bwkvb])
            with ExitStack() as p1:
                g_kva = sb([128, DKV], F32, p1)
                S.dma("sp", g_kva[:], g_kva_d, [], [Bg])
                w_in_kvu = sb([128, 8, 800], BF16, p1); Bw1 = Buf("w_in_kvu")
                w_nope = sb([128, 2, 512], BF16, p1); Bwn = Buf("w_nope")
                w_glu = sb([128, 4, 512], BF16, p1); Bwg = Buf("w_glu")
                dl = sb([128, 4], F32, p1); bglu = sb([128, 4], F32, p1); mask8 = sb([128, 8], F32, p1)
                Bdl = Buf("dl")
                Rm = sb([128, NG], F32, p1)
                Bpad = sb([128, NG, 128], BF16, p1); Bpsw = sb([128, NG, 128], BF16, p1)
                Cw1 = sb([128, NG, 128], BF16, p1); Cw2 = sb([128, NG, 128], BF16, p1)
                diagD = sb([128, 4, 128], BF16, p1)
                Bw5 = Buf("s5w")
                COS = sb([128, NG, 128], F32, p1); SIN = sb([128, NG, 128], F32, p1)
                Btab = Buf("tab")
                CL = sb([128, NG], F32, p1); NSL = sb([128, NG], F32, p1); carry = sb([128, NG], F32, p1)
                CLN = sb([128, 8, 8], F32, p1)
                Bcarry = Buf("carry")
                LS = I.LS
                xf = [sb([128, D], F32, p1) for _ in range(2)]; Bxf = [Buf("xf0"), Buf("xf1")]
                cs_t = [sb([128, 2, 16], F32, p1) for _ in range(2)]; Bcs = [Buf("cs0"), Buf("cs1")]
                if I.KVC == 0:
                    S.dma("sp", xf[0][:], I.x[0:128, :], [], [Bxf[0]])
                    S.dma("sp", cs_t[0][:, 0, :], I.cos[:, 0, :], [], [Bcs[0]])
                    S.dma("sp", cs_t[0][:, 1, :], I.sin[:, 0, :], [], [Bcs[0]])
                if I.name == "p":
                    p1s = ExitStack()
                    are = sb([128, NG], F32, p1s); aim = sb([128, NG], F32, p1s); ldt = sb([128, NG], F32, p1s)
                    Bs5p = Buf("s5p")
                    bst = sb([128, NG, 16], F32, p1s); bsw = sb([128, NG, 16], F32, p1s); Bbst = Buf("bst")
                    cn = sb([128, 4, 128], F32, p1s); cns = sb([128, 4, 128], F32, p1s); Bcn = Buf("cn")
                    sm = [sb([128, NG], F32, p1s) for _ in range(14)]
                    Bsm = Buf("s5small")
                    bb = sb([128, NG, 16], F32, p1s); bbs = sb([128, NG, 16], F32, p1s); btmp = sb([128, NG, 16], F32, p1s)
                    Bbb = Buf("bb")
                    ta_ = sb([128, NG, 64], F32, p1s); tb_ = sb([128, NG, 64], F32, p1s)

                    S.dma("pool", w_in_kvu[:], w_in.rearrange("(k p) n -> p k n", p=128)[:, :, 384:1184], [], [Bw1])
                    for k in range(2):
                        S.dma("pool", w_nope[:, k, :].rearrange("p (h d) -> p h d", d=64),
                              w_kv_b[k * 128:(k + 1) * 128, :].rearrange("p (h d) -> p h d", d=128)[:, :, 0:64],
                              [], [Bwn])
                    S.dma("pool", w_glu[:], w_glu_d.rearrange("(k p) n -> p k n", p=128), [], [Bwg])
                    S.dma("sp", are[:], are_d, [], [Bs5p])
                    S.dma("sp", aim[:], aim_d, [], [Bs5p])
                    S.dma("sp", ldt[:], ldt_d, [], [Bs5p])
                    S.dma("sp", bst[:], bst_d, [], [Bbst])
                    S.dma("sp", bsw[:], bsw_d, [], [Bbst])
                    S.dma("sp", cn[:], cn_d, [], [Bcn])
                    S.dma("sp", cns[:], cns_d, [], [Bcn])
                    S.dma("sp", dl[:], d_d, [], [Bdl])
                    S.dma("sp", bglu[:], bglu_d, [], [Bdl])
                    S.dma("sp", mask8[:], mask8_d, [], [Bdl])
                    V([Bdl], [Bdl], lambda e: e.tensor_scalar(out=bglu[:], in0=bglu[:], scalar1=0.5, scalar2=None, op0=ALU.mult))

                    dt_, e1, mag, ang, shh, s16, cth, sth, cc, ss, cs, fre, fim, tmp = sm
                    A([Bs5p], [Bsm], lambda e: e.activation(out=dt_[:], in_=ldt[:], func=AF.Exp))
                    V([Bs5p, Bsm], [Bsm], lambda e: e.tensor_tensor(out=e1[:], in0=are[:], in1=dt_[:], op=ALU.mult))
                    A([Bsm], [Bsm], lambda e: e.activation(out=mag[:], in_=e1[:], func=AF.Exp))
                    V([Bs5p, Bsm], [Bsm], lambda e: e.tensor_tensor(out=ang[:], in0=aim[:], in1=dt_[:], op=ALU.mult))
                    A([Bsm], [Bsm], lambda e: e.activation(out=shh[:], in_=ang[:], func=AF.Sin, scale=1.0 / 32))
                    A([Bsm], [Bsm], lambda e: e.activation(out=sth[:], in_=ang[:], func=AF.Sin, scale=1.0 / 16))
                    V([Bsm], [Bsm], lambda e: e.tensor_tensor(out=cth[:], in0=shh[:], in1=shh[:], op=ALU.mult))
                    V([Bsm], [Bsm], lambda e: e.tensor_scalar(out=cth[:], in0=cth[:], scalar1=-2.0, scalar2=1.0,
                                                              op0=ALU.mult, op1=ALU.add))
                    for _ in range(4):
                        V([Bsm], [Bsm], lambda e: e.tensor_tensor(out=cc[:], in0=cth[:], in1=cth[:], op=ALU.mult))
                        V([Bsm], [Bsm], lambda e: e.tensor_tensor(out=ss[:], in0=sth[:], in1=sth[:], op=ALU.mult))
                        V([Bsm], [Bsm], lambda e: e.tensor_tensor(out=cs[:], in0=cth[:], in1=sth[:], op=ALU.mult))
                        V([Bsm], [Bsm], lambda e: e.tensor_tensor(out=cth[:], in0=cc[:], in1=ss[:], op=ALU.subtract))
                        V([Bsm], [Bsm], lambda e: e.tensor_scalar(out=sth[:], in0=cs[:], scalar1=2.0, scalar2=None, op0=ALU.mult))
                    V([Bsm], [Bsm], lambda e: e.tensor_copy(out=Rm[:], in_=mag[:]))
                    lbr, lbi = cc, ss
                    V([Bsm], [Bsm], lambda e: e.tensor_tensor(out=lbr[:], in0=mag[:], in1=cth[:], op=ALU.mult))
                    V([Bsm], [Bsm], lambda e: e.tensor_tensor(out=lbi[:], in0=mag[:], in1=sth[:], op=ALU.mult))
                    V([Bsm], [Bsm], lambda e: e.tensor_scalar(out=lbr[:], in0=lbr[:], scalar1=-1.0, scalar2=None, op0=ALU.add))
                    den = cs
                    V([Bs5p], [Bsm], lambda e: e.tensor_tensor(out=den[:], in0=are[:], in1=are[:], op=ALU.mult))
                    V([Bs5p, Bsm], [Bsm], lambda e: e.tensor_tensor(out=tmp[:], in0=aim[:], in1=aim[:], op=ALU.mult))
                    V([Bsm], [Bsm], lambda e: e.tensor_tensor(out=den[:], in0=den[:], in1=tmp[:], op=ALU.add))
                    V([Bsm], [Bsm], lambda e: e.reciprocal(out=den[:], in_=den[:]))
                    V([Bs5p, Bsm], [Bsm], lambda e: e.tensor_tensor(out=fre[:], in0=lbr[:], in1=are[:], op=ALU.mult))
                    V([Bs5p, Bsm], [Bsm], lambda e: e.tensor_tensor(out=tmp[:], in0=lbi[:], in1=aim[:], op=ALU.mult))
                    V([Bsm], [Bsm], lambda e: e.tensor_tensor(out=fre[:], in0=fre[:], in1=tmp[:], op=ALU.add))
                    V([Bsm], [Bsm], lambda e: e.tensor_tensor(out=fre[:], in0=fre[:], in1=den[:], op=ALU.mult))
                    V([Bs5p, Bsm], [Bsm], lambda e: e.tensor_tensor(out=fim[:], in0=lbi[:], in1=are[:], op=ALU.mult))
                    V([Bs5p, Bsm], [Bsm], lambda e: e.tensor_tensor(out=tmp[:], in0=lbr[:], in1=aim[:], op=ALU.mult))
                    V([Bsm], [Bsm], lambda e: e.tensor_tensor(out=fim[:], in0=fim[:], in1=tmp[:], op=ALU.subtract))
                    V([Bsm], [Bsm], lambda e: e.tensor_tensor(out=fim[:], in0=fim[:], in1=den[:], op=ALU.mult))
                    V([Bsm, Bsgn], [Bsm], lambda e: e.tensor_scalar(out=fim[:], in0=fim[:], scalar1=sgn[:, 0:1], scalar2=None, op0=ALU.mult))
                    V([Bsm, Bbst], [Bbb], lambda e: e.tensor_tensor(out=bb[:], in0=bst[:], in1=bc_last(fre[:], 16), op=ALU.mult))
                    V([Bsm, Bbst, Bbb], [Bbb], lambda e: e.tensor_tensor(out=btmp[:], in0=bsw[:], in1=bc_last(fim[:], 16), op=ALU.mult))
                    V([Bbb], [Bbb], lambda e: e.tensor_tensor(out=bb[:], in0=bb[:], in1=btmp[:], op=ALU.add))
                    V([Bsm, Bbst, Bbb], [Bbb], lambda e: e.tensor_tensor(out=bbs[:], in0=bsw[:], in1=bc_last(fre[:], 16), op=ALU.mult))
                    V([Bsm, Bbst, Bbb], [Bbb], lambda e: e.tensor_tensor(out=btmp[:], in0=bst[:], in1=bc_last(fim[:], 16), op=ALU.mult))
                    V([Bbb], [Bbb], lambda e: e.tensor_tensor(out=bbs[:], in0=bbs[:], in1=btmp[:], op=ALU.subtract))
                    G([], [Bw5], lambda e: e.memset(Cw1[:], 0.0))
                    G([Bw5], [Bw5], lambda e: e.memset(Cw2[:], 0.0))
                    for gb in range(4):
                        for (src, dst) in ((bb, Bpad), (bbs, Bpsw)):
                            bkk = gb if src is bb else 4 + gb
                            T([Bbb, Bident], [Bbank[bkk]], lambda e: e.transpose(
                                banks[bkk][:, 0:128], src[:, gb * 8:(gb + 1) * 8, :].rearrange("p a b -> p (a b)"), identf[:]))
                            for j in range(8):
                                V([Bbank[bkk], Bdl], [Bw5], lambda e: e.tensor_scalar(
                                    out=dst[:, gb * 8 + j, :], in0=banks[bkk][:, 0:128], scalar1=mask8[:, j:j + 1],
                                    scalar2=None, op0=ALU.mult))
                        for (src, dst, sc) in ((cn, Cw1, 1), (cns, Cw2, 0)):
                            bkc = gb if sc == 1 else 4 + gb
                            T([Bcn, Bident], [Bbank[bkc]], lambda e: e.transpose(banks[bkc][:, 128:256], src[:, gb, :], identf[:]))
                            for j in range(8):
                                V([Bbank[bkc], Bsgn], [Bw5], lambda e: e.tensor_scalar(
                                    out=dst[:, gb * 8 + j, 16 * j:16 * j + 16], in0=banks[bkc][:, 128 + 16 * j:128 + 16 * j + 16],
                                    scalar1=sgn[:, sc:sc + 1], scalar2=None, op0=ALU.mult))
                        V([Bdl, Bident], [Bw5], lambda e: e.tensor_scalar(
                            out=diagD[:, gb, :], in0=identf[:], scalar1=dl[:, gb:gb + 1], scalar2=None, op0=ALU.mult))
                    V([Bsm], [Btab], lambda e: e.tensor_copy(out=COS[:, :, 0], in_=cth[:]))
                    V([Bsm, Btab], [Btab], lambda e: e.tensor_copy(out=SIN[:, :, 0], in_=sth[:]))
                    m = 1
                    while m < 128:
                        cm = COS[:, :, m - 1:m].to_broadcast([128, NG, m])
                        smm = SIN[:, :, m - 1:m].to_broadcast([128, NG, m])
                        V([Btab], [Btab], lambda e: e.tensor_tensor(out=ta_[:, :, 0:m], in0=COS[:, :, 0:m], in1=cm, op=ALU.mult))
                        V([Btab], [Btab], lambda e: e.tensor_tensor(out=tb_[:, :, 0:m], in0=SIN[:, :, 0:m], in1=smm, op=ALU.mult))
                        V([Btab], [Btab], lambda e: e.tensor_tensor(out=COS[:, :, m:2 * m], in0=ta_[:, :, 0:m], in1=tb_[:, :, 0:m], op=ALU.subtract))
                        V([Btab], [Btab], lambda e: e.tensor_tensor(out=ta_[:, :, 0:m], in0=SIN[:, :, 0:m], in1=cm, op=ALU.mult))
                        V([Btab], [Btab], lambda e: e.tensor_tensor(out=tb_[:, :, 0:m], in0=COS[:, :, 0:m], in1=smm, op=ALU.mult))
                        V([Btab], [Btab], lambda e: e.tensor_tensor(out=SIN[:, :, m:2 * m], in0=ta_[:, :, 0:m], in1=tb_[:, :, 0:m], op=ALU.add))
                        m *= 2
                    V([Btab, Bsgn], [Btab], lambda e: e.tensor_scalar(
                        out=SIN[:].rearrange("p a b -> p (a b)"), in0=SIN[:].rearrange("p a b -> p (a b)"),
                        scalar1=sgn[:, 1:2], scalar2=None, op0=ALU.mult))

                    flat = lambda ap: ap.rearrange("p a b -> p (a b)")
                    S.dma("sp", sc5["cos"], flat(COS[:]), [Btab], [Bsc5["cos"]])
                    S.dma("sp", sc5["sin"], flat(SIN[:]), [Btab], [Bsc5["sin"]])
                    S.dma("sp", sc5["bpad"], flat(Bpad[:]), [Bw5], [Bsc5["bpad"]])
                    S.dma("sp", sc5["bpsw"], flat(Bpsw[:]), [Bw5], [Bsc5["bpsw"]])
                    S.dma("sp", sc5["cw1"], flat(Cw1[:]), [Bw5], [Bsc5["cw1"]])
                    S.dma("sp", sc5["cw2"], flat(Cw2[:]), [Bw5], [Bsc5["cw2"]])
                    S.dma("sp", sc5["diag"], flat(diagD[:]), [Bw5], [Bsc5["diag"]])
                    S.dma("sp", sc5["rm"], Rm[:], [Bsm], [Bsc5["rm"]])
                else:
                    p1s = ExitStack()
                    Bsm = Buf("s5small")
                    flat = lambda ap: ap.rearrange("p a b -> p (a b)")
                    S.dma("pool", w_in_kvu[:], w_in.rearrange("(k p) n -> p k n", p=128)[:, :, 384:1184], [], [Bw1])
                    for k in range(2):
                        S.dma("pool", w_nope[:, k, :].rearrange("p (h d) -> p h d", d=64),
                              w_kv_b[k * 128:(k + 1) * 128, :].rearrange("p (h d) -> p h d", d=128)[:, :, 0:64],
                              [], [Bwn])
                    S.dma("pool", w_glu[:], w_glu_d.rearrange("(k p) n -> p k n", p=128), [], [Bwg])
                    S.dma("sp", bglu[:], bglu_d, [], [Bdl])
                    V([Bdl], [Bdl], lambda e: e.tensor_scalar(out=bglu[:], in0=bglu[:], scalar1=0.5, scalar2=None, op0=ALU.mult))
                    tb_ = [Buf("ld%d" % k) for k in range(8)]
                    S.dma("sp", flat(COS[:]), sc5["cos"], [Bsc5["cos"]], [tb_[0]])
                    S.dma("sp", flat(SIN[:]), sc5["sin"], [Bsc5["sin"]], [tb_[1]])
                    S.dma("sp", flat(Bpad[:]), sc5["bpad"], [Bsc5["bpad"]], [tb_[2]])
                    S.dma("sp", flat(Bpsw[:]), sc5["bpsw"], [Bsc5["bpsw"]], [tb_[3]])
                    S.dma("sp", flat(Cw1[:]), sc5["cw1"], [Bsc5["cw1"]], [tb_[4]])
                    S.dma("sp", flat(Cw2[:]), sc5["cw2"], [Bsc5["cw2"]], [tb_[5]])
                    S.dma("sp", flat(diagD[:]), sc5["diag"], [Bsc5["diag"]], [tb_[6]])
                    S.dma("sp", Rm[:], sc5["rm"], [Bsc5["rm"]], [tb_[7]])
                    V(tb_, [Btab, Bw5, Bsm], lambda e: e.memset(NSL[:, 0:1], 0.0))
                V([Btab], [Btab], lambda e: e.tensor_copy(out=CL[:], in_=COS[:, :, LS - 1]))
                V([Btab], [Btab], lambda e: e.tensor_scalar(out=NSL[:], in0=SIN[:, :, LS - 1], scalar1=-1.0, scalar2=None, op0=ALU.mult))
                V([Btab], [Btab], lambda e: e.tensor_copy(out=CLN[:, :, 0:4], in_=CL[:].rearrange("p (a b) -> p a b", b=4)))
                V([Btab], [Btab], lambda e: e.tensor_copy(out=CLN[:, :, 4:8], in_=NSL[:].rearrange("p (a b) -> p a b", b=4)))
                if I.s5init is None:
                    V([], [Bcarry], lambda e: e.memset(carry[:], 0.0))
                else:
                    S.dma("sp", carry[:], I.s5init, [], [Bcarry])
                S.barrier(dma=False)
                p1s.close()

                xs_b = sb([128, D], BF16, p1); Bxs = Buf("xs")
                xT = [sb([128, 8, 128], BF16, p1) for _ in range(2)]; BxT = [Buf("xT0"), Buf("xT1")]
                uT = [sb([128, 4, 128], BF16, p1) for _ in range(2)]; BuT = [Buf("uT0"), Buf("uT1")]
                st = sb([128, 16], F32, p1); Bst = Buf("st")
                ckv_f = [sb([128, DKV], F32, p1) for _ in range(2)]; Bckv = [Buf("ckvf0"), Buf("ckvf1")]
                kpe_f = [sb([128, ROPE], F32, p1) for _ in range(2)]; Bkpe = [Buf("kpef0"), Buf("kpef1")]
                ckv_b = sb([128, DKV], BF16, p1); Bckvb = Buf("ckvb")
                krin = sb([128, 96], BF16, p1); Bkrin = Buf("krin")
                rt = sb([128, 4, 16], F32, p1); Brt = Buf("rt")
                ssqn = sb([128, 8], F32, p1); Bssqn = Buf("ssqn")
                t1 = [sb([128, 512], F32, p1) for _ in range(2)]; Bt1 = [Buf("t1a"), Buf("t1b")]
                t2 = [sb([128, 512], F32, p1) for _ in range(2)]; Bt2 = [Buf("t2a"), Buf("t2b")]
                Z = [sb([128, 4, 128], F32, p1) for _ in range(2)]; BZ = [Buf("Za"), Buf("Zb")]
                Z1 = [sb([128, 4, 128], BF16, p1) for _ in range(2)]
                Z2 = [sb([128, 4, 128], BF16, p1) for _ in range(2)]; BZ12 = [Buf("Z12a"), Buf("Z12b")]
                cst = [sb([128, 8], F32, p1) for _ in range(2)]; Bcst = [Buf("csta"), Buf("cstb")]
                Bcar = [Buf("carry%d" % q) for q in range(8)]
                for q in range(8):
                    Bcar[q].w = Bcarry.w
                gl = sb([128, 4, 128], BF16, p1); gh = sb([128, 4, 128], BF16, p1); th = sb([128, 4, 128], BF16, p1)
                Bgl = Buf("gl"); Bgh = Buf("gh"); Bth = Buf("th")
                G([], [Bkrin], lambda e: e.memset(krin[:], 0.0))
                for zi in range(2):
                    V([], [BZ[zi]], lambda e: e.memset(Z[zi][:], 0.0))

                def load_tile(t):
                    i = t % 2
                    if t < I.KVC:
                        S.dma("sp", ckv_f[i][:], I.cache_ckv[t * 128:(t + 1) * 128, :], [], [Bckv[i]])
                        S.dma("sp", kpe_f[i][:], I.cache_kpe[t * 128:(t + 1) * 128, :], [], [Bkpe[i]])
                    else:
                        tt = t - I.KVC
                        S.dma("sp", xf[i][:], I.x[tt * 128:(tt + 1) * 128, :], [], [Bxf[i]])
                        S.dma("sp", cs_t[i][:, 0, :], I.cos[:, tt, :], [], [Bcs[i]])
                        S.dma("sp", cs_t[i][:, 1, :], I.sin[:, tt, :], [], [Bcs[i]])

                def stage1_steps(t):
                    i = t % 2
                    steps = [[] for _ in range(8)]

                    def add(k, fn):
                        steps[k].append(fn)
                    if t + 1 < NW:
                        add(0, lambda: load_tile(t + 1))
                    if t >= I.KVC:
                        add(0, lambda: A([Bxf[i]], [Bjunk, Bst], lambda e: e.activation(
                            out=junk[:], in_=xf[i][:], func=AF.Square, accum_out=st[:, 0:1])))
                        add(1, lambda: rstd_from(st[:, 0:1], Bst, 1, D, st[:, 1:2], Bst))
                        add(2, lambda: V([Bxf[i], Bst, Bg], [Bxs], lambda e: e.scalar_tensor_tensor(
                            out=xs_b[:], in0=xf[i][:], scalar=st[:, 1:2], in1=g_mix[:], op0=ALU.mult, op1=ALU.mult)))

                        def tr(e):
                            for k in range(8):
                                r = e.transpose(bankb[0][:, k * 128:(k + 1) * 128], xs_b[:, k * 128:(k + 1) * 128], ident[:])
                            return r
                        add(3, lambda: T([Bxs, Bident], [Bbank[0]], tr))
                        add(3, lambda: A([Bbank[0]], [BxT[i]], lambda e: e.activation(
                            out=xT[i][:].rearrange("p a b -> p (a b)"), in_=bankb[0][:, 0:1024], func=AF.Copy)))

                        def mm_kv(e):
                            for k in range(8):
                                r = e.matmul(banks[1][:, 0:288], lhsT=xT[i][:, k, :], rhs=w_in_kvu[:, k, 0:288],
                                             start=(k == 0), stop=(k == 7))
                            return r

                        def mk_mm_u(mo):
                            def mm_u(e):
                                for k in range(8):
                                    r = e.matmul(banks[0][:, mo * 128:(mo + 1) * 128],
                                                 lhsT=w_in_kvu[:, k, 288 + mo * 128:288 + (mo + 1) * 128],
                                                 rhs=xT[i][:, k, :], start=(k == 0), stop=(k == 7))
                                return r
                            return mm_u
                        add(4, lambda: T([BxT[i], Bw1], [Bbank[1]], mm_kv))
                        for mo in range(4):
                            add(4 + mo, (lambda mo=mo: T([BxT[i], Bw1], [Bbank[0]], mk_mm_u(mo))))
                        add(7, lambda: A([Bbank[0]], [BuT[i]], lambda e: e.activation(
                            out=uT[i][:].rearrange("p a b -> p (a b)"), in_=banks[0][:, 0:512], func=AF.Copy)))
                        add(4, lambda: A([Bbank[1]], [Bjunk, Bst], lambda e: e.activation(
                            out=junk[:, 0:DKV], in_=banks[1][:, 0:DKV], func=AF.Square, accum_out=st[:, 2:3])))
                        add(5, lambda: rstd_from(st[:, 2:3], Bst, 1, DKV, st[:, 3:4], Bst))
                        x1 = banks[1][:, 256:272]; x2 = banks[1][:, 272:288]
                        cs_, sn_ = cs_t[i][:, 0, :], cs_t[i][:, 1, :]
                        add(5, lambda: V([Bbank[1], Bcs[i]], [Brt], lambda e: e.tensor_tensor(out=rt[:, 0, :], in0=x1, in1=cs_, op=ALU.mult)))
                        add(5, lambda: V([Bbank[1], Bcs[i]], [Brt], lambda e: e.tensor_tensor(out=rt[:, 1, :], in0=x2, in1=sn_, op=ALU.mult)))
                        add(5, lambda: V([Bbank[1], Bcs[i]], [Brt], lambda e: e.tensor_tensor(out=rt[:, 2, :], in0=x2, in1=cs_, op=ALU.mult)))
                        add(5, lambda: V([Bbank[1], Bcs[i]], [Brt], lambda e: e.tensor_tensor(out=rt[:, 3, :], in0=x1, in1=sn_, op=ALU.mult)))
                        add(5, lambda: V([Brt], [Bkpe[i]], lambda e: e.tensor_tensor(out=kpe_f[i][:, 0:16], in0=rt[:, 0, :], in1=rt[:, 1, :], op=ALU.subtract)))
                        add(5, lambda: V([Brt], [Bkpe[i]], lambda e: e.tensor_tensor(out=kpe_f[i][:, 16:32], in0=rt[:, 2, :], in1=rt[:, 3, :], op=ALU.add)))
                        add(6, lambda: V([Bbank[1], Bst, Bg], [Bckv[i]], lambda e: e.scalar_tensor_tensor(
                            out=ckv_f[i][:], in0=banks[1][:, 0:DKV], scalar=st[:, 3:4], in1=g_kva[:],
                            op0=ALU.mult, op1=ALU.mult)))
                        if I.out_rows(t) is not None:
                            r0, n = I.out_rows(t)
                            add(7, lambda: S.dma("sp", I.nkv[r0:r0 + n, :], ckv_f[i][0:n, :], [Bckv[i]], []))
                            add(7, lambda: S.dma("sp", I.nkr[r0:r0 + n, :], kpe_f[i][0:n, :], [Bkpe[i]], []))
                    add(7, lambda: A([Bckv[i]], [Bckvb], lambda e: e.activation(out=ckv_b[:], in_=ckv_f[i][:], func=AF.Copy)))
                    add(7, lambda: A([Bkpe[i]], [Bkrin], lambda e: e.activation(out=krin[:, 64:96], in_=kpe_f[i][:], func=AF.Copy)))
                    add(7, lambda: A([Bkpe[i]], [Bjunk, Bst], lambda e: e.activation(
                        out=junk[:, 992:1024], in_=kpe_f[i][:], func=AF.Square, accum_out=st[:, 4 + t % 8:5 + t % 8])))

                    def tr2(e):
                        e.transpose(bankb[3][:, 0:128], ckv_b[:, 0:128], ident[:])
                        e.transpose(bankb[3][:, 128:256], ckv_b[:, 128:256], ident[:])
                        return e.transpose(bankb[3][0:96, 256:384], krin[:], ident[:])

                    def mm_st(e):
                        for k in range(2):
                            r = e.matmul(banks[1][:, 0:512], lhsT=ckvT[:, k, t * 128:(t + 1) * 128], rhs=w_nope[:, k, :],
                                         start=(k == 0), stop=(k == 1))
                        return r
                    post = [
                        lambda: T([Bckvb, Bkrin, Bident], [Bbank[3]], tr2),
                        lambda: A([Bbank[3]], [BckvT[t]], lambda e: e.activation(
                            out=ckvT[:, :, t * 128:(t + 1) * 128], in_=bankb[3][:, 0:256].rearrange("p (a b) -> p a b", b=128), func=AF.Copy)),
                        lambda: A([Bbank[3]], [BkrotT[t]], lambda e: e.activation(
                            out=krotT[64:96, t * 128:(t + 1) * 128], in_=bankb[3][64:96, 256:384], func=AF.Copy)),
                        lambda: T([BckvT[t], Bwn], [Bbank[1]], mm_st),
                        lambda: A([Bbank[1]], [Bjunk], lambda e: e.activation(out=junk[:, 0:512], in_=banks[1][:, 0:512], func=AF.Square)),
                        lambda: V([Bjunk], [Bssqn], lambda e: e.tensor_reduce(
                            out=ssqn[:], in_=junk[:, 0:512].rearrange("p (a b) -> p a b", b=64), axis=AX.X, op=ALU.add)),
                        lambda: V([Bssqn, Bst], [Bssqn], lambda e: e.tensor_scalar(
                            out=ssqn[:], in0=ssqn[:], scalar1=st[:, 4 + t % 8:5 + t % 8], scalar2=1.0 / QKD, op0=ALU.add, op1=ALU.mult)),
                        lambda: V([Bssqn], [Bssqn], lambda e: e.tensor_scalar(out=ssqn[:], in0=ssqn[:], scalar1=EPS, scalar2=None, op0=ALU.add)),
                        lambda: G([Bssqn, Bmh], [Bkscale[t]], lambda e: e.tensor_tensor(out=kscale[:, t, :], in0=ssqn[:], in1=mh[:, 0:8], op=ALU.pow)),
                        lambda: G([Bkscale[t]], [Bkscale[t]], lambda e: e.tensor_scalar(
                            out=kscale[:, t, :], in0=kscale[:, t, :], scalar1=ATTN_SCALE, scalar2=0.0, op0=ALU.mult, op1=ALU.add)),
                    ]
                    return steps, post

                def stage1(t):
                    steps, post = stage1_steps(t)
                    for k in range(8):
                        for fn in steps[k]:
                            fn()
                    for fn in post:
                        fn()

                def s5A(t, q, par, part):
                    i = t % 2
                    gb = q // 2
                    if part == 1:
                        V([Bbank[4 + par], Btab], [Bt1[par]], lambda e: e.tensor_tensor(
                            out=t1[par][:], in0=banks[4 + par][:, :],
                            in1=COS[:, q * 4:q * 4 + 4, :].rearrange("p a b -> p (a b)"), op=ALU.mult))
                        V([Bbank[6 + par], Btab], [Bt2[par]], lambda e: e.tensor_tensor(
                            out=t2[par][:], in0=banks[6 + par][:, :],
                            in1=SIN[:, q * 4:q * 4 + 4, :].rearrange("p a b -> p (a b)"), op=ALU.mult))
                        V([Bt1[par], Bt2[par]], [Bt1[par]], lambda e: e.tensor_tensor(out=t1[par][:], in0=t1[par][:], in1=t2[par][:], op=ALU.add))
                        return

                    def mmAB(e):
                        for j in range(4):
                            e.matmul(banks[4 + par][:, j * 128:(j + 1) * 128],
                                     lhsT=Bpad[:, q * 4 + j, :], rhs=uT[i][:, gb, :], start=True, stop=True)
                        for j in range(4):
                            r = e.matmul(banks[6 + par][:, j * 128:(j + 1) * 128],
                                         lhsT=Bpsw[:, q * 4 + j, :], rhs=uT[i][:, gb, :], start=True, stop=True)
                        return r
                    T([BuT[i], Bw5], [Bbank[4 + par], Bbank[6 + par]], mmAB)

                def s5B(t, q, par, own, part):
                    i = t % 2
                    gb = q // 2
                    gs = slice(q * 4, q * 4 + 4)
                    if part == 1:
                        A([Bbank[3]], [Bcst[par]], lambda e: e.activation(out=cst[par][:, 4:8], in_=banks[3][:, 400 + 4 * par:404 + 4 * par], func=AF.Copy))
                        A([BZ[par]], [Bcst[par]], lambda e: e.activation(out=cst[par][:, 0:4], in_=Z[par][:, :, LS - 1], func=AF.Copy))
                        G([Bcst[par], Btab], [Bcst[par]], lambda e: e.tensor_tensor(out=cst[par][:], in0=cst[par][:], in1=CLN[:, q, :], op=ALU.mult))
                        G([Bcst[par]], [Bcar[q]], lambda e: e.tensor_tensor(out=carry[:, gs], in0=cst[par][:, 0:4], in1=cst[par][:, 4:8], op=ALU.add))
                        return
                    for j in range(4):
                        g = q * 4 + j
                        V([Bt1[par], Bcar[q]], [BZ[par]], lambda e: e.tensor_tensor_scan(
                            out=Z[par][:, j, 0:LS], data0=Rm[:, g:g + 1].to_broadcast([128, LS]),
                            data1=t1[par][:, j * 128:j * 128 + LS], initial=carry[:, g:g + 1],
                            op0=ALU.mult, op1=ALU.add))
                    T([BZ[par], Bswp], [Bbank[3]], lambda e: e.matmul(
                        banks[3][:, 400 + 4 * par:404 + 4 * par], lhsT=swp[:], rhs=Z[par][:, :, LS - 1], start=True, stop=True))

                def s5B2(t, q, par, own, part):
                    i = t % 2
                    gb = q // 2
                    gs = slice(q * 4, q * 4 + 4)
                    if own and part == 0:
                        G([BZ[par], Btab], [BZ12[par]], lambda e: e.tensor_tensor(
                            out=Z1[par][:].rearrange("p a b -> p (a b)"), in0=Z[par][:].rearrange("p a b -> p (a b)"),
                            in1=COS[:, gs, :].rearrange("p a b -> p (a b)"), op=ALU.mult))
                        G([BZ[par], Btab], [BZ12[par]], lambda e: e.tensor_tensor(
                            out=Z2[par][:].rearrange("p a b -> p (a b)"), in0=Z[par][:].rearrange("p a b -> p (a b)"),
                            in1=SIN[:, gs, :].rearrange("p a b -> p (a b)"), op=ALU.mult))

                    if own and part == 1:
                        def mmY(e):
                            o = banks[2][:, gb * 128:(gb + 1) * 128]
                            if q % 2 == 0:
                                e.matmul(o, lhsT=diagD[:, gb, :], rhs=uT[i][:, gb, :], start=True, stop=False)
                            for j in range(4):
                                e.matmul(o, lhsT=Cw1[:, q * 4 + j, :], rhs=Z1[par][:, j, :], start=False, stop=False)
                                r = e.matmul(o, lhsT=Cw2[:, q * 4 + j, :], rhs=Z2[par][:, j, :], start=False,
                                             stop=(q % 2 == 1 and j == 3))
                            return r
                        T([BZ12[par], Bw5, BuT[i]], [Bbank[2]], mmY)

                def s5end(t):
                    oc = (t - OWN0) * 128
                    A([Bbank[2]], [Bgl], lambda e: e.activation(
                        out=gl[:].rearrange("p a b -> p (a b)"), in_=banks[2][:, :], func=AF.Gelu_apprx_tanh))
                    G([Bgl], [Bgh], lambda e: e.tensor_scalar(
                        out=gh[:].rearrange("p a b -> p (a b)"), in0=gl[:].rearrange("p a b -> p (a b)"),
                        scalar1=0.5, scalar2=0.0, op0=ALU.mult, op1=ALU.add))

                    def mmG(e):
                        for mo in range(4):
                            for k in range(4):
                                r = e.matmul(banks[2][:, mo * 128:(mo + 1) * 128],
                                             lhsT=w_glu[:, k, mo * 128:(mo + 1) * 128], rhs=gl[:, k, :],
                                             start=(k == 0), stop=(k == 3))
                        return r
                    T([Bgl, Bwg], [Bbank[2]], mmG)
                    for mo in range(4):
                        A([Bbank[2], Bdl], [Bth], lambda e: e.activation(
                            out=th[:, mo, :], in_=banks[2][:, mo * 128:(mo + 1) * 128], func=AF.Tanh,
                            scale=0.5, bias=bglu[:, mo:mo + 1]))
                    V([Bth, Bgh], [BobT], lambda e: e.scalar_tensor_tensor(
                        out=obT[:, :, oc:oc + 128], in0=th[:], scalar=1.0, in1=gh[:], op0=ALU.add, op1=ALU.mult))

                units = [(t, q) for t in range(I.KVC, NW) for q in range(8)]
                NU = len(units)
                if I.KVC > 0:
                    load_tile(0)
                if I.KVC > 0:
                    cst_ = []
                    for t in range(I.KVC):
                        stp, post = stage1_steps(t)
                        cst_.append([stp[0], stp[7], post[0:1], post[1:3], post[3:4], post[4:5], post[5:8], post[8:10]])
                    for step in range(I.KVC + 8):
                        for k in reversed(range(8)):
                            ti = step - k
                            if 0 <= ti < I.KVC:
                                for fn in cst_[ti][k]:
                                    fn()
                stage1(I.KVC)
                pend_post = None
                cur_steps = None
                def unit(ix):
                    return units[ix] if 0 <= ix < NU else None
                for it in range(-3, NU + 2):
                    u3, u2, u1, u0, um = unit(it + 3), unit(it + 2), unit(it + 1), unit(it), unit(it - 1)
                    if u2 is not None:
                        ta_, qa_ = u2
                        if ta_ + 1 < NW:
                            if qa_ == 0:
                                cur_steps = stage1_steps(ta_ + 1)
                            for fn in cur_steps[0][qa_]:
                                fn()
                        if qa_ < 5 and pend_post is not None:
                            for fn in pend_post[qa_ * 2:qa_ * 2 + 2]:
                                fn()
                        if qa_ == 7 and ta_ + 1 < NW:
                            pend_post = cur_steps[1]
                    if u0 is not None:
                        s5B2(u0[0], u0[1], it % 2, u0[0] >= OWN0, 0)
                    if u3 is not None and (u3[1] != 0 or u3[0] == units[0][0] or True):
                        s5A(u3[0], u3[1], (it + 3) % 2, 0)
                    if u2 is not None:
                        s5A(u2[0], u2[1], (it + 2) % 2, 1)
                    if u1 is not None:
                        s5B(u1[0], u1[1], (it + 1) % 2, u1[0] >= OWN0, 0)
                    if u0 is not None:
                        s5B(u0[0], u0[1], it % 2, u0[0] >= OWN0, 1)
                    if um is not None:
                        s5B2(um[0], um[1], (it - 1) % 2, um[0] >= OWN0, 1)
                        if um[1] == 7 and um[0] >= OWN0:
                            s5end(um[0])
                for q in range(8):
                    if Bcar[q].w is not None:
                        Bcarry.r.append(Bcar[q].w)
                S.dma("sp", I.s5out, carry[:], Bcar, [])
                S.barrier()
            if stop_after <= 1:
                pk.close()
                return
            QT = sb([96, NH, NOWN * 128], BF16, pk); BQT = Buf("QT")
            with ExitStack() as p2:
                g_qa = sb([128, DIN_Q], F32, p2); gqk = sb([128, QKD], F32, p2); gkh = sb([128, QKD], F32, p2)
                S.dma("sp", g_qa[:], g_qa_d, [], [Bg])
                S.dma("sp", gqk[:], g_qh_d, [], [Bg])
                S.dma("sp", gkh[:], g_kh_d, [], [Bg])
                V([Bg], [Bg], lambda e: e.tensor_tensor(out=gqk[:], in0=gqk[:], in1=gkh[:], op=ALU.mult))
                w_in_q = sb([128, 8, DIN_Q], BF16, p2); Bwq = Buf("w_in_q")
                w_qb = sb([128, 3, NH * QKD], BF16, p2); Bwqb = Buf("w_qb")
                S.dma("pool", w_in_q[:], w_in.rearrange("(k p) n -> p k n", p=128)[:, :, 0:DIN_Q], [], [Bwq])
                S.dma("pool", w_qb[:], w_q_b.rearrange("(k p) n -> p k n", p=128), [], [Bwqb])
                C4 = 4
                CC = 12

                def mk(n, shape, dt, nm):
                    return [sb(shape, dt, p2) for _ in range(n)], [Buf("%s%d" % (nm, k)) for k in range(n)]
                xf, Bxf = mk(C4, [128, D], F32, "xf")
                cs_t, Bcs = mk(CC, [128, 2, 16], F32, "cs")
                xs_b, Bxs = mk(C4, [128, D], BF16, "xs")
                xTq, BxTq = mk(C4, [128, 8, 128], BF16, "xTq")
                stq, Bstq = mk(C4, [128, 8], F32, "st")
                cq, Bcq = mk(C4, [128, DIN_Q], BF16, "cq")
                cqT, BcqT = mk(C4, [128, 3, 128], BF16, "cqT")
                qf, Bqf = mk(C4, [128, NH, QKD], F32, "qf")
                qb, Bqb = mk(C4, [128, NH, QKD], BF16, "qb")
                rt, Brt = mk(C4, [128, 4, NH, 16], F32, "rt")
                sq8, Bsq8 = mk(C4, [128, 8], F32, "sq8")
                junk2 = sb([128, 768], F32, p2); Bjunk2 = Buf("junk2")
                hg = ((3, 0, 5), (4, 5, 3))

                def q_stages(t):
                    c = t % C4
                    cc = t % CC
                    oc = (t - OWN0) * 128
                    tt = t - I.KVC
                    st = stq[c]; Bst = Bstq[c]
                    cs_, sn_ = cs_t[cc][:, 0, :], cs_t[cc][:, 1, :]

                    def s0():
                        S.dma("sp", xf[c][:], I.x[tt * 128:(tt + 1) * 128, :], [], [Bxf[c]])
                        S.dma("sp", cs_t[cc][:, 0, :], I.cos[:, tt, :], [], [Bcs[cc]])
                        S.dma("sp", cs_t[cc][:, 1, :], I.sin[:, tt, :], [], [Bcs[cc]])

                    def s1():
                        A([Bxf[c]], [Bjunk, Bst], lambda e: e.activation(out=junk[:], in_=xf[c][:], func=AF.Square, accum_out=st[:, 0:1]))
                        rstd_from(st[:, 0:1], Bst, 1, D, st[:, 1:2], Bst)

                    def s2():
                        V([Bxf[c], Bst, Bg], [Bxs[c]], lambda e: e.scalar_tensor_tensor(
                            out=xs_b[c][:], in0=xf[c][:], scalar=st[:, 1:2], in1=g_mix[:], op0=ALU.mult, op1=ALU.mult))

                    def s3():
                        def tr(e):
                            for k in range(8):
                                r = e.transpose(bankb[0][:, k * 128:(k + 1) * 128], xs_b[c][:, k * 128:(k + 1) * 128], ident[:])
                            return r
                        T([Bxs[c], Bident], [Bbank[0]], tr)
                        A([Bbank[0]], [BxTq[c]], lambda e: e.activation(
                            out=xTq[c][:].rearrange("p a b -> p (a b)"), in_=bankb[0][:, 0:1024], func=AF.Copy))

                    def s4():
                        def mm_q(e):
                            for k in range(8):
                                r = e.matmul(banks[1][:, 0:DIN_Q], lhsT=xTq[c][:, k, :], rhs=w_in_q[:, k, :], start=(k == 0), stop=(k == 7))
                            return r
                        T([BxTq[c], Bwq], [Bbank[1]], mm_q)
                        A([Bbank[1]], [Bjunk, Bst], lambda e: e.activation(
                            out=junk[:, 0:DIN_Q], in_=banks[1][:, 0:DIN_Q], func=AF.Square, accum_out=st[:, 2:3]))
                        rstd_from(st[:, 2:3], Bst, 1, DIN_Q, st[:, 3:4], Bst)

                    def s5():
                        V([Bbank[1], Bst, Bg], [Bcq[c]], lambda e: e.scalar_tensor_tensor(
                            out=cq[c][:], in0=banks[1][:, 0:DIN_Q], scalar=st[:, 3:4], in1=g_qa[:], op0=ALU.mult, op1=ALU.mult))

                        def tr3(e):
                            for k in range(3):
                                r = e.transpose(bankb[2][:, k * 128:(k + 1) * 128], cq[c][:, k * 128:(k + 1) * 128], ident[:])
                            return r
                        T([Bcq[c], Bident], [Bbank[2]], tr3)
                        V([Bbank[2]], [BcqT[c]], lambda e: e.tensor_copy(
                            out=cqT[c][:].rearrange("p a b -> p (a b)"), in_=bankb[2][:, 0:384]))

                    def s6():
                        def mm_qb(e):
                            for (bk, h0, nh_) in hg:
                                for k in range(3):
                                    r = e.matmul(banks[bk][:, 0:nh_ * QKD], lhsT=cqT[c][:, k, :],
                                                 rhs=w_qb[:, k, h0 * QKD:(h0 + nh_) * QKD], start=(k == 0), stop=(k == 2))
                            return r
                        T([BcqT[c], Bwqb], [Bbank[3], Bbank[4]], mm_qb)
                        for (bk, h0, nh_) in hg:
                            pv = banks[bk][:, 0:nh_ * QKD].rearrange("p (h d) -> p h d", d=QKD)
                            hs = slice(h0, h0 + nh_)
                            A([Bbank[bk]], [Bqf[c]], lambda e: e.activation(out=qf[c][:, hs, :], in_=pv, func=AF.Copy))

                    def s7():
                        x1 = qf[c][:, :, 64:80]; x2 = qf[c][:, :, 80:96]
                        cb = bc_mid(cs_, NH); sbb = bc_mid(sn_, NH)
                        V([Bqf[c], Bcs[cc]], [Brt[c]], lambda e: e.tensor_tensor(out=rt[c][:, 0, :, :], in0=x1, in1=cb, op=ALU.mult))
                        V([Bqf[c], Bcs[cc]], [Brt[c]], lambda e: e.tensor_tensor(out=rt[c][:, 1, :, :], in0=x2, in1=sbb, op=ALU.mult))
                        V([Bqf[c], Bcs[cc]], [Brt[c]], lambda e: e.tensor_tensor(out=rt[c][:, 2, :, :], in0=x2, in1=cb, op=ALU.mult))
                        V([Bqf[c], Bcs[cc]], [Brt[c]], lambda e: e.tensor_tensor(out=rt[c][:, 3, :, :], in0=x1, in1=sbb, op=ALU.mult))
                        G([Brt[c]], [Bqf[c]], lambda e: e.tensor_tensor(out=qf[c][:, :, 64:80], in0=rt[c][:, 0, :, :], in1=rt[c][:, 1, :, :], op=ALU.subtract))
                        G([Brt[c]], [Bqf[c]], lambda e: e.tensor_tensor(out=qf[c][:, :, 80:96], in0=rt[c][:, 2, :, :], in1=rt[c][:, 3, :, :], op=ALU.add))

                    def s8():
                        A([Bqf[c]], [Bjunk2], lambda e: e.activation(
                            out=junk2[:, 0:768], in_=qf[c][:].rearrange("p a b -> p (a b)"), func=AF.Square))
                        V([Bjunk2], [Bsq8[c]], lambda e: e.tensor_reduce(
                            out=sq8[c][:], in_=junk2[:, 0:768].rearrange("p (a b) -> p a b", b=QKD), axis=AX.X, op=ALU.add))
                        rstd_from(sq8[c][:], Bsq8[c], 8, QKD, sq8[c][:], Bsq8[c])

                    def s9():
                        V([Bqf[c], Bsq8[c]], [Bqf[c]], lambda e: e.tensor_tensor(out=qf[c][:], in0=qf[c][:], in1=bc_last(sq8[c][:], QKD), op=ALU.mult))
                        V([Bqf[c], Bg], [Bqb[c]], lambda e: e.tensor_tensor(out=qb[c][:], in0=qf[c][:], in1=bc_mid(gqk[:], NH), op=ALU.mult))

                    def s10():
                        def tr8(e):
                            for h in range(NH):
                                r = e.transpose(bankb[5][0:96, h * 128:(h + 1) * 128], qb[c][:, h, :], ident[:])
                            return r
                        T([Bqb[c], Bident], [Bbank[5]], tr8)
                        V([Bbank[5]], [BQT], lambda e: e.tensor_copy(
                            out=QT[:, :, oc:oc + 128], in_=bankb[5][0:96, 0:1024].rearrange("p (a b) -> p a b", b=128)))
                    return [s0, s1, s2, s3, s4, s5, s6, s7, s8, s9, s10]

                tiles = list(range(OWN0, NW))
                stg = [q_stages(t) for t in tiles]
                NSTG = 11
                for step in range(len(tiles) + NSTG):
                    for k in reversed(range(NSTG)):
                        ti = step - k
                        if 0 <= ti < len(tiles):
                            stg[ti][k]()
                S.barrier()
            if dbg and I.name == "p":
                S.dma("sp", di["dbg_ob"], obT[:], [BobT], [])
                S.dma("sp", di["dbg_q"], QT[:], [BQT], [])
            if stop_after <= 2:
                pk.close()
                return
            with ExitStack() as p3:
                NK = NW * 128
                ktb = [sb([96, NK], BF16, p3) for _ in range(2)]; Bktb = [Buf("ktb0"), Buf("ktb1")]
                vxb = [sb([128, NW, 128], BF16, p3) for _ in range(2)]; Bvxb = [Buf("vxb0"), Buf("vxb1")]
                vld = sb([128, NW], F32, p3); Bvld = Buf("vld")
                PT = [sb([128, 512], BF16, p3) for _ in range(4)]; BPT = [Buf("PT%d" % i) for i in range(4)]
                msk = sb([128, 4, 512], BF16, p3); Bmsk = Buf("msk")
                rec = sb([128, 512], F32, p3); Brec = Buf("rec")
                S.dma("sp", vld[:], I.valid, [], [Bvld])
                if I.masked:
                    S.dma("pool", msk[:], masks_d, [], [Bmsk])
                allkr = BkrotT[:NW]
                A(allkr, [Bktb[0]], lambda e: e.activation(out=ktb[0][64:96, :], in_=krotT[64:96, 0:NK], func=AF.Copy))
                V(allkr, [Bktb[1]], lambda e: e.tensor_copy(out=ktb[1][64:96, :], in_=krotT[64:96, 0:NK]))
                for b_ in range(2):
                    ooff = 64 if b_ == 0 else 0
                    V([Bvld], [Bvxb[b_]], lambda e: e.tensor_copy(
                        out=vxb[b_][:, :, ooff:ooff + 64], in_=bc_last(vld[:], 64)))
                Bscr = [Buf("scr%d" % k) for k in range(8)]
                P3S = int(os.environ.get("P3S", "99"))
                if I.name == "p" and P3S >= 1:
                    for k in range(8):
                        S.dma("pool", scr_up[k * 128:(k + 1) * 128, :], w_up_d[k * 128:(k + 1) * 128, :], [], [Bscr[k]])
                    I.Bscr = Bscr
                qblocks = []
                t = OWN0
                while t < NW:
                    t1_ = min(NW, (t // 4 + 1) * 4)
                    qblocks.append((t, t1_))
                    t = t1_
                PSB = [0, 1, 4, 5]

                def kv_tasks(h):
                    b_ = h % 2
                    voff = 0 if b_ == 0 else 64
                    tasks = []
                    c0 = 0
                    while c0 < NK:
                        n = min(512, NK - c0)

                        def tk(c0=c0, n=n):
                            def mmK(e):
                                for k in range(2):
                                    r = e.matmul(banks[6][0:64, 0:n], lhsT=w_kvb[:, k, h * 128:h * 128 + 64],
                                                 rhs=ckvT[:, k, c0:c0 + n], start=(k == 0), stop=(k == 1))
                                return r
                            T(BckvT[c0 // 128:(c0 + n) // 128] + [Bwkvb], [Bbank[6]], mmK)
                            V([Bbank[6]], [Bktb[b_]], lambda e: e.tensor_copy(out=ktb[b_][0:64, c0:c0 + n], in_=banks[6][0:64, 0:n]))
                        tasks.append(tk)
                        c0 += n
                    t0 = 0
                    while t0 < NW:
                        nt = min(4, NW - t0)

                        def tv(t0=t0, nt=nt):
                            def mmV(e):
                                for j in range(nt):
                                    for k in range(2):
                                        r = e.matmul(banks[7][:, j * 64:(j + 1) * 64],
                                                     lhsT=ckvT[:, k, (t0 + j) * 128:(t0 + j + 1) * 128],
                                                     rhs=w_kvb[:, k, h * 128 + 64:h * 128 + 128], start=(k == 0), stop=(k == 1))
                                return r
                            T(BckvT[t0:t0 + nt] + [Bwkvb], [Bbank[7]], mmV)
                            V([Bbank[7]], [Bvxb[b_]], lambda e: e.tensor_copy(
                                out=vxb[b_][:, t0:t0 + nt, voff:voff + 64],
                                in_=banks[7][:, 0:nt * 64].rearrange("p (a b) -> p a b", b=64)))
                        tasks.append(tv)
                        t0 += nt
                    return tasks

                def produce_kv(h):
                    for tk in kv_tasks(h):
                        tk()

                steps = []
                for h in range(NH):
                    for qi, (ta, tb) in enumerate(qblocks):
                        for kt in range(tb):
                            steps.append((h, qi, ta, tb, kt))
                NS = len(steps)
                LAG = 2
                gidx = {}
                for (h, qi, ta, tb, kt) in steps:
                    gidx.setdefault((h, qi), len(gidx))
                hstart = {}
                for jj, st_ in enumerate(steps):
                    hstart.setdefault(st_[0], jj)
                pending = []
                if P3S >= 2:
                    produce_kv(0)
                for j in range(NS + LAG):
                    if P3S < 3:
                        break
                    if j < NS:
                        h, qi, ta, tb, kt = steps[j]
                        b_ = h % 2
                        nq = (tb - ta) * 128
                        qc = (ta - OWN0) * 128
                        sbk = PSB[j % 4]
                        pi = j % 4
                        if j == hstart[h] + LAG and h + 1 < NH:
                            pending = kv_tasks(h + 1)
                        if pending:
                            pending.pop(0)()
                        off = (kt - ta) * 128 if (I.masked and kt > ta) else 0
                        T([Bktb[b_], BQT], [Bbank[sbk]], lambda e: e.matmul(
                            banks[sbk][:, off:nq], lhsT=ktb[b_][:, kt * 128:(kt + 1) * 128],
                            rhs=QT[:, h, qc + off:qc + nq], start=True, stop=True))
                        A([Bbank[sbk], Bkscale[kt]], [BPT[pi]], lambda e: e.activation(
                            out=PT[pi][:, off:nq], in_=banks[sbk][:, off:nq], func=AF.Exp, scale=kscale[:, kt, h:h + 1]))
                        if I.masked and kt >= ta:
                            V([BPT[pi], Bmsk], [BPT[pi]], lambda e: e.tensor_tensor(
                                out=PT[pi][:, off:nq], in0=PT[pi][:, off:nq], in1=msk[:, kt - ta, off:nq], op=ALU.mult))
                    jp = j - LAG
                    if jp >= 0:
                        h, qi, ta, tb, kt = steps[jp]
                        b_ = h % 2
                        nq = (tb - ta) * 128
                        qc = (ta - OWN0) * 128
                        ob = 2 + gidx[(h, qi)] % 2
                        pp_ = jp % 4
                        off = (kt - ta) * 128 if (I.masked and kt > ta) else 0
                        T([Bvxb[b_], BPT[pp_]], [Bbank[ob]], lambda e: e.matmul(
                            banks[ob][:, off:nq], lhsT=vxb[b_][:, kt, :], rhs=PT[pp_][:, off:nq],
                            start=(kt == 0), stop=(kt == tb - 1)))
                        if kt == tb - 1:
                            dlo = 64 if b_ == 0 else 0
                            olo = 0 if b_ == 0 else 64
                            V([Bbank[ob]], [Brec], lambda e: e.tensor_scalar(
                                out=rec[dlo:dlo + 64, 0:nq], in0=banks[ob][dlo:dlo + 64, 0:nq], scalar1=1e-30, scalar2=None, op0=ALU.max))
                            V([Brec], [Brec], lambda e: e.reciprocal(out=rec[dlo:dlo + 64, 0:nq], in_=rec[dlo:dlo + 64, 0:nq]))
                            V([Bbank[ob], Brec], [BoaT], lambda e: e.tensor_tensor(
                                out=oaT[olo:olo + 64, h // 2, qc:qc + nq], in0=banks[ob][olo:olo + 64, 0:nq],
                                in1=rec[dlo:dlo + 64, 0:nq], op=ALU.mult))
                S.barrier()
            if dbg and I.name == "p":
                S.dma("sp", di["dbg_oa"], oaT[:], [BoaT], [])
            pk.close()
            if stop_after <= 3:
                return
            yield
            if "v" not in P4:
                p4 = ExitStack(); P4["stack"] = p4
                g_ffn = sb([128, D], F32, p4)
                S.dma("sp", g_ffn[:], g_ffn_d, [], [Bg])
                w_out = sb([128, 8, D], BF16, p4); Bwo = Buf("w_out")
                w_dn = sb([128, NFF, D], BF16, p4); Bwd = [Buf("w_dn%d" % i) for i in range(NFF)]
                S.dma("pool", w_out[:], w_out_d.rearrange("(k p) n -> p k n", p=128), [], [Bwo])
                P4["wdn_pending"] = True
                wdw = sb([128, 2 * NFF, 3], F32, p4); bdw = sb([128, 2 * NFF], F32, p4); Bdw = Buf("dw")
                S.dma("sp", wdw[:], wdw_d, [], [Bdw])
                S.dma("sp", bdw[:], bdw_d, [], [Bdw])
                tail = sb([128, 2 * NFF, 2], F32, p4); Btail = Buf("tail")
                corr = sb([128, 2 * NFF, 2], F32, p4); Bcorr = Buf("corr")
                wu = [sb([128, 8, 512], BF16, p4) for _ in range(2)]; Bwu = [Buf("wu%d" % i) for i in range(2)]
                xr = [sb([128, D], F32, p4) for _ in range(2)]; Bxr = [Buf("xr0"), Buf("xr1")]
                hf_ = sb([128, 4, D], F32, p4); Bhf = [Buf("hf%d" % i) for i in range(4)]
                hn = [sb([128, D], BF16, p4) for _ in range(2)]; Bhn = [Buf("hn0"), Buf("hn1")]
                hnT = sb([128, 8, 512], BF16, p4); BhnT = Buf("hnT")
                aT = sb([128, NFF, 512], BF16, p4); BaT = [Buf("aT%d" % i) for i in range(NFF)]
                st = sb([128, 8], F32, p4); Bst = Buf("st")
                cg = [sb([128, 512], F32, p4) for _ in range(3)]; Bcg = [Buf("cg%d" % k) for k in range(3)]
                cv = [sb([128, 512], F32, p4) for _ in range(3)]; Bcv = [Buf("cv%d" % k) for k in range(3)]
                sg = sb([128, 512], F32, p4); Bsg = Buf("sg")
                yo = [sb([128, D], F32, p4) for _ in range(2)]; Byo = [Buf("yo0"), Buf("yo1")]
                P4["v"] = (g_ffn, w_out, Bwo, w_dn, Bwd, wdw, bdw, Bdw, tail, Btail, wu, Bwu, xr, Bxr, hf_, Bhf,
                           hn, Bhn, hnT, BhnT, aT, BaT, st, Bst, cg, Bcg, cv, Bcv, sg, Bsg, yo, Byo, [0], corr, Bcorr)
            (g_ffn, w_out, Bwo, w_dn, Bwd, wdw, bdw, Bdw, tail, Btail, wu, Bwu, xr, Bxr, hf_, Bhf,
             hn, Bhn, hnT, BhnT, aT, BaT, st, Bst, cg, Bcg, cv, Bcv, sg, Bsg, yo, Byo, wui, corr, Bcorr) = P4["v"]
            if True:
                if I.convh is None:
                    V([], [Btail], lambda e: e.memset(tail[:], 0.0))
                else:
                    S.dma("sp", tail[:], I.convh, [], [Btail])
                qblocks = []
                t = OWN0
                while t < NW:
                    t1_ = min(NW, (t // 4 + 1) * 4)
                    qblocks.append((t, t1_))
                    t = t1_
                first_block = True
                for (ta, tb) in qblocks:
                    ntl = tb - ta
                    nq = ntl * 128
                    qc = (ta - OWN0) * 128
                    def wout_head(j):
                        t = ta + j
                        i = t % 2
                        tt = t - I.KVC
                        hb = 2 * (j % 2)
                        S.dma("sp", xr[i][:], I.x[tt * 128:(tt + 1) * 128, :], [], [Bxr[i]])

                        def mmH(e):
                            for nh_ in range(2):
                                for k in range(8):
                                    src = oaT if k < 4 else obT
                                    r = e.matmul(banks[hb + nh_][:, :], lhsT=src[:, k % 4, qc + j * 128:qc + (j + 1) * 128],
                                                 rhs=w_out[:, k, nh_ * 512:(nh_ + 1) * 512], start=(k == 0), stop=(k == 7))
                            return r
                        T([BoaT, BobT, Bwo], [Bbank[hb], Bbank[hb + 1]], mmH)
                        for nh_ in range(2):
                            V([Bbank[hb + nh_], Bxr[i]], [Bhf[j]], lambda e: e.tensor_tensor(
                                out=hf_[:, j, nh_ * 512:(nh_ + 1) * 512], in0=banks[hb + nh_][:, :],
                                in1=xr[i][:, nh_ * 512:(nh_ + 1) * 512], op=ALU.add))
                        A([Bhf[j]], [Bjunk, Bst], lambda e: e.activation(out=junk[:], in_=hf_[:, j, :], func=AF.Square, accum_out=st[:, 2 * (j % 2):2 * (j % 2) + 1]))
                        rstd_from(st[:, 2 * (j % 2):2 * (j % 2) + 1], Bst, 1, D, st[:, 2 * (j % 2) + 1:2 * (j % 2) + 2], Bst)
                        V([Bhf[j], Bst, Bg], [Bhn[j % 2]], lambda e: e.scalar_tensor_tensor(
                            out=hn[j % 2][:], in0=hf_[:, j, :], scalar=st[:, 2 * (j % 2) + 1:2 * (j % 2) + 2], in1=g_ffn[:], op0=ALU.mult, op1=ALU.mult))

                    def wout_tail(j):
                        def trh(e):
                            for k in range(8):
                                r = e.transpose(bankb[4][:, k * 128:(k + 1) * 128], hn[j % 2][:, k * 128:(k + 1) * 128], ident[:])
                            return r
                        T([Bhn[j % 2], Bident], [Bbank[4]], trh)
                        A([Bbank[4]], [BhnT], lambda e: e.activation(
                            out=hnT[:, :, j * 128:(j + 1) * 128], in_=bankb[4][:, 0:1024].rearrange("p (a b) -> p a b", b=128),
                            func=AF.Copy))
                    for j in range(ntl + 1):
                        if j < ntl:
                            wout_head(j)
                        if j >= 1:
                            wout_tail(j - 1)
                    V([Btail, Bdw], [Bcorr], lambda e: e.tensor_tensor(out=corr[:, :, 0], in0=tail[:, :, 0], in1=wdw[:, :, 0], op=ALU.mult))
                    V([Btail, Bdw], [Bcorr], lambda e: e.tensor_tensor(out=corr[:, :, 1], in0=tail[:, :, 1], in1=wdw[:, :, 1], op=ALU.mult))
                    V([Bcorr], [Bcorr], lambda e: e.tensor_tensor(out=corr[:, :, 0], in0=corr[:, :, 0], in1=corr[:, :, 1], op=ALU.add))
                    V([Btail, Bdw, Bcorr], [Bcorr], lambda e: e.tensor_tensor(out=corr[:, :, 1], in0=tail[:, :, 1], in1=wdw[:, :, 0], op=ALU.mult))
                    for f in range(NFF):
                        if P4.get("wdn_pending"):
                            S.dma("pool", w_dn[:, f, :], w_down_d[f * 128:(f + 1) * 128, :], [], [Bwd[f]])
                            if f == NFF - 1:
                                P4["wdn_pending"] = False
                        if f % 2 == 0:
                            wui[0] += 1
                            wi = wui[0] % 2
                            scv = scr_up.rearrange("(k p) c -> p k c", p=128)
                            S.dma("sp", wu[wi][:, :, 0:256], scv[:, :, f * 128:f * 128 + 256], I.Bscr, [Bwu[wi]])
                            S.dma("sp", wu[wi][:, :, 256:512], scv[:, :, DFF + f * 128:DFF + f * 128 + 256], I.Bscr, [Bwu[wi]])
                        wi = wui[0] % 2
                        bg = 2 + 2 * (f % 3)
                        bv = bg + 1
                        jo = (f % 2) * 128

                        def mmU(e):
                            for (bk, c0) in ((bg, jo), (bv, 256 + jo)):
                                for k in range(8):
                                    r = e.matmul(banks[bk][:, 0:nq], lhsT=wu[wi][:, k, c0:c0 + 128], rhs=hnT[:, k, 0:nq],
                                                 start=(k == 0), stop=(k == 7))
                            return r
                        T([Bwu[wi], BhnT], [Bbank[bg], Bbank[bv]], mmU)
                        ci = f % 3
                        for (bk, ch, dst, Bdst) in ((bg, f, cg[ci], Bcg[ci]), (bv, NFF + f, cv[ci], Bcv[ci])):
                            ps_ = banks[bk]
                            A([Bbank[bk], Bdw], [Bdst], lambda e: e.activation(
                                out=dst[:, 0:nq], in_=ps_[:, 0:nq], func=AF.Identity, scale=wdw[:, ch, 2:3], bias=bdw[:, ch:ch + 1]))
                            V([Bbank[bk], Bdw, Bdst], [Bdst], lambda e: e.scalar_tensor_tensor(
                                out=dst[:, 1:nq], in0=ps_[:, 0:nq - 1], scalar=wdw[:, ch, 1:2], in1=dst[:, 1:nq], op0=ALU.mult, op1=ALU.add))
                            V([Bbank[bk], Bdw, Bdst], [Bdst], lambda e: e.scalar_tensor_tensor(
                                out=dst[:, 2:nq], in0=ps_[:, 0:nq - 2], scalar=wdw[:, ch, 0:1], in1=dst[:, 2:nq], op0=ALU.mult, op1=ALU.add))
                            G([Bcorr, Bdst], [Bdst], lambda e: e.tensor_tensor(
                                out=dst[:, 0:2], in0=dst[:, 0:2], in1=corr[:, ch, :], op=ALU.add))
                            e0 = I.tail_end(ta, tb)
                            A([Bbank[bk], Btail], [Btail], lambda e: e.activation(out=tail[:, ch, :], in_=ps_[:, e0 - 2:e0], func=AF.Copy))
                        A([Bcg[ci]], [Bsg], lambda e: e.activation(out=sg[:, 0:nq], in_=cg[ci][:, 0:nq], func=AF.Silu))
                        G([Bsg, Bcv[ci]], [BaT[f]], lambda e: e.tensor_tensor(
                            out=aT[:, f, 0:nq], in0=sg[:, 0:nq], in1=cv[ci][:, 0:nq], op=ALU.mult))
                    if first_block and I.name == "p":
                        V([Btail, Bhval], [Btail], lambda e: e.tensor_scalar(
                            out=tail[:].rearrange("p a b -> p (a b)"), in0=tail[:].rearrange("p a b -> p (a b)"),
                            scalar1=hval[:, 0:1], scalar2=None, op0=ALU.mult))
                    first_block = False
                    for j in range(ntl):
                        t = ta + j
                        yi = j % 2

                        db = 2 * (j % 2)

                        for f0 in range(0, NFF, 6):
                            f1 = min(NFF, f0 + 6)

                            def mmD(e):
                                for f in range(f0, f1):
                                    for nh_ in range(2):
                                        r = e.matmul(banks[db + nh_][:, :], lhsT=aT[:, f, j * 128:(j + 1) * 128],
                                                     rhs=w_dn[:, f, nh_ * 512:(nh_ + 1) * 512], start=(f == 0), stop=(f == NFF - 1))
                                return r
                            T(BaT[f0:f1] + Bwd[f0:f1], [Bbank[db], Bbank[db + 1]], mmD)
                        for nh_ in range(2):
                            V([Bbank[db + nh_], Bhf[j]], [Byo[yi]], lambda e: e.tensor_tensor(
                                out=yo[yi][:, nh_ * 512:(nh_ + 1) * 512], in0=banks[db + nh_][:, :],
                                in1=hf_[:, j, nh_ * 512:(nh_ + 1) * 512], op=ALU.add))
                        if I.out_rows(t) is not None:
                            r0, n = I.out_rows(t)
                            S.dma("sp", I.y[r0:r0 + n, :], yo[yi][0:n, :], [Byo[yi]], [])
                S.dma("sp", I.convout, tail[:], [Btail], [])

        Ip = Inst()
        Ip.name = "p"; Ip.NW = 32; Ip.OWN0 = 15; Ip.KVC = 0; Ip.LS = 128
        Ip.x = xw; Ip.cos = cos_p; Ip.sin = sin_p; Ip.valid = valid_p; Ip.masked = True
        Ip.cache_ckv = None; Ip.cache_kpe = None; Ip.s5init = None; Ip.convh = None
        Ip.nkv = nkv_p; Ip.nkr = nkr_p; Ip.y = y_p; Ip.s5out = s5_p; Ip.convout = conv_p
        Ip.out_rows = lambda t: ((t - 16) * 128, 128) if t >= 16 else None
        Ip.tail_end = lambda ta, tb: (tb - ta) * 128
        Ip.oaT, Ip.obT = oaT_p, obT_p
        gens = [run_instance(Ip)]
        next(gens[0], None)
        if do_sample and stop_after > 4:
            Is = Inst()
            Is.name = "s"; Is.NW = 17; Is.OWN0 = 16; Is.KVC = 16; Is.LS = 16
            Is.x = xsw; Is.cos = cos_s; Is.sin = sin_s; Is.valid = valid_s; Is.masked = False
            Is.cache_ckv = cache_ckv; Is.cache_kpe = cache_kpe; Is.s5init = s5init_s; Is.convh = convh_s
            Is.nkv = nkv_s; Is.nkr = nkr_s; Is.y = y_s; Is.s5out = s5_s; Is.convout = conv_s
            Is.out_rows = lambda t: (0, DEC) if t == 16 else None
            Is.tail_end = lambda ta, tb: DEC
            Is.Bscr = Ip.Bscr
            Is.oaT, Is.obT = oaT_s, obT_s
            gens.append(run_instance(Is))
            next(gens[1], None)
        for g_ in gens:
            next(g_, None)
        S.barrier()
        if "stack" in P4:
            P4["stack"].close()
        S.barrier()
        print("ops", S.nops, "waits", S.nwait)
    return nc


def _rope_tables(pos):
    inv = 10000.0 ** (-np.arange(0, ROPE, 2, dtype=np.float32) / ROPE)
    ang = pos.astype(np.float32)[:, None] * inv[None, :]
    return np.cos(ang).astype(np.float32), np.sin(ang).astype(np.float32)


def make_in_maps(inp):
    f32 = lambda a: np.ascontiguousarray(a, dtype=np.float32)
    bcast = lambda v, n: f32(np.broadcast_to(np.asarray(v).reshape(1, -1), (128, n)))
    common = {}
    for k in ("w_in", "w_q_b", "w_kv_b", "w_s5_glu", "w_out", "w_up", "w_down"):
        common[k] = f32(inp[k][0])
    common["g_mix_b"] = bcast(inp["g_mix_norm"][0], D)
    common["g_qa_b"] = bcast(inp["g_q_a"][0], DIN_Q)
    common["g_kva_b"] = bcast(inp["g_kv_a"][0], DKV)
    common["g_ffn_b"] = bcast(inp["g_ffn_norm"][0], D)
    common["g_qh_b"] = bcast(inp["g_q_head"][0], QKD)
    common["g_kh_b"] = bcast(inp["g_k_head"][0], QKD)
    dup = lambda a: f32(np.concatenate([a.T, a.T], axis=0))
    common["s5_are"] = dup(inp["s5_a_re"][0])
    common["s5_aim"] = dup(inp["s5_a_im"][0])
    common["s5_ldt"] = bcast(inp["s5_log_dt"][0], NG)
    bre = np.transpose(inp["s5_b_re"][0], (1, 0, 2))
    bim = np.transpose(inp["s5_b_im"][0], (1, 0, 2))
    common["s5_bst"] = f32(np.concatenate([bre, bim], axis=0))
    common["s5_bsw"] = f32(np.concatenate([bim, bre], axis=0))
    cre = inp["s5_c_re"][0].reshape(4, 128, 64)
    cim = inp["s5_c_im"][0].reshape(4, 128, 64)
    common["s5_cn"] = f32(np.transpose(np.concatenate([cre, cim], axis=2), (1, 0, 2)))
    common["s5_cns"] = f32(np.transpose(np.concatenate([cim, cre], axis=2), (1, 0, 2)))
    common["s5_dl"] = f32(inp["s5_d"][0].reshape(4, 128).T)
    common["s5_bglu"] = f32(inp["b_s5_glu"][0].reshape(4, 128).T)
    common["w_dw_l"] = f32(np.transpose(inp["w_dw"][0].reshape(3, 2 * NFF, 128), (2, 1, 0)))
    common["b_dw_l"] = f32(inp["b_dw"][0].reshape(2 * NFF, 128).T)
    pidx = np.arange(128)
    common["mask8"] = f32((pidx[:, None] // 16) == np.arange(8)[None, :])
    sg = np.where(pidx < 64, -1.0, 1.0)
    common["sgn"] = f32(np.stack([sg, -sg], axis=1))
    kk = (np.arange(4)[None, :, None] * 128 + pidx[:, None, None]) // 64
    qq = np.arange(512)[None, None, :] // 64
    common["masks"] = f32(kk <= qq)
    cos_s, sin_s = _rope_tables(PAST + np.arange(128))
    common["cos_s"] = f32(cos_s[:, None, :])
    common["sin_s"] = f32(sin_s[:, None, :])
    vs = np.ones((128, 17), np.float32)
    vs[:, 16] = (pidx < DEC)
    common["valid_s"] = vs
    maps = []
    for c in range(8):
        b, h = c // 2, c % 2
        m = dict(common)
        xw = np.zeros((SEQ, D), np.float32)
        if h == 0:
            xw[HALF:] = inp["x_prompt"][b, :HALF]
            pos = np.arange(SEQ) - HALF
        else:
            xw[:] = inp["x_prompt"][b]
            pos = np.arange(SEQ)
        m["xw"] = xw
        cp, sp_ = _rope_tables(np.maximum(pos, 0))
        m["cos_p"] = f32(cp.reshape(32, 128, 16).transpose(1, 0, 2))
        m["sin_p"] = f32(sp_.reshape(32, 128, 16).transpose(1, 0, 2))
        m["valid_p"] = f32((pos >= 0).reshape(32, 128).T)
        m["hval"] = np.full((128, 1), float(h), np.float32)
        xs = np.zeros((128, D), np.float32)
        xs[:DEC] = inp["x_sample"][c]
        m["xsw"] = xs
        m["cache_ckv"] = f32(inp["cache_kv_latent"][0, c])
        m["cache_kpe"] = f32(inp["cache_k_rope"][0, c])
        m["s5init_s"] = f32(np.concatenate([inp["state_s5_re"][0, c].T, inp["state_s5_im"][0, c].T], axis=0))
        m["convh_s"] = f32(np.transpose(inp["state_ffn_conv"][0, c].reshape(2, 2 * NFF, 128), (2, 1, 0)))
        maps.append(m)
    return maps


_NC_CACHE = {}


def kernel(**inputs):
    inp = {k: np.asarray(v) for k, v in inputs.items()}
    if "nc" not in _NC_CACHE:
        _NC_CACHE["nc"] = build_program()
    nc = _NC_CACHE["nc"]
    maps = make_in_maps(inp)
    res = run_bass_kernel_spmd(nc, maps, core_ids=list(range(8)))
    R = res.results
    yp = np.zeros((4, SEQ, D), np.float32)
    nkv = np.zeros((1, 4, SEQ, DKV), np.float32)
    nkr = np.zeros((1, 4, SEQ, ROPE), np.float32)
    s5re = np.zeros((1, 4, NG, 64), np.float32); s5im = np.zeros((1, 4, NG, 64), np.float32)
    conv = np.zeros((1, 4, 2, 2 * DFF), np.float32)
    ys = np.zeros((8, DEC, D), np.float32)
    nkvs = np.zeros((1, 8, DEC, DKV), np.float32); nkrs = np.zeros((1, 8, DEC, ROPE), np.float32)
    s5res = np.zeros((1, 8, NG, 64), np.float32); s5ims = np.zeros((1, 8, NG, 64), np.float32)
    convs = np.zeros((1, 8, 2, 2 * DFF), np.float32)
    unconv = lambda a: np.transpose(a, (2, 1, 0)).reshape(2, 2 * DFF)
    for c in range(8):
        b, h = c // 2, c % 2
        r = R[c]
        yp[b, h * HALF:(h + 1) * HALF] = r["y_p"]
        nkv[0, b, h * HALF:(h + 1) * HALF] = r["nkv_p"]
        nkr[0, b, h * HALF:(h + 1) * HALF] = r["nkr_p"]
        if h == 1:
            s5re[0, b] = r["s5_p"][0:64].T
            s5im[0, b] = r["s5_p"][64:128].T
            conv[0, b] = unconv(r["conv_p"])
        ys[c] = r["y_s"]
        nkvs[0, c] = r["nkv_s"]; nkrs[0, c] = r["nkr_s"]
        s5res[0, c] = r["s5_s"][0:64].T; s5ims[0, c] = r["s5_s"][64:128].T
        convs[0, c] = unconv(r["conv_s"])
    return (yp, ys, nkv, nkr, s5re, s5im, conv, nkvs, nkrs, s5res, s5ims, convs)
```

```python
import math
import os
import numpy as np
from contextlib import ExitStack
import ml_dtypes
import concourse.bass as bass
import concourse.mybir as mybir
from concourse.bass_utils import run_bass_kernel_spmd
from concourse.alu_op_type import AluOpType as ALU

F32 = mybir.dt.float32
BF16 = mybir.dt.bfloat16
AF = mybir.ActivationFunctionType
AX = mybir.AxisListType

D = 1024
NH = 8
QKD = 96
DIN_Q = 384
DKV = 256
ROPE = 32
S5W = 512
NG = 32
DFF = 2816
NFF = DFF // 128
EPS = 1e-6
ATTN_SCALE = 1.0 / math.sqrt(96.0)
GELU_C = math.sqrt(2.0 / math.pi)
SEQ = 4096
HALF = 2048
PAST = 2048
DEC = 16


NO_SELF_RAW = set()


class Buf:
    __slots__ = ("name", "w", "r")

    def __init__(self, name):
        self.name = name
        self.w = None
        self.r = []


class Sched:
    def __init__(self, nc, es, n_dma_sems=40):
        self.nc = nc
        self.eng = {"pe": nc.tensor, "act": nc.scalar, "dve": nc.vector,
                    "pool": nc.gpsimd, "sp": nc.sync}
        self.sem = {}
        self.cnt = {}
        for k in self.eng:
            self.sem[k] = es.enter_context(nc.semaphore("s_" + k))
            self.cnt[k] = 0
        self.dsem = [es.enter_context(nc.semaphore("d%d" % i)) for i in range(n_dma_sems)]
        self.dcnt = [0] * n_dma_sems
        self.dnext = 0
        self.seen = {k: {} for k in self.eng}
        self.nwait = 0
        self.nops = 0

    def _semof(self, key):
        if isinstance(key, tuple):
            return self.dsem[key[1]]
        return self.sem[key]

    def _need(self, e, deps):
        best = {}
        for d in deps:
            k, v = d
            if v > best.get(k, 0):
                best[k] = v
        for k, v in best.items():
            if self.seen[e].get(k, 0) >= v:
                continue
            self.eng[e].wait_ge(self._semof(k), v)
            self.nwait += 1
            self.seen[e][k] = v

    def op(self, e, reads, writes, fn):
        deps = []
        for b in reads:
            if b.w is not None and (b.w[0] != e or e not in NO_SELF_RAW):
                deps.append(b.w)
        for b in writes:
            if b.w is not None and b.w[0] != e:
                deps.append(b.w)
            for r in b.r:
                if r[0] != e:
                    deps.append(r)
        self._need(e, deps)
        ins = fn(self.eng[e])
        self.cnt[e] += 1
        v = self.cnt[e]
        ins.then_inc(self.sem[e], 1)
        self.nops += 1
        for b in writes:
            b.w = (e, v)
            b.r = []
        for b in reads:
            if b not in writes:
                b.r.append((e, v))
        return ins

    def dma(self, q, out, in_, reads, writes, **kw):
        slot = self.dnext
        self.dnext = (self.dnext + 1) % len(self.dsem)
        key = ("dma", slot)
        deps = []
        if self.dcnt[slot] > 0:
            deps.append((key, self.dcnt[slot]))
        for b in reads:
            if b.w is not None:
                deps.append(b.w)
        for b in writes:
            if b.w is not None:
                deps.append(b.w)
            deps.extend(b.r)
        self._need(q, deps)
        ins = self.eng[q].dma_start(out=out, in_=in_, **kw)
        self.dcnt[slot] += 16
        v = self.dcnt[slot]
        ins.then_inc(self.dsem[slot], 16)
        for b in writes:
            b.w = (key, v)
            b.r = []
        for b in reads:
            if b not in writes:
                b.r.append((key, v))
        return ins

    def barrier(self, dma=True):
        deps = [(k, self.cnt[k]) for k in self.eng if self.cnt[k] > 0]
        if dma:
            deps += [(("dma", i), c) for i, c in enumerate(self.dcnt) if c > 0]
        for e in self.eng:
            self._need(e, [d for d in deps if d[0] != e])


class Inst:
    pass


def build_program(do_sample=True, stop_after=99, dbg=False):
    nc = bass.Bass("TRN2", target_bir_lowering=False)
    di = {}

    def inp(name, shape, dt=F32):
        di[name] = nc.dram_tensor(name, list(shape), dt, kind="ExternalInput").ap()
        return di[name]

    def outp(name, shape, dt=F32):
        di[name] = nc.dram_tensor(name, list(shape), dt, kind="ExternalOutput").ap()
        return di[name]

    xw = inp("xw", [SEQ, D])
    xsw = inp("xsw", [128, D])
    cache_ckv = inp("cache_ckv", [PAST, DKV])
    cache_kpe = inp("cache_kpe", [PAST, ROPE])
    valid_p = inp("valid_p", [128, 32])
    valid_s = inp("valid_s", [128, 17])
    cos_p = inp("cos_p", [128, 32, 16])
    sin_p = inp("sin_p", [128, 32, 16])
    cos_s = inp("cos_s", [128, 1, 16])
    sin_s = inp("sin_s", [128, 1, 16])
    masks_d = inp("masks", [128, 4, 512])
    hval_d = inp("hval", [128, 1])
    s5init_s = inp("s5init_s", [128, NG])
    convh_s = inp("convh_s", [128, 2 * NFF, 2])
    mask8_d = inp("mask8", [128, 8])
    sgn_d = inp("sgn", [128, 2])
    w_in = inp("w_in", [D, 1184])
    w_q_b = inp("w_q_b", [DIN_Q, NH * QKD])
    w_kv_b = inp("w_kv_b", [DKV, 1024])
    w_glu_d = inp("w_s5_glu", [S5W, S5W])
    w_out_d = inp("w_out", [D, D])
    w_up_d = inp("w_up", [D, 2 * DFF])
    w_down_d = inp("w_down", [DFF, D])
    g_mix_d = inp("g_mix_b", [128, D])
    g_qa_d = inp("g_qa_b", [128, DIN_Q])
    g_kva_d = inp("g_kva_b", [128, DKV])
    g_ffn_d = inp("g_ffn_b", [128, D])
    g_qh_d = inp("g_qh_b", [128, QKD])
    g_kh_d = inp("g_kh_b", [128, QKD])
    are_d = inp("s5_are", [128, NG])
    aim_d = inp("s5_aim", [128, NG])
    ldt_d = inp("s5_ldt", [128, NG])
    bst_d = inp("s5_bst", [128, NG, 16])
    bsw_d = inp("s5_bsw", [128, NG, 16])
    cn_d = inp("s5_cn", [128, 4, 128])
    cns_d = inp("s5_cns", [128, 4, 128])
    d_d = inp("s5_dl", [128, 4])
    bglu_d = inp("s5_bglu", [128, 4])
    wdw_d = inp("w_dw_l", [128, 2 * NFF, 3])
    bdw_d = inp("b_dw_l", [128, 2 * NFF])

    y_p = outp("y_p", [HALF, D])
    nkv_p = outp("nkv_p", [HALF, DKV])
    nkr_p = outp("nkr_p", [HALF, ROPE])
    s5_p = outp("s5_p", [128, NG])
    conv_p = outp("conv_p", [128, 2 * NFF, 2])
    y_s = outp("y_s", [DEC, D])
    nkv_s = outp("nkv_s", [DEC, DKV])
    nkr_s = outp("nkr_s", [DEC, ROPE])
    s5_s = outp("s5_s", [128, NG])
    conv_s = outp("conv_s", [128, 2 * NFF, 2])
    if dbg:
        dbg_ob = outp("dbg_ob", [128, 4, 17 * 128], BF16)
        dbg_oa = outp("dbg_oa", [128, 4, 17 * 128], BF16)
        dbg_q = outp("dbg_q", [96, 8, 17 * 128], BF16)
    scr_up = nc.dram_tensor("scr_up", [D, 2 * DFF], BF16, kind="Internal").ap()
    sc5 = {
        "cos": nc.dram_tensor("sc_cos", [128, NG * 128], F32, kind="Internal").ap(),
        "sin": nc.dram_tensor("sc_sin", [128, NG * 128], F32, kind="Internal").ap(),
        "bpad": nc.dram_tensor("sc_bpad", [128, NG * 128], BF16, kind="Internal").ap(),
        "bpsw": nc.dram_tensor("sc_bpsw", [128, NG * 128], BF16, kind="Internal").ap(),
        "cw1": nc.dram_tensor("sc_cw1", [128, NG * 128], BF16, kind="Internal").ap(),
        "cw2": nc.dram_tensor("sc_cw2", [128, NG * 128], BF16, kind="Internal").ap(),
        "diag": nc.dram_tensor("sc_diag", [128, 512], BF16, kind="Internal").ap(),
        "rm": nc.dram_tensor("sc_rm", [128, NG], F32, kind="Internal").ap(),
    }
    Bsc5 = {k: Buf("sc5_" + k) for k in sc5}

    es = ExitStack()
    with es:
        cnt = [0]

        def sb(shape, dt, stack=None, name=None):
            cnt[0] += 1
            return (stack or es).enter_context(
                nc.sbuf_tensor((name or "t") + "_%d" % cnt[0], list(shape), dt))

        banks = [es.enter_context(nc.psum_tensor("bank%d" % i, [128, 512], F32)) for i in range(8)]
        Bbank = [Buf("bank%d" % i) for i in range(8)]
        bankb = [b[:].bitcast(BF16) for b in banks]

        ident = sb([128, 128], BF16); Bident = Buf("ident")
        identf = sb([128, 128], F32)
        swp = sb([128, 128], F32); Bswp = Buf("swp")
        mh = sb([128, 8], F32); Bmh = Buf("mh")
        Bg = Buf("gconst")
        obT_p = sb([128, 4, 17 * 128], BF16)
        oaT_p = sb([128, 4, 17 * 128], BF16)
        obT_s = sb([128, 4, 128], BF16)
        oaT_s = sb([128, 4, 128], BF16)
        P4 = {}
        hval = sb([128, 1], F32); Bhval = Buf("hval")
        sgn = sb([128, 2], F32); Bsgn = Buf("sgn")
        junk = sb([128, 1024], F32); Bjunk = Buf("junk")

        blk = es.enter_context(nc.Block())
        S = Sched(nc, es)
        A = lambda r, w, f: S.op("act", r, w, f)
        V = lambda r, w, f: S.op("dve", r, w, f)
        G = lambda r, w, f: S.op("pool", r, w, f)
        T = lambda r, w, f: S.op("pe", r, w, f)

        def bc_mid(ap, n):
            return ap.unsqueeze(1).to_broadcast([ap.shape[0], n, ap.shape[1]])

        def bc_last(ap, n):
            return ap.unsqueeze(2).to_broadcast([ap.shape[0], ap.shape[1], n])

        G([], [Bident], lambda e: e.memset(identf[:], 0.0))
        G([Bident], [Bident], lambda e: e.affine_select(
            out=identf[:], in_=identf[:], pattern=[[-1, 128]], compare_op=ALU.not_equal,
            fill=1.0, base=0, channel_multiplier=1))
        G([Bident], [Bident], lambda e: e.tensor_copy(out=ident[:], in_=identf[:]))
        G([Bident], [Bswp], lambda e: e.tensor_copy(out=swp[:, 0:64], in_=identf[:, 64:128]))
        G([Bident, Bswp], [Bswp], lambda e: e.tensor_copy(out=swp[:, 64:128], in_=identf[:, 0:64]))
        G([], [Bmh], lambda e: e.memset(mh[:], -0.5))
        S.dma("sp", hval[:], hval_d, [], [Bhval])
        S.dma("sp", sgn[:], sgn_d, [], [Bsgn])

        def rstd_from(ssq, Bssq, n, dim, out, Bout):
            V([Bssq], [Bssq], lambda e: e.tensor_scalar(out=ssq, in0=ssq, scalar1=1.0 / dim, scalar2=EPS,
                                                        op0=ALU.mult, op1=ALU.add))
            G([Bssq, Bmh], [Bout], lambda e: e.tensor_tensor(out=out, in0=ssq, in1=mh[:, 0:n], op=ALU.pow))

        def run_instance(I):
            NW, OWN0 = I.NW, I.OWN0
            NOWN = NW - OWN0
            oaT, obT, BoaT, BobT = I.oaT, I.obT, Buf("oaT"), Buf("obT")
            pk = ExitStack()
            w_kvb = sb([128, 2, 1024], BF16, pk); Bwkvb = Buf("wkvb")
            ckvT = sb([128, 2, NW * 128], BF16, pk); BckvT = [Buf("ckvT%d" % i) for i in range(NW)]
            krotT = sb([96, NW * 128], BF16, pk); BkrotT = [Buf("krotT%d" % i) for i in range(NW)]
            kscale = sb([128, NW, NH], F32, pk); Bkscale = [Buf("kscale%d" % i) for i in range(NW)]
            g_mix = sb([128, D], F32, pk)
            S.dma("sp", g_mix[:], g_mix_d, [], [Bg])
            S.dma("pool", w_kvb[:], w_kv_b.rearrange("(k p) n -> p k n", p=128), [], [Bwkvb])
            with ExitStack() as p1:
                g_kva = sb([128, DKV], F32, p1)
                S.dma("sp", g_kva[:], g_kva_d, [], [Bg])
                w_in_kvu = sb([128, 8, 800], BF16, p1); Bw1 = Buf("w_in_kvu")
                w_nope = sb([128, 2, 512], BF16, p1); Bwn = Buf("w_nope")
                w_glu = sb([128, 4, 512], BF16, p1); Bwg = Buf("w_glu")
                dl = sb([128, 4], F32, p1); bglu = sb([128, 4], F32, p1); mask8 = sb([128, 8], F32, p1)
                Bdl = Buf("dl")
                Rm = sb([128, NG], F32, p1)
                Bpad = sb([128, NG, 128], BF16, p1); Bpsw = sb([128, NG, 128], BF16, p1)
                Cw1 = sb([128, NG, 128], BF16, p1); Cw2 = sb([128, NG, 128], BF16, p1)
                diagD = sb([128, 4, 128], BF16, p1)
                Bw5 = Buf("s5w")
                COS = sb([128, NG, 128], F32, p1); SIN = sb([128, NG, 128], F32, p1)
                Btab = Buf("tab")
                CL = sb([128, NG], F32, p1); NSL = sb([128, NG], F32, p1); carry = sb([128, NG], F32, p1)
                CLN = sb([128, 8, 8], F32, p1)
                Bcarry = Buf("carry")
                LS = I.LS
                xf = [sb([128, D], F32, p1) for _ in range(2)]; Bxf = [Buf("xf0"), Buf("xf1")]
                cs_t = [sb([128, 2, 16], F32, p1) for _ in range(2)]; Bcs = [Buf("cs0"), Buf("cs1")]
                if I.KVC == 0:
                    S.dma("sp", xf[0][:], I.x[0:128, :], [], [Bxf[0]])
                    S.dma("sp", cs_t[0][:, 0, :], I.cos[:, 0, :], [], [Bcs[0]])
                    S.dma("sp", cs_t[0][:, 1, :], I.sin[:, 0, :], [], [Bcs[0]])
                if I.name == "p":
                    p1s = ExitStack()
                    are = sb([128, NG], F32, p1s); aim = sb([128, NG], F32, p1s); ldt = sb([128, NG], F32, p1s)
                    Bs5p = Buf("s5p")
                    bst = sb([128, NG, 16], F32, p1s); bsw = sb([128, NG, 16], F32, p1s); Bbst = Buf("bst")
                    cn = sb([128, 4, 128], F32, p1s); cns = sb([128, 4, 128], F32, p1s); Bcn = Buf("cn")
                    sm = [sb([128, NG], F32, p1s) for _ in range(14)]
                    Bsm = Buf("s5small")
                    bb = sb([128, NG, 16], F32, p1s); bbs = sb([128, NG, 16], F32, p1s); btmp = sb([128, NG, 16], F32, p1s)
                    Bbb = Buf("bb")
                    ta_ = sb([128, NG, 64], F32, p1s); tb_ = sb([128, NG, 64], F32, p1s)

                    S.dma("pool", w_in_kvu[:], w_in.rearrange("(k p) n -> p k n", p=128)[:, :, 384:1184], [], [Bw1])
                    for k in range(2):
                        S.dma("pool", w_nope[:, k, :].rearrange("p (h d) -> p h d", d=64),
                              w_kv_b[k * 128:(k + 1) * 128, :].rearrange("p (h d) -> p h d", d=128)[:, :, 0:64],
                              [], [Bwn])
                    S.dma("pool", w_glu[:], w_glu_d.rearrange("(k p) n -> p k n", p=128), [], [Bwg])
                    S.dma("sp", are[:], are_d, [], [Bs5p])
                    S.dma("sp", aim[:], aim_d, [], [Bs5p])
                    S.dma("sp", ldt[:], ldt_d, [], [Bs5p])
                    S.dma("sp", bst[:], bst_d, [], [Bbst])
                    S.dma("sp", bsw[:], bsw_d, [], [Bbst])
                    S.dma("sp", cn[:], cn_d, [], [Bcn])
                    S.dma("sp", cns[:], cns_d, [], [Bcn])
                    S.dma("sp", dl[:], d_d, [], [Bdl])
                    S.dma("sp", bglu[:], bglu_d, [], [Bdl])
                    S.dma("sp", mask8[:], mask8_d, [], [Bdl])
                    V([Bdl], [Bdl], lambda e: e.tensor_scalar(out=bglu[:], in0=bglu[:], scalar1=0.5, scalar2=None, op0=ALU.mult))

                    dt_, e1, mag, ang, shh, s16, cth, sth, cc, ss, cs, fre, fim, tmp = sm
                    A([Bs5p], [Bsm], lambda e: e.activation(out=dt_[:], in_=ldt[:], func=AF.Exp))
                    V([Bs5p, Bsm], [Bsm], lambda e: e.tensor_tensor(out=e1[:], in0=are[:], in1=dt_[:], op=ALU.mult))
                    A([Bsm], [Bsm], lambda e: e.activation(out=mag[:], in_=e1[:], func=AF.Exp))
                    V([Bs5p, Bsm], [Bsm], lambda e: e.tensor_tensor(out=ang[:], in0=aim[:], in1=dt_[:], op=ALU.mult))
                    A([Bsm], [Bsm], lambda e: e.activation(out=shh[:], in_=ang[:], func=AF.Sin, scale=1.0 / 32))
                    A([Bsm], [Bsm], lambda e: e.activation(out=sth[:], in_=ang[:], func=AF.Sin, scale=1.0 / 16))
                    V([Bsm], [Bsm], lambda e: e.tensor_tensor(out=cth[:], in0=shh[:], in1=shh[:], op=ALU.mult))
                    V([Bsm], [Bsm], lambda e: e.tensor_scalar(out=cth[:], in0=cth[:], scalar1=-2.0, scalar2=1.0,
                                                              op0=ALU.mult, op1=ALU.add))
                    for _ in range(4):
                        V([Bsm], [Bsm], lambda e: e.tensor_tensor(out=cc[:], in0=cth[:], in1=cth[:], op=ALU.mult))
                        V([Bsm], [Bsm], lambda e: e.tensor_tensor(out=ss[:], in0=sth[:], in1=sth[:], op=ALU.mult))
                        V([Bsm], [Bsm], lambda e: e.tensor_tensor(out=cs[:], in0=cth[:], in1=sth[:], op=ALU.mult))
                        V([Bsm], [Bsm], lambda e: e.tensor_tensor(out=cth[:], in0=cc[:], in1=ss[:], op=ALU.subtract))
                        V([Bsm], [Bsm], lambda e: e.tensor_scalar(out=sth[:], in0=cs[:], scalar1=2.0, scalar2=None, op0=ALU.mult))
                    V([Bsm], [Bsm], lambda e: e.tensor_copy(out=Rm[:], in_=mag[:]))
                    lbr, lbi = cc, ss
                    V([Bsm], [Bsm], lambda e: e.tensor_tensor(out=lbr[:], in0=mag[:], in1=cth[:], op=ALU.mult))
                    V([Bsm], [Bsm], lambda e: e.tensor_tensor(out=lbi[:], in0=mag[:], in1=sth[:], op=ALU.mult))
                    V([Bsm], [Bsm], lambda e: e.tensor_scalar(out=lbr[:], in0=lbr[:], scalar1=-1.0, scalar2=None, op0=ALU.add))
                    den = cs
                    V([Bs5p], [Bsm], lambda e: e.tensor_tensor(out=den[:], in0=are[:], in1=are[:], op=ALU.mult))
                    V([Bs5p, Bsm], [Bsm], lambda e: e.tensor_tensor(out=tmp[:], in0=aim[:], in1=aim[:], op=ALU.mult))
                    V([Bsm], [Bsm], lambda e: e.tensor_tensor(out=den[:], in0=den[:], in1=tmp[:], op=ALU.add))
                    V([Bsm], [Bsm], lambda e: e.reciprocal(out=den[:], in_=den[:]))
                    V([Bs5p, Bsm], [Bsm], lambda e: e.tensor_tensor(out=fre[:], in0=lbr[:], in1=are[:], op=ALU.mult))
                    V([Bs5p, Bsm], [Bsm], lambda e: e.tensor_tensor(out=tmp[:], in0=lbi[:], in1=aim[:], op=ALU.mult))
                    V([Bsm], [Bsm], lambda e: e.tensor_tensor(out=fre[:], in0=fre[:], in1=tmp[:], op=ALU.add))
                    V([Bsm], [Bsm], lambda e: e.tensor_tensor(out=fre[:], in0=fre[:], in1=den[:], op=ALU.mult))
                    V([Bs5p, Bsm], [Bsm], lambda e: e.tensor_tensor(out=fim[:], in0=lbi[:], in1=are[:], op=ALU.mult))
                    V([Bs5p, Bsm], [Bsm], lambda e: e.tensor_tensor(out=tmp[:], in0=lbr[:], in1=aim[:], op=ALU.mult))
                    V([Bsm], [Bsm], lambda e: e.tensor_tensor(out=fim[:], in0=fim[:], in1=tmp[:], op=ALU.subtract))
                    V([Bsm], [Bsm], lambda e: e.tensor_tensor(out=fim[:], in0=fim[:], in1=den[:], op=ALU.mult))
                    V([Bsm, Bsgn], [Bsm], lambda e: e.tensor_scalar(out=fim[:], in0=fim[:], scalar1=sgn[:, 0:1], scalar2=None, op0=ALU.mult))
                    V([Bsm, Bbst], [Bbb], lambda e: e.tensor_tensor(out=bb[:], in0=bst[:], in1=bc_last(fre[:], 16), op=ALU.mult))
                    V([Bsm, Bbst, Bbb], [Bbb], lambda e: e.tensor_tensor(out=btmp[:], in0=bsw[:], in1=bc_last(fim[:], 16), op=ALU.mult))
                    V([Bbb], [Bbb], lambda e: e.tensor_tensor(out=bb[:], in0=bb[:], in1=btmp[:], op=ALU.add))
                    V([Bsm, Bbst, Bbb], [Bbb], lambda e: e.tensor_tensor(out=bbs[:], in0=bsw[:], in1=bc_last(fre[:], 16), op=ALU.mult))
                    V([Bsm, Bbst, Bbb], [Bbb], lambda e: e.tensor_tensor(out=btmp[:], in0=bst[:], in1=bc_last(fim[:], 16), op=ALU.mult))
                    V([Bbb], [Bbb], lambda e: e.tensor_tensor(out=bbs[:], in0=bbs[:], in1=btmp[:], op=ALU.subtract))
                    G([], [Bw5], lambda e: e.memset(Cw1[:], 0.0))
                    G([Bw5], [Bw5], lambda e: e.memset(Cw2[:], 0.0))
                    for gb in range(4):
                        for (src, dst) in ((bb, Bpad), (bbs, Bpsw)):
                            bkk = gb if src is bb else 4 + gb
                            T([Bbb, Bident], [Bbank[bkk]], lambda e: e.transpose(
                                banks[bkk][:, 0:128], src[:, gb * 8:(gb + 1) * 8, :].rearrange("p a b -> p (a b)"), identf[:]))
                            for j in range(8):
                                V([Bbank[bkk], Bdl], [Bw5], lambda e: e.tensor_scalar(
                                    out=dst[:, gb * 8 + j, :], in0=banks[bkk][:, 0:128], scalar1=mask8[:, j:j + 1],
                                    scalar2=None, op0=ALU.mult))
                        for (src, dst, sc) in ((cn, Cw1, 1), (cns, Cw2, 0)):
                            bkc = gb if sc == 1 else 4 + gb
                            T([Bcn, Bident], [Bbank[bkc]], lambda e: e.transpose(banks[bkc][:, 128:256], src[:, gb, :], identf[:]))
                            for j in range(8):
                                V([Bbank[bkc], Bsgn], [Bw5], lambda e: e.tensor_scalar(
                                    out=dst[:, gb * 8 + j, 16 * j:16 * j + 16], in0=banks[bkc][:, 128 + 16 * j:128 + 16 * j + 16],
                                    scalar1=sgn[:, sc:sc + 1], scalar2=None, op0=ALU.mult))
                        V([Bdl, Bident], [Bw5], lambda e: e.tensor_scalar(
                            out=diagD[:, gb, :], in0=identf[:], scalar1=dl[:, gb:gb + 1], scalar2=None, op0=ALU.mult))
                    V([Bsm], [Btab], lambda e: e.tensor_copy(out=COS[:, :, 0], in_=cth[:]))
                    V([Bsm, Btab], [Btab], lambda e: e.tensor_copy(out=SIN[:, :, 0], in_=sth[:]))
                    m = 1
                    while m < 128:
                        cm = COS[:, :, m - 1:m].to_broadcast([128, NG, m])
                        smm = SIN[:, :, m - 1:m].to_broadcast([128, NG, m])
                        V([Btab], [Btab], lambda e: e.tensor_tensor(out=ta_[:, :, 0:m], in0=COS[:, :, 0:m], in1=cm, op=ALU.mult))
                        V([Btab], [Btab], lambda e: e.tensor_tensor(out=tb_[:, :, 0:m], in0=SIN[:, :, 0:m], in1=smm, op=ALU.mult))
                        V([Btab], [Btab], lambda e: e.tensor_tensor(out=COS[:, :, m:2 * m], in0=ta_[:, :, 0:m], in1=tb_[:, :, 0:m], op=ALU.subtract))
                        V([Btab], [Btab], lambda e: e.tensor_tensor(out=ta_[:, :, 0:m], in0=SIN[:, :, 0:m], in1=cm, op=ALU.mult))
                        V([Btab], [Btab], lambda e: e.tensor_tensor(out=tb_[:, :, 0:m], in0=COS[:, :, 0:m], in1=smm, op=ALU.mult))
                        V([Btab], [Btab], lambda e: e.tensor_tensor(out=SIN[:, :, m:2 * m], in0=ta_[:, :, 0:m], in1=tb_[:, :, 0:m], op=ALU.add))
                        m *= 2
                    V([Btab, Bsgn], [Btab], lambda e: e.tensor_scalar(
                        out=SIN[:].rearrange("p a b -> p (a b)"), in0=SIN[:].rearrange("p a b -> p (a b)"),
                        scalar1=sgn[:, 1:2], scalar2=None, op0=ALU.mult))

                    flat = lambda ap: ap.rearrange("p a b -> p (a b)")
                    S.dma("sp", sc5["cos"], flat(COS[:]), [Btab], [Bsc5["cos"]])
                    S.dma("sp", sc5["sin"], flat(SIN[:]), [Btab], [Bsc5["sin"]])
                    S.dma("sp", sc5["bpad"], flat(Bpad[:]), [Bw5], [Bsc5["bpad"]])
                    S.dma("sp", sc5["bpsw"], flat(Bpsw[:]), [Bw5], [Bsc5["bpsw"]])
                    S.dma("sp", sc5["cw1"], flat(Cw1[:]), [Bw5], [Bsc5["cw1"]])
                    S.dma("sp", sc5["cw2"], flat(Cw2[:]), [Bw5], [Bsc5["cw2"]])
                    S.dma("sp", sc5["diag"], flat(diagD[:]), [Bw5], [Bsc5["diag"]])
                    S.dma("sp", sc5["rm"], Rm[:], [Bsm], [Bsc5["rm"]])
                else:
                    p1s = ExitStack()
                    Bsm = Buf("s5small")
                    flat = lambda ap: ap.rearrange("p a b -> p (a b)")
                    S.dma("pool", w_in_kvu[:], w_in.rearrange("(k p) n -> p k n", p=128)[:, :, 384:1184], [], [Bw1])
                    for k in range(2):
                        S.dma("pool", w_nope[:, k, :].rearrange("p (h d) -> p h d", d=64),
                              w_kv_b[k * 128:(k + 1) * 128, :].rearrange("p (h d) -> p h d", d=128)[:, :, 0:64],
                              [], [Bwn])
                    S.dma("pool", w_glu[:], w_glu_d.rearrange("(k p) n -> p k n", p=128), [], [Bwg])
                    S.dma("sp", bglu[:], bglu_d, [], [Bdl])
                    V([Bdl], [Bdl], lambda e: e.tensor_scalar(out=bglu[:], in0=bglu[:], scalar1=0.5, scalar2=None, op0=ALU.mult))
                    tb_ = [Buf("ld%d" % k) for k in range(8)]
                    S.dma("sp", flat(COS[:]), sc5["cos"], [Bsc5["cos"]], [tb_[0]])
                    S.dma("sp", flat(SIN[:]), sc5["sin"], [Bsc5["sin"]], [tb_[1]])
                    S.dma("sp", flat(Bpad[:]), sc5["bpad"], [Bsc5["bpad"]], [tb_[2]])
                    S.dma("sp", flat(Bpsw[:]), sc5["bpsw"], [Bsc5["bpsw"]], [tb_[3]])
                    S.dma("sp", flat(Cw1[:]), sc5["cw1"], [Bsc5["cw1"]], [tb_[4]])
                    S.dma("sp", flat(Cw2[:]), sc5["cw2"], [Bsc5["cw2"]], [tb_[5]])
                    S.dma("sp", flat(diagD[:]), sc5["diag"], [Bsc5["diag"]], [tb_[6]])
                    S.dma("sp", Rm[:], sc5["rm"], [Bsc5["rm"]], [tb_[7]])
                    V(tb_, [Btab, Bw5, Bsm], lambda e: e.memset(NSL[:, 0:1], 0.0))
                V([Btab], [Btab], lambda e: e.tensor_copy(out=CL[:], in_=COS[:, :, LS - 1]))
                V([Btab], [Btab], lambda e: e.tensor_scalar(out=NSL[:], in0=SIN[:, :, LS - 1], scalar1=-1.0, scalar2=None, op0=ALU.mult))
                V([Btab], [Btab], lambda e: e.tensor_copy(out=CLN[:, :, 0:4], in_=CL[:].rearrange("p (a b) -> p a b", b=4)))
                V([Btab], [Btab], lambda e: e.tensor_copy(out=CLN[:, :, 4:8], in_=NSL[:].rearrange("p (a b) -> p a b", b=4)))
                if I.s5init is None:
                    V([], [Bcarry], lambda e: e.memset(carry[:], 0.0))
                else:
                    S.dma("sp", carry[:], I.s5init, [], [Bcarry])
                S.barrier(dma=False)
                p1s.close()

                xs_b = sb([128, D], BF16, p1); Bxs = Buf("xs")
                xT = [sb([128, 8, 128], BF16, p1) for _ in range(2)]; BxT = [Buf("xT0"), Buf("xT1")]
                uT = [sb([128, 4, 128], BF16, p1) for _ in range(2)]; BuT = [Buf("uT0"), Buf("uT1")]
                st = sb([128, 16], F32, p1); Bst = Buf("st")
                ckv_f = [sb([128, DKV], F32, p1) for _ in range(2)]; Bckv = [Buf("ckvf0"), Buf("ckvf1")]
                kpe_f = [sb([128, ROPE], F32, p1) for _ in range(2)]; Bkpe = [Buf("kpef0"), Buf("kpef1")]
                ckv_b = sb([128, DKV], BF16, p1); Bckvb = Buf("ckvb")
                krin = sb([128, 96], BF16, p1); Bkrin = Buf("krin")
                rt = sb([128, 4, 16], F32, p1); Brt = Buf("rt")
                ssqn = sb([128, 8], F32, p1); Bssqn = Buf("ssqn")
                t1 = [sb([128, 512], F32, p1) for _ in range(2)]; Bt1 = [Buf("t1a"), Buf("t1b")]
                t2 = [sb([128, 512], F32, p1) for _ in range(2)]; Bt2 = [Buf("t2a"), Buf("t2b")]
                Z = [sb([128, 4, 128], F32, p1) for _ in range(2)]; BZ = [Buf("Za"), Buf("Zb")]
                Z1 = [sb([128, 4, 128], BF16, p1) for _ in range(2)]
                Z2 = [sb([128, 4, 128], BF16, p1) for _ in range(2)]; BZ12 = [Buf("Z12a"), Buf("Z12b")]
                cst = [sb([128, 8], F32, p1) for _ in range(2)]; Bcst = [Buf("csta"), Buf("cstb")]
                Bcar = [Buf("carry%d" % q) for q in range(8)]
                for q in range(8):
                    Bcar[q].w = Bcarry.w
                gl = sb([128, 4, 128], BF16, p1); gh = sb([128, 4, 128], BF16, p1); th = sb([128, 4, 128], BF16, p1)
                Bgl = Buf("gl"); Bgh = Buf("gh"); Bth = Buf("th")
                G([], [Bkrin], lambda e: e.memset(krin[:], 0.0))
                for zi in range(2):
                    V([], [BZ[zi]], lambda e: e.memset(Z[zi][:], 0.0))

                def load_tile(t):
                    i = t % 2
                    if t < I.KVC:
                        S.dma("sp", ckv_f[i][:], I.cache_ckv[t * 128:(t + 1) * 128, :], [], [Bckv[i]])
                        S.dma("sp", kpe_f[i][:], I.cache_kpe[t * 128:(t + 1) * 128, :], [], [Bkpe[i]])
                    else:
                        tt = t - I.KVC
                        S.dma("sp", xf[i][:], I.x[tt * 128:(tt + 1) * 128, :], [], [Bxf[i]])
                        S.dma("sp", cs_t[i][:, 0, :], I.cos[:, tt, :], [], [Bcs[i]])
                        S.dma("sp", cs_t[i][:, 1, :], I.sin[:, tt, :], [], [Bcs[i]])

                def stage1_steps(t):
                    i = t % 2
                    steps = [[] for _ in range(8)]

                    def add(k, fn):
                        steps[k].append(fn)
                    if t + 1 < NW:
                        add(0, lambda: load_tile(t + 1))
                    if t >= I.KVC:
                        add(0, lambda: A([Bxf[i]], [Bjunk, Bst], lambda e: e.activation(
                            out=junk[:], in_=xf[i][:], func=AF.Square, accum_out=st[:, 0:1])))
                        add(1, lambda: rstd_from(st[:, 0:1], Bst, 1, D, st[:, 1:2], Bst))
                        add(2, lambda: V([Bxf[i], Bst, Bg], [Bxs], lambda e: e.scalar_tensor_tensor(
                            out=xs_b[:], in0=xf[i][:], scalar=st[:, 1:2], in1=g_mix[:], op0=ALU.mult, op1=ALU.mult)))

                        def tr(e):
                            for k in range(8):
                                r = e.transpose(bankb[0][:, k * 128:(k + 1) * 128], xs_b[:, k * 128:(k + 1) * 128], ident[:])
                            return r
                        add(3, lambda: T([Bxs, Bident], [Bbank[0]], tr))
                        add(3, lambda: A([Bbank[0]], [BxT[i]], lambda e: e.activation(
                            out=xT[i][:].rearrange("p a b -> p (a b)"), in_=bankb[0][:, 0:1024], func=AF.Copy)))

                        def mm_kv(e):
                            for k in range(8):
                                r = e.matmul(banks[1][:, 0:288], lhsT=xT[i][:, k, :], rhs=w_in_kvu[:, k, 0:288],
                                             start=(k == 0), stop=(k == 7))
                            return r

                        def mk_mm_u(mo):
                            def mm_u(e):
                                for k in range(8):
                                    r = e.matmul(banks[0][:, mo * 128:(mo + 1) * 128],
                                                 lhsT=w_in_kvu[:, k, 288 + mo * 128:288 + (mo + 1) * 128],
                                                 rhs=xT[i][:, k, :], start=(k == 0), stop=(k == 7))
                                return r
                            return mm_u
                        add(4, lambda: T([BxT[i], Bw1], [Bbank[1]], mm_kv))
                        for mo in range(4):
                            add(4 + mo, (lambda mo=mo: T([BxT[i], Bw1], [Bbank[0]], mk_mm_u(mo))))
                        add(7, lambda: A([Bbank[0]], [BuT[i]], lambda e: e.activation(
                            out=uT[i][:].rearrange("p a b -> p (a b)"), in_=banks[0][:, 0:512], func=AF.Copy)))
                        add(4, lambda: A([Bbank[1]], [Bjunk, Bst], lambda e: e.activation(
                            out=junk[:, 0:DKV], in_=banks[1][:, 0:DKV], func=AF.Square, accum_out=st[:, 2:3])))
                        add(5, lambda: rstd_from(st[:, 2:3], Bst, 1, DKV, st[:, 3:4], Bst))
                        x1 = banks[1][:, 256:272]; x2 = banks[1][:, 272:288]
                        cs_, sn_ = cs_t[i][:, 0, :], cs_t[i][:, 1, :]
                        add(5, lambda: V([Bbank[1], Bcs[i]], [Brt], lambda e: e.tensor_tensor(out=rt[:, 0, :], in0=x1, in1=cs_, op=ALU.mult)))
                        add(5, lambda: V([Bbank[1], Bcs[i]], [Brt], lambda e: e.tensor_tensor(out=rt[:, 1, :], in0=x2, in1=sn_, op=ALU.mult)))
                        add(5, lambda: V([Bbank[1], Bcs[i]], [Brt], lambda e: e.tensor_tensor(out=rt[:, 2, :], in0=x2, in1=cs_, op=ALU.mult)))
                        add(5, lambda: V([Bbank[1], Bcs[i]], [Brt], lambda e: e.tensor_tensor(out=rt[:, 3, :], in0=x1, in1=sn_, op=ALU.mult)))
                        add(5, lambda: V([Brt], [Bkpe[i]], lambda e: e.tensor_tensor(out=kpe_f[i][:, 0:16], in0=rt[:, 0, :], in1=rt[:, 1, :], op=ALU.subtract)))
                        add(5, lambda: V([Brt], [Bkpe[i]], lambda e: e.tensor_tensor(out=kpe_f[i][:, 16:32], in0=rt[:, 2, :], in1=rt[:, 3, :], op=ALU.add)))
                        add(6, lambda: V([Bbank[1], Bst, Bg], [Bckv[i]], lambda e: e.scalar_tensor_tensor(
                            out=ckv_f[i][:], in0=banks[1][:, 0:DKV], scalar=st[:, 3:4], in1=g_kva[:],
                            op0=ALU.mult, op1=ALU.mult)))
                        if I.out_rows(t) is not None:
                            r0, n = I.out_rows(t)
                            add(7, lambda: S.dma("sp", I.nkv[r0:r0 + n, :], ckv_f[i][0:n, :], [Bckv[i]], []))
                            add(7, lambda: S.dma("sp", I.nkr[r0:r0 + n, :], kpe_f[i][0:n, :], [Bkpe[i]], []))
                    add(7, lambda: A([Bckv[i]], [Bckvb], lambda e: e.activation(out=ckv_b[:], in_=ckv_f[i][:], func=AF.Copy)))
                    add(7, lambda: A([Bkpe[i]], [Bkrin], lambda e: e.activation(out=krin[:, 64:96], in_=kpe_f[i][:], func=AF.Copy)))
                    add(7, lambda: A([Bkpe[i]], [Bjunk, Bst], lambda e: e.activation(
                        out=junk[:, 992:1024], in_=kpe_f[i][:], func=AF.Square, accum_out=st[:, 4 + t % 8:5 + t % 8])))

                    def tr2(e):
                        e.transpose(bankb[3][:, 0:128], ckv_b[:, 0:128], ident[:])
                        e.transpose(bankb[3][:, 128:256], ckv_b[:, 128:256], ident[:])
                        return e.transpose(bankb[3][0:96, 256:384], krin[:], ident[:])

                    def mm_st(e):
                        for k in range(2):
                            r = e.matmul(banks[1][:, 0:512], lhsT=ckvT[:, k, t * 128:(t + 1) * 128], rhs=w_nope[:, k, :],
                                         start=(k == 0), stop=(k == 1))
                        return r
                    post = [
                        lambda: T([Bckvb, Bkrin, Bident], [Bbank[3]], tr2),
                        lambda: A([Bbank[3]], [BckvT[t]], lambda e: e.activation(
                            out=ckvT[:, :, t * 128:(t + 1) * 128], in_=bankb[3][:, 0:256].rearrange("p (a b) -> p a b", b=128), func=AF.Copy)),
                        lambda: A([Bbank[3]], [BkrotT[t]], lambda e: e.activation(
                            out=krotT[64:96, t * 128:(t + 1) * 128], in_=bankb[3][64:96, 256:384], func=AF.Copy)),
                        lambda: T([BckvT[t], Bwn], [Bbank[1]], mm_st),
                        lambda: A([Bbank[1]], [Bjunk], lambda e: e.activation(out=junk[:, 0:512], in_=banks[1][:, 0:512], func=AF.Square)),
                        lambda: V([Bjunk], [Bssqn], lambda e: e.tensor_reduce(
                            out=ssqn[:], in_=junk[:, 0:512].rearrange("p (a b) -> p a b", b=64), axis=AX.X, op=ALU.add)),
                        lambda: V([Bssqn, Bst], [Bssqn], lambda e: e.tensor_scalar(
                            out=ssqn[:], in0=ssqn[:], scalar1=st[:, 4 + t % 8:5 + t % 8], scalar2=1.0 / QKD, op0=ALU.add, op1=ALU.mult)),
                        lambda: V([Bssqn], [Bssqn], lambda e: e.tensor_scalar(out=ssqn[:], in0=ssqn[:], scalar1=EPS, scalar2=None, op0=ALU.add)),
                        lambda: G([Bssqn, Bmh], [Bkscale[t]], lambda e: e.tensor_tensor(out=kscale[:, t, :], in0=ssqn[:], in1=mh[:, 0:8], op=ALU.pow)),
                        lambda: G([Bkscale[t]], [Bkscale[t]], lambda e: e.tensor_scalar(
                            out=kscale[:, t, :], in0=kscale[:, t, :], scalar1=ATTN_SCALE, scalar2=0.0, op0=ALU.mult, op1=ALU.add)),
                    ]
                    return steps, post

                def stage1(t):
                    steps, post = stage1_steps(t)
                    for k in range(8):
                        for fn in steps[k]:
                            fn()
                    for fn in post:
                        fn()

                def s5A(t, q, par, part):
                    i = t % 2
                    gb = q // 2
                    if part == 1:
                        V([Bbank[4 + par], Btab], [Bt1[par]], lambda e: e.tensor_tensor(
                            out=t1[par][:], in0=banks[4 + par][:, :],
                            in1=COS[:, q * 4:q * 4 + 4, :].rearrange("p a b -> p (a b)"), op=ALU.mult))
                        V([Bbank[6 + par], Btab], [Bt2[par]], lambda e: e.tensor_tensor(
                            out=t2[par][:], in0=banks[6 + par][:, :],
                            in1=SIN[:, q * 4:q * 4 + 4, :].rearrange("p a b -> p (a b)"), op=ALU.mult))
                        V([Bt1[par], Bt2[par]], [Bt1[par]], lambda e: e.tensor_tensor(out=t1[par][:], in0=t1[par][:], in1=t2[par][:], op=ALU.add))
                        return

                    def mmAB(e):
                        for j in range(4):
                            e.matmul(banks[4 + par][:, j * 128:(j + 1) * 128],
                                     lhsT=Bpad[:, q * 4 + j, :], rhs=uT[i][:, gb, :], start=True, stop=True)
                        for j in range(4):
                            r = e.matmul(banks[6 + par][:, j * 128:(j + 1) * 128],
                                         lhsT=Bpsw[:, q * 4 + j, :], rhs=uT[i][:, gb, :], start=True, stop=True)
                        return r
                    T([BuT[i], Bw5], [Bbank[4 + par], Bbank[6 + par]], mmAB)

                def s5B(t, q, par, own, part):
                    i = t % 2
                    gb = q // 2
                    gs = slice(q * 4, q * 4 + 4)
                    if part == 1:
                        A([Bbank[3]], [Bcst[par]], lambda e: e.activation(out=cst[par][:, 4:8], in_=banks[3][:, 400 + 4 * par:404 + 4 * par], func=AF.Copy))
                        A([BZ[par]], [Bcst[par]], lambda e: e.activation(out=cst[par][:, 0:4], in_=Z[par][:, :, LS - 1], func=AF.Copy))
                        G([Bcst[par], Btab], [Bcst[par]], lambda e: e.tensor_tensor(out=cst[par][:], in0=cst[par][:], in1=CLN[:, q, :], op=ALU.mult))
                        G([Bcst[par]], [Bcar[q]], lambda e: e.tensor_tensor(out=carry[:, gs], in0=cst[par][:, 0:4], in1=cst[par][:, 4:8], op=ALU.add))
                        return
                    for j in range(4):
                        g = q * 4 + j
                        V([Bt1[par], Bcar[q]], [BZ[par]], lambda e: e.tensor_tensor_scan(
                            out=Z[par][:, j, 0:LS], data0=Rm[:, g:g + 1].to_broadcast([128, LS]),
                            data1=t1[par][:, j * 128:j * 128 + LS], initial=carry[:, g:g + 1],
                            op0=ALU.mult, op1=ALU.add))
                    T([BZ[par], Bswp], [Bbank[3]], lambda e: e.matmul(
                        banks[3][:, 400 + 4 * par:404 + 4 * par], lhsT=swp[:], rhs=Z[par][:, :, LS - 1], start=True, stop=True))

                def s5B2(t, q, par, own, part):
                    i = t % 2
                    gb = q // 2
                    gs = slice(q * 4, q * 4 + 4)
                    if own and part == 0:
                        G([BZ[par], Btab], [BZ12[par]], lambda e: e.tensor_tensor(
                            out=Z1[par][:].rearrange("p a b -> p (a b)"), in0=Z[par][:].rearrange("p a b -> p (a b)"),
                            in1=COS[:, gs, :].rearrange("p a b -> p (a b)"), op=ALU.mult))
                        G([BZ[par], Btab], [BZ12[par]], lambda e: e.tensor_tensor(
                            out=Z2[par][:].rearrange("p a b -> p (a b)"), in0=Z[par][:].rearrange("p a b -> p (a b)"),
                            in1=SIN[:, gs, :].rearrange("p a b -> p (a b)"), op=ALU.mult))

                    if own and part == 1:
                        def mmY(e):
                            o = banks[2][:, gb * 128:(gb + 1) * 128]
                            if q % 2 == 0:
                                e.matmul(o, lhsT=diagD[:, gb, :], rhs=uT[i][:, gb, :], start=True, stop=False)
                            for j in range(4):
                                e.matmul(o, lhsT=Cw1[:, q * 4 + j, :], rhs=Z1[par][:, j, :], start=False, stop=False)
                                r = e.matmul(o, lhsT=Cw2[:, q * 4 + j, :], rhs=Z2[par][:, j, :], start=False,
                                             stop=(q % 2 == 1 and j == 3))
                            return r
                        T([BZ12[par], Bw5, BuT[i]], [Bbank[2]], mmY)

                def s5end(t):
                    oc = (t - OWN0) * 128
                    A([Bbank[2]], [Bgl], lambda e: e.activation(
                        out=gl[:].rearrange("p a b -> p (a b)"), in_=banks[2][:, :], func=AF.Gelu_apprx_tanh))
                    G([Bgl], [Bgh], lambda e: e.tensor_scalar(
                        out=gh[:].rearrange("p a b -> p (a b)"), in0=gl[:].rearrange("p a b -> p (a b)"),
                        scalar1=0.5, scalar2=0.0, op0=ALU.mult, op1=ALU.add))

                    def mmG(e):
                        for mo in range(4):
                            for k in range(4):
                                r = e.matmul(banks[2][:, mo * 128:(mo + 1) * 128],
                                             lhsT=w_glu[:, k, mo * 128:(mo + 1) * 128], rhs=gl[:, k, :],
                                             start=(k == 0), stop=(k == 3))
                        return r
                    T([Bgl, Bwg], [Bbank[2]], mmG)
                    for mo in range(4):
                        A([Bbank[2], Bdl], [Bth], lambda e: e.activation(
                            out=th[:, mo, :], in_=banks[2][:, mo * 128:(mo + 1) * 128], func=AF.Tanh,
                            scale=0.5, bias=bglu[:, mo:mo + 1]))
                    V([Bth, Bgh], [BobT], lambda e: e.scalar_tensor_tensor(
                        out=obT[:, :, oc:oc + 128], in0=th[:], scalar=1.0, in1=gh[:], op0=ALU.add, op1=ALU.mult))

                units = [(t, q) for t in range(I.KVC, NW) for q in range(8)]
                NU = len(units)
                if I.KVC > 0:
                    load_tile(0)
                if I.KVC > 0:
                    cst_ = []
                    for t in range(I.KVC):
                        stp, post = stage1_steps(t)
                        cst_.append([stp[0], stp[7], post[0:1], post[1:3], post[3:4], post[4:5], post[5:8], post[8:10]])
                    for step in range(I.KVC + 8):
                        for k in reversed(range(8)):
                            ti = step - k
                            if 0 <= ti < I.KVC:
                                for fn in cst_[ti][k]:
                                    fn()
                stage1(I.KVC)
                pend_post = None
                cur_steps = None
                def unit(ix):
                    return units[ix] if 0 <= ix < NU else None
                for it in range(-3, NU + 2):
                    u3, u2, u1, u0, um = unit(it + 3), unit(it + 2), unit(it + 1), unit(it), unit(it - 1)
                    if u2 is not None:
                        ta_, qa_ = u2
                        if ta_ + 1 < NW:
                            if qa_ == 0:
                                cur_steps = stage1_steps(ta_ + 1)
                            for fn in cur_steps[0][qa_]:
                                fn()
                        if qa_ < 5 and pend_post is not None:
                            for fn in pend_post[qa_ * 2:qa_ * 2 + 2]:
                                fn()
                        if qa_ == 7 and ta_ + 1 < NW:
                            pend_post = cur_steps[1]
                    if u0 is not None:
                        s5B2(u0[0], u0[1], it % 2, u0[0] >= OWN0, 0)
                    if u3 is not None and (u3[1] != 0 or u3[0] == units[0][0] or True):
                        s5A(u3[0], u3[1], (it + 3) % 2, 0)
                    if u2 is not None:
                        s5A(u2[0], u2[1], (it + 2) % 2, 1)
                    if u1 is not None:
                        s5B(u1[0], u1[1], (it + 1) % 2, u1[0] >= OWN0, 0)
                    if u0 is not None:
                        s5B(u0[0], u0[1], it % 2, u0[0] >= OWN0, 1)
                    if um is not None:
                        s5B2(um[0], um[1], (it - 1) % 2, um[0] >= OWN0, 1)
                        if um[1] == 7 and um[0] >= OWN0:
                            s5end(um[0])
                for q in range(8):
                    if Bcar[q].w is not None:
                        Bcarry.r.append(Bcar[q].w)
                S.dma("sp", I.s5out, carry[:], Bcar, [])
                S.barrier()
            if stop_after <= 1:
                pk.close()
                return
            QT = sb([96, NH, NOWN * 128], BF16, pk); BQT = Buf("QT")
            with ExitStack() as p2:
                g_qa = sb([128, DIN_Q], F32, p2); gqk = sb([128, QKD], F32, p2); gkh = sb([128, QKD], F32, p2)
                S.dma("sp", g_qa[:], g_qa_d, [], [Bg])
                S.dma("sp", gqk[:], g_qh_d, [], [Bg])
                S.dma("sp", gkh[:], g_kh_d, [], [Bg])
                V([Bg], [Bg], lambda e: e.tensor_tensor(out=gqk[:], in0=gqk[:], in1=gkh[:], op=ALU.mult))
                w_in_q = sb([128, 8, DIN_Q], BF16, p2); Bwq = Buf("w_in_q")
                w_qb = sb([128, 3, NH * QKD], BF16, p2); Bwqb = Buf("w_qb")
                S.dma("pool", w_in_q[:], w_in.rearrange("(k p) n -> p k n", p=128)[:, :, 0:DIN_Q], [], [Bwq])
                S.dma("pool", w_qb[:], w_q_b.rearrange("(k p) n -> p k n", p=128), [], [Bwqb])
                C4 = 4
                CC = 12

                def mk(n, shape, dt, nm):
                    return [sb(shape, dt, p2) for _ in range(n)], [Buf("%s%d" % (nm, k)) for k in range(n)]
                xf, Bxf = mk(C4, [128, D], F32, "xf")
                cs_t, Bcs = mk(CC, [128, 2, 16], F32, "cs")
                xs_b, Bxs = mk(C4, [128, D], BF16, "xs")
                xTq, BxTq = mk(C4, [128, 8, 128], BF16, "xTq")
                stq, Bstq = mk(C4, [128, 8], F32, "st")
                cq, Bcq = mk(C4, [128, DIN_Q], BF16, "cq")
                cqT, BcqT = mk(C4, [128, 3, 128], BF16, "cqT")
                qf, Bqf = mk(C4, [128, NH, QKD], F32, "qf")
                qb, Bqb = mk(C4, [128, NH, QKD], BF16, "qb")
                rt, Brt = mk(C4, [128, 4, NH, 16], F32, "rt")
                sq8, Bsq8 = mk(C4, [128, 8], F32, "sq8")
                junk2 = sb([128, 768], F32, p2); Bjunk2 = Buf("junk2")
                hg = ((3, 0, 5), (4, 5, 3))

                def q_stages(t):
                    c = t % C4
                    cc = t % CC
                    oc = (t - OWN0) * 128
                    tt = t - I.KVC
                    st = stq[c]; Bst = Bstq[c]
                    cs_, sn_ = cs_t[cc][:, 0, :], cs_t[cc][:, 1, :]

                    def s0():
                        S.dma("sp", xf[c][:], I.x[tt * 128:(tt + 1) * 128, :], [], [Bxf[c]])
                        S.dma("sp", cs_t[cc][:, 0, :], I.cos[:, tt, :], [], [Bcs[cc]])
                        S.dma("sp", cs_t[cc][:, 1, :], I.sin[:, tt, :], [], [Bcs[cc]])

                    def s1():
                        A([Bxf[c]], [Bjunk, Bst], lambda e: e.activation(out=junk[:], in_=xf[c][:], func=AF.Square, accum_out=st[:, 0:1]))
                        rstd_from(st[:, 0:1], Bst, 1, D, st[:, 1:2], Bst)

                    def s2():
                        V([Bxf[c], Bst, Bg], [Bxs[c]], lambda e: e.scalar_tensor_tensor(
                            out=xs_b[c][:], in0=xf[c][:], scalar=st[:, 1:2], in1=g_mix[:], op0=ALU.mult, op1=ALU.mult))

                    def s3():
                        def tr(e):
                            for k in range(8):
                                r = e.transpose(bankb[0][:, k * 128:(k + 1) * 128], xs_b[c][:, k * 128:(k + 1) * 128], ident[:])
                            return r
                        T([Bxs[c], Bident], [Bbank[0]], tr)
                        A([Bbank[0]], [BxTq[c]], lambda e: e.activation(
                            out=xTq[c][:].rearrange("p a b -> p (a b)"), in_=bankb[0][:, 0:1024], func=AF.Copy))

                    def s4():
                        def mm_q(e):
                            for k in range(8):
                                r = e.matmul(banks[1][:, 0:DIN_Q], lhsT=xTq[c][:, k, :], rhs=w_in_q[:, k, :], start=(k == 0), stop=(k == 7))
                            return r
                        T([BxTq[c], Bwq], [Bbank[1]], mm_q)
                        A([Bbank[1]], [Bjunk, Bst], lambda e: e.activation(
                            out=junk[:, 0:DIN_Q], in_=banks[1][:, 0:DIN_Q], func=AF.Square, accum_out=st[:, 2:3]))
                        rstd_from(st[:, 2:3], Bst, 1, DIN_Q, st[:, 3:4], Bst)

                    def s5():
                        V([Bbank[1], Bst, Bg], [Bcq[c]], lambda e: e.scalar_tensor_tensor(
                            out=cq[c][:], in0=banks[1][:, 0:DIN_Q], scalar=st[:, 3:4], in1=g_qa[:], op0=ALU.mult, op1=ALU.mult))

                        def tr3(e):
                            for k in range(3):
                                r = e.transpose(bankb[2][:, k * 128:(k + 1) * 128], cq[c][:, k * 128:(k + 1) * 128], ident[:])
                            return r
                        T([Bcq[c], Bident], [Bbank[2]], tr3)
                        V([Bbank[2]], [BcqT[c]], lambda e: e.tensor_copy(
                            out=cqT[c][:].rearrange("p a b -> p (a b)"), in_=bankb[2][:, 0:384]))

                    def s6():
                        def mm_qb(e):
                            for (bk, h0, nh_) in hg:
                                for k in range(3):
                                    r = e.matmul(banks[bk][:, 0:nh_ * QKD], lhsT=cqT[c][:, k, :],
                                                 rhs=w_qb[:, k, h0 * QKD:(h0 + nh_) * QKD], start=(k == 0), stop=(k == 2))
                            return r
                        T([BcqT[c], Bwqb], [Bbank[3], Bbank[4]], mm_qb)
                        for (bk, h0, nh_) in hg:
                            pv = banks[bk][:, 0:nh_ * QKD].rearrange("p (h d) -> p h d", d=QKD)
                            hs = slice(h0, h0 + nh_)
                            A([Bbank[bk]], [Bqf[c]], lambda e: e.activation(out=qf[c][:, hs, :], in_=pv, func=AF.Copy))

                    def s7():
                        x1 = qf[c][:, :, 64:80]; x2 = qf[c][:, :, 80:96]
                        cb = bc_mid(cs_, NH); sbb = bc_mid(sn_, NH)
                        V([Bqf[c], Bcs[cc]], [Brt[c]], lambda e: e.tensor_tensor(out=rt[c][:, 0, :, :], in0=x1, in1=cb, op=ALU.mult))
                        V([Bqf[c], Bcs[cc]], [Brt[c]], lambda e: e.tensor_tensor(out=rt[c][:, 1, :, :], in0=x2, in1=sbb, op=ALU.mult))
                        V([Bqf[c], Bcs[cc]], [Brt[c]], lambda e: e.tensor_tensor(out=rt[c][:, 2, :, :], in0=x2, in1=cb, op=ALU.mult))
                        V([Bqf[c], Bcs[cc]], [Brt[c]], lambda e: e.tensor_tensor(out=rt[c][:, 3, :, :], in0=x1, in1=sbb, op=ALU.mult))
                        G([Brt[c]], [Bqf[c]], lambda e: e.tensor_tensor(out=qf[c][:, :, 64:80], in0=rt[c][:, 0, :, :], in1=rt[c][:, 1, :, :], op=ALU.subtract))
                        G([Brt[c]], [Bqf[c]], lambda e: e.tensor_tensor(out=qf[c][:, :, 80:96], in0=rt[c][:, 2, :, :], in1=rt[c][:, 3, :, :], op=ALU.add))

                    def s8():
                        A([Bqf[c]], [Bjunk2], lambda e: e.activation(
                            out=junk2[:, 0:768], in_=qf[c][:].rearrange("p a b -> p (a b)"), func=AF.Square))
                        V([Bjunk2], [Bsq8[c]], lambda e: e.tensor_reduce(
                            out=sq8[c][:], in_=junk2[:, 0:768].rearrange("p (a b) -> p a b", b=QKD), axis=AX.X, op=ALU.add))
                        rstd_from(sq8[c][:], Bsq8[c], 8, QKD, sq8[c][:], Bsq8[c])

                    def s9():
                        V([Bqf[c], Bsq8[c]], [Bqf[c]], lambda e: e.tensor_tensor(out=qf[c][:], in0=qf[c][:], in1=bc_last(sq8[c][:], QKD), op=ALU.mult))
                        V([Bqf[c], Bg], [Bqb[c]], lambda e: e.tensor_tensor(out=qb[c][:], in0=qf[c][:], in1=bc_mid(gqk[:], NH), op=ALU.mult))

                    def s10():
                        def tr8(e):
                            for h in range(NH):
                                r = e.transpose(bankb[5][0:96, h * 128:(h + 1) * 128], qb[c][:, h, :], ident[:])
                            return r
                        T([Bqb[c], Bident], [Bbank[5]], tr8)
                        V([Bbank[5]], [BQT], lambda e: e.tensor_copy(
                            out=QT[:, :, oc:oc + 128], in_=bankb[5][0:96, 0:1024].rearrange("p (a b) -> p a b", b=128)))
                    return [s0, s1, s2, s3, s4, s5, s6, s7, s8, s9, s10]

                tiles = list(range(OWN0, NW))
                stg = [q_stages(t) for t in tiles]
                NSTG = 11
                for step in range(len(tiles) + NSTG):
                    for k in reversed(range(NSTG)):
                        ti = step - k
                        if 0 <= ti < len(tiles):
                            stg[ti][k]()
                S.barrier()
            if dbg and I.name == "p":
                S.dma("sp", di["dbg_ob"], obT[:], [BobT], [])
                S.dma("sp", di["dbg_q"], QT[:], [BQT], [])
            if stop_after <= 2:
                pk.close()
                return
            with ExitStack() as p3:
                NK = NW * 128
                ktb = [sb([96, NK], BF16, p3) for _ in range(2)]; Bktb = [Buf("ktb0"), Buf("ktb1")]
                vxb = [sb([128, NW, 128], BF16, p3) for _ in range(2)]; Bvxb = [Buf("vxb0"), Buf("vxb1")]
                vld = sb([128, NW], F32, p3); Bvld = Buf("vld")
                PT = [sb([128, 512], BF16, p3) for _ in range(4)]; BPT = [Buf("PT%d" % i) for i in range(4)]
                msk = sb([128, 4, 512], BF16, p3); Bmsk = Buf("msk")
                rec = sb([128, 512], F32, p3); Brec = Buf("rec")
                S.dma("sp", vld[:], I.valid, [], [Bvld])
                if I.masked:
                    S.dma("pool", msk[:], masks_d, [], [Bmsk])
                allkr = BkrotT[:NW]
                A(allkr, [Bktb[0]], lambda e: e.activation(out=ktb[0][64:96, :], in_=krotT[64:96, 0:NK], func=AF.Copy))
                V(allkr, [Bktb[1]], lambda e: e.tensor_copy(out=ktb[1][64:96, :], in_=krotT[64:96, 0:NK]))
                for b_ in range(2):
                    ooff = 64 if b_ == 0 else 0
                    V([Bvld], [Bvxb[b_]], lambda e: e.tensor_copy(
                        out=vxb[b_][:, :, ooff:ooff + 64], in_=bc_last(vld[:], 64)))
                Bscr = [Buf("scr%d" % k) for k in range(8)]
                P3S = int(os.environ.get("P3S", "99"))
                if I.name == "p" and P3S >= 1:
                    for k in range(8):
                        S.dma("pool", scr_up[k * 128:(k + 1) * 128, :], w_up_d[k * 128:(k + 1) * 128, :], [], [Bscr[k]])
                    I.Bscr = Bscr
                qblocks = []
                t = OWN0
                while t < NW:
                    t1_ = min(NW, (t // 4 + 1) * 4)
                    qblocks.append((t, t1_))
                    t = t1_
                PSB = [0, 1, 4, 5]

                def kv_tasks(h):
                    b_ = h % 2
                    voff = 0 if b_ == 0 else 64
                    tasks = []
                    c0 = 0
                    while c0 < NK:
                        n = min(512, NK - c0)

                        def tk(c0=c0, n=n):
                            def mmK(e):
                                for k in range(2):
                                    r = e.matmul(banks[6][0:64, 0:n], lhsT=w_kvb[:, k, h * 128:h * 128 + 64],
                                                 rhs=ckvT[:, k, c0:c0 + n], start=(k == 0), stop=(k == 1))
                                return r
                            T(BckvT[c0 // 128:(c0 + n) // 128] + [Bwkvb], [Bbank[6]], mmK)
                            V([Bbank[6]], [Bktb[b_]], lambda e: e.tensor_copy(out=ktb[b_][0:64, c0:c0 + n], in_=banks[6][0:64, 0:n]))
                        tasks.append(tk)
                        c0 += n
                    t0 = 0
                    while t0 < NW:
                        nt = min(4, NW - t0)

                        def tv(t0=t0, nt=nt):
                            def mmV(e):
                                for j in range(nt):
                                    for k in range(2):
                                        r = e.matmul(banks[7][:, j * 64:(j + 1) * 64],
                                                     lhsT=ckvT[:, k, (t0 + j) * 128:(t0 + j + 1) * 128],
                                                     rhs=w_kvb[:, k, h * 128 + 64:h * 128 + 128], start=(k == 0), stop=(k == 1))
                                return r
                            T(BckvT[t0:t0 + nt] + [Bwkvb], [Bbank[7]], mmV)
                            V([Bbank[7]], [Bvxb[b_]], lambda e: e.tensor_copy(
                                out=vxb[b_][:, t0:t0 + nt, voff:voff + 64],
                                in_=banks[7][:, 0:nt * 64].rearrange("p (a b) -> p a b", b=64)))
                        tasks.append(tv)
                        t0 += nt
                    return tasks

                def produce_kv(h):
                    for tk in kv_tasks(h):
                        tk()

                steps = []
                for h in range(NH):
                    for qi, (ta, tb) in enumerate(qblocks):
                        for kt in range(tb):
                            steps.append((h, qi, ta, tb, kt))
                NS = len(steps)
                LAG = 3
                gidx = {}
                for (h, qi, ta, tb, kt) in steps:
                    gidx.setdefault((h, qi), len(gidx))
                hstart = {}
                for jj, st_ in enumerate(steps):
                    hstart.setdefault(st_[0], jj)
                pending = []
                if P3S >= 2:
                    produce_kv(0)
                for j in range(NS + LAG):
                    if P3S < 3:
                        break
                    if j < NS:
                        h, qi, ta, tb, kt = steps[j]
                        b_ = h % 2
                        nq = (tb - ta) * 128
                        qc = (ta - OWN0) * 128
                        sbk = PSB[j % 4]
                        pi = j % 4
                        if j == hstart[h] + LAG and h + 1 < NH:
                            pending = kv_tasks(h + 1)
                        if pending:
                            pending.pop(0)()
                        off = (kt - ta) * 128 if (I.masked and kt > ta) else 0
                        T([Bktb[b_], BQT], [Bbank[sbk]], lambda e: e.matmul(
                            banks[sbk][:, off:nq], lhsT=ktb[b_][:, kt * 128:(kt + 1) * 128],
                            rhs=QT[:, h, qc + off:qc + nq], start=True, stop=True))
                        A([Bbank[sbk], Bkscale[kt]], [BPT[pi]], lambda e: e.activation(
                            out=PT[pi][:, off:nq], in_=banks[sbk][:, off:nq], func=AF.Exp, scale=kscale[:, kt, h:h + 1]))
                        if I.masked and kt >= ta:
                            V([BPT[pi], Bmsk], [BPT[pi]], lambda e: e.tensor_tensor(
                                out=PT[pi][:, off:nq], in0=PT[pi][:, off:nq], in1=msk[:, kt - ta, off:nq], op=ALU.mult))
                    jp = j - LAG
                    if jp >= 0:
                        h, qi, ta, tb, kt = steps[jp]
                        b_ = h % 2
                        nq = (tb - ta) * 128
                        qc = (ta - OWN0) * 128
                        ob = 2 + gidx[(h, qi)] % 2
                        pp_ = jp % 4
                        off = (kt - ta) * 128 if (I.masked and kt > ta) else 0
                        T([Bvxb[b_], BPT[pp_]], [Bbank[ob]], lambda e: e.matmul(
                            banks[ob][:, off:nq], lhsT=vxb[b_][:, kt, :], rhs=PT[pp_][:, off:nq],
                            start=(kt == 0), stop=(kt == tb - 1)))
                        if kt == tb - 1:
                            dlo = 64 if b_ == 0 else 0
                            olo = 0 if b_ == 0 else 64
                            V([Bbank[ob]], [Brec], lambda e: e.tensor_scalar(
                                out=rec[dlo:dlo + 64, 0:nq], in0=banks[ob][dlo:dlo + 64, 0:nq], scalar1=1e-30, scalar2=None, op0=ALU.max))
                            V([Brec], [Brec], lambda e: e.reciprocal(out=rec[dlo:dlo + 64, 0:nq], in_=rec[dlo:dlo + 64, 0:nq]))
                            V([Bbank[ob], Brec], [BoaT], lambda e: e.tensor_tensor(
                                out=oaT[olo:olo + 64, h // 2, qc:qc + nq], in0=banks[ob][olo:olo + 64, 0:nq],
                                in1=rec[dlo:dlo + 64, 0:nq], op=ALU.mult))
                S.barrier()
            if dbg and I.name == "p":
                S.dma("sp", di["dbg_oa"], oaT[:], [BoaT], [])
            pk.close()
            if stop_after <= 3:
                return
            yield
            if "v" not in P4:
                p4 = ExitStack(); P4["stack"] = p4
                g_ffn = sb([128, D], F32, p4)
                S.dma("sp", g_ffn[:], g_ffn_d, [], [Bg])
                w_out = sb([128, 8, D], BF16, p4); Bwo = Buf("w_out")
                w_dn = sb([128, NFF, D], BF16, p4); Bwd = [Buf("w_dn%d" % i) for i in range(NFF)]
                S.dma("pool", w_out[:], w_out_d.rearrange("(k p) n -> p k n", p=128), [], [Bwo])
                P4["wdn_pending"] = True
                wdw = sb([128, 2 * NFF, 3], F32, p4); bdw = sb([128, 2 * NFF], F32, p4); Bdw = Buf("dw")
                S.dma("sp", wdw[:], wdw_d, [], [Bdw])
                S.dma("sp", bdw[:], bdw_d, [], [Bdw])
                tail = sb([128, 2 * NFF, 2], F32, p4); Btail = Buf("tail")
                corr = sb([128, 2 * NFF, 2], F32, p4); Bcorr = Buf("corr")
                wu = [sb([128, 8, 512], BF16, p4) for _ in range(2)]; Bwu = [Buf("wu%d" % i) for i in range(2)]
                xr = [sb([128, D], F32, p4) for _ in range(2)]; Bxr = [Buf("xr0"), Buf("xr1")]
                hf_ = sb([128, 4, D], F32, p4); Bhf = [Buf("hf%d" % i) for i in range(4)]
                hn = [sb([128, D], BF16, p4) for _ in range(2)]; Bhn = [Buf("hn0"), Buf("hn1")]
                hnT = sb([128, 8, 512], BF16, p4); BhnT = Buf("hnT")
                aT = sb([128, NFF, 512], BF16, p4); BaT = [Buf("aT%d" % i) for i in range(NFF)]
                st = sb([128, 8], F32, p4); Bst = Buf("st")
                cg = [sb([128, 512], F32, p4) for _ in range(3)]; Bcg = [Buf("cg%d" % k) for k in range(3)]
                cv = [sb([128, 512], F32, p4) for _ in range(3)]; Bcv = [Buf("cv%d" % k) for k in range(3)]
                sg = sb([128, 512], F32, p4); Bsg = Buf("sg")
                yo = [sb([128, D], F32, p4) for _ in range(2)]; Byo = [Buf("yo0"), Buf("yo1")]
                P4["v"] = (g_ffn, w_out, Bwo, w_dn, Bwd, wdw, bdw, Bdw, tail, Btail, wu, Bwu, xr, Bxr, hf_, Bhf,
                           hn, Bhn, hnT, BhnT, aT, BaT, st, Bst, cg, Bcg, cv, Bcv, sg, Bsg, yo, Byo, [0], corr, Bcorr)
            (g_ffn, w_out, Bwo, w_dn, Bwd, wdw, bdw, Bdw, tail, Btail, wu, Bwu, xr, Bxr, hf_, Bhf,
             hn, Bhn, hnT, BhnT, aT, BaT, st, Bst, cg, Bcg, cv, Bcv, sg, Bsg, yo, Byo, wui, corr, Bcorr) = P4["v"]
            if True:
                if I.convh is None:
                    V([], [Btail], lambda e: e.memset(tail[:], 0.0))
                else:
                    S.dma("sp", tail[:], I.convh, [], [Btail])
                qblocks = []
                t = OWN0
                while t < NW:
                    t1_ = min(NW, (t // 4 + 1) * 4)
                    qblocks.append((t, t1_))
                    t = t1_
                first_block = True
                for (ta, tb) in qblocks:
                    ntl = tb - ta
                    nq = ntl * 128
                    qc = (ta - OWN0) * 128
                    def wout_head(j):
                        t = ta + j
                        i = t % 2
                        tt = t - I.KVC
                        hb = 2 * (j % 2)
                        S.dma("sp", xr[i][:], I.x[tt * 128:(tt + 1) * 128, :], [], [Bxr[i]])

                        def mmH(e):
                            for nh_ in range(2):
                                for k in range(8):
                                    src = oaT if k < 4 else obT
                                    r = e.matmul(banks[hb + nh_][:, :], lhsT=src[:, k % 4, qc + j * 128:qc + (j + 1) * 128],
                                                 rhs=w_out[:, k, nh_ * 512:(nh_ + 1) * 512], start=(k == 0), stop=(k == 7))
                            return r
                        T([BoaT, BobT, Bwo], [Bbank[hb], Bbank[hb + 1]], mmH)
                        for nh_ in range(2):
                            V([Bbank[hb + nh_], Bxr[i]], [Bhf[j]], lambda e: e.tensor_tensor(
                                out=hf_[:, j, nh_ * 512:(nh_ + 1) * 512], in0=banks[hb + nh_][:, :],
                                in1=xr[i][:, nh_ * 512:(nh_ + 1) * 512], op=ALU.add))
                        A([Bhf[j]], [Bjunk, Bst], lambda e: e.activation(out=junk[:], in_=hf_[:, j, :], func=AF.Square, accum_out=st[:, 2 * (j % 2):2 * (j % 2) + 1]))
                        rstd_from(st[:, 2 * (j % 2):2 * (j % 2) + 1], Bst, 1, D, st[:, 2 * (j % 2) + 1:2 * (j % 2) + 2], Bst)
                        V([Bhf[j], Bst, Bg], [Bhn[j % 2]], lambda e: e.scalar_tensor_tensor(
                            out=hn[j % 2][:], in0=hf_[:, j, :], scalar=st[:, 2 * (j % 2) + 1:2 * (j % 2) + 2], in1=g_ffn[:], op0=ALU.mult, op1=ALU.mult))

                    def wout_tail(j):
                        def trh(e):
                            for k in range(8):
                                r = e.transpose(bankb[4][:, k * 128:(k + 1) * 128], hn[j % 2][:, k * 128:(k + 1) * 128], ident[:])
                            return r
                        T([Bhn[j % 2], Bident], [Bbank[4]], trh)
                        A([Bbank[4]], [BhnT], lambda e: e.activation(
                            out=hnT[:, :, j * 128:(j + 1) * 128], in_=bankb[4][:, 0:1024].rearrange("p (a b) -> p a b", b=128),
                            func=AF.Copy))
                    for j in range(ntl + 1):
                        if j < ntl:
                            wout_head(j)
                        if j >= 1:
                            wout_tail(j - 1)
                    V([Btail, Bdw], [Bcorr], lambda e: e.tensor_tensor(out=corr[:, :, 0], in0=tail[:, :, 0], in1=wdw[:, :, 0], op=ALU.mult))
                    V([Btail, Bdw], [Bcorr], lambda e: e.tensor_tensor(out=corr[:, :, 1], in0=tail[:, :, 1], in1=wdw[:, :, 1], op=ALU.mult))
                    V([Bcorr], [Bcorr], lambda e: e.tensor_tensor(out=corr[:, :, 0], in0=corr[:, :, 0], in1=corr[:, :, 1], op=ALU.add))
                    V([Btail, Bdw, Bcorr], [Bcorr], lambda e: e.tensor_tensor(out=corr[:, :, 1], in0=tail[:, :, 1], in1=wdw[:, :, 0], op=ALU.mult))
                    for f in range(NFF):
                        if P4.get("wdn_pending"):
                            S.dma("pool", w_dn[:, f, :], w_down_d[f * 128:(f + 1) * 128, :], [], [Bwd[f]])
                            if f == NFF - 1:
                                P4["wdn_pending"] = False
                        if f % 2 == 0:
                            wui[0] += 1
                            wi = wui[0] % 2
                            scv = scr_up.rearrange("(k p) c -> p k c", p=128)
                            S.dma("sp", wu[wi][:, :, 0:256], scv[:, :, f * 128:f * 128 + 256], I.Bscr, [Bwu[wi]])
                            S.dma("sp", wu[wi][:, :, 256:512], scv[:, :, DFF + f * 128:DFF + f * 128 + 256], I.Bscr, [Bwu[wi]])
                        wi = wui[0] % 2
                        bg = 2 + 2 * (f % 3)
                        bv = bg + 1
                        jo = (f % 2) * 128

                        def mmU(e):
                            for (bk, c0) in ((bg, jo), (bv, 256 + jo)):
                                for k in range(8):
                                    r = e.matmul(banks[bk][:, 0:nq], lhsT=wu[wi][:, k, c0:c0 + 128], rhs=hnT[:, k, 0:nq],
                                                 start=(k == 0), stop=(k == 7))
                            return r
                        T([Bwu[wi], BhnT], [Bbank[bg], Bbank[bv]], mmU)
                        ci = f % 3
                        for (bk, ch, dst, Bdst) in ((bg, f, cg[ci], Bcg[ci]), (bv, NFF + f, cv[ci], Bcv[ci])):
                            ps_ = banks[bk]
                            A([Bbank[bk], Bdw], [Bdst], lambda e: e.activation(
                                out=dst[:, 0:nq], in_=ps_[:, 0:nq], func=AF.Identity, scale=wdw[:, ch, 2:3], bias=bdw[:, ch:ch + 1]))
                            V([Bbank[bk], Bdw, Bdst], [Bdst], lambda e: e.scalar_tensor_tensor(
                                out=dst[:, 1:nq], in0=ps_[:, 0:nq - 1], scalar=wdw[:, ch, 1:2], in1=dst[:, 1:nq], op0=ALU.mult, op1=ALU.add))
                            V([Bbank[bk], Bdw, Bdst], [Bdst], lambda e: e.scalar_tensor_tensor(
                                out=dst[:, 2:nq], in0=ps_[:, 0:nq - 2], scalar=wdw[:, ch, 0:1], in1=dst[:, 2:nq], op0=ALU.mult, op1=ALU.add))
                            G([Bcorr, Bdst], [Bdst], lambda e: e.tensor_tensor(
                                out=dst[:, 0:2], in0=dst[:, 0:2], in1=corr[:, ch, :], op=ALU.add))
                            e0 = I.tail_end(ta, tb)
                            A([Bbank[bk], Btail], [Btail], lambda e: e.activation(out=tail[:, ch, :], in_=ps_[:, e0 - 2:e0], func=AF.Copy))
                        A([Bcg[ci]], [Bsg], lambda e: e.activation(out=sg[:, 0:nq], in_=cg[ci][:, 0:nq], func=AF.Silu))
                        G([Bsg, Bcv[ci]], [BaT[f]], lambda e: e.tensor_tensor(
                            out=aT[:, f, 0:nq], in0=sg[:, 0:nq], in1=cv[ci][:, 0:nq], op=ALU.mult))
                    if first_block and I.name == "p":
                        V([Btail, Bhval], [Btail], lambda e: e.tensor_scalar(
                            out=tail[:].rearrange("p a b -> p (a b)"), in0=tail[:].rearrange("p a b -> p (a b)"),
                            scalar1=hval[:, 0:1], scalar2=None, op0=ALU.mult))
                    first_block = False
                    for j in range(ntl):
                        t = ta + j
                        yi = j % 2

                        db = 2 * (j % 2)

                        for f0 in range(0, NFF, 6):
                            f1 = min(NFF, f0 + 6)

                            def mmD(e):
                                for f in range(f0, f1):
                                    for nh_ in range(2):
                                        r = e.matmul(banks[db + nh_][:, :], lhsT=aT[:, f, j * 128:(j + 1) * 128],
                                                     rhs=w_dn[:, f, nh_ * 512:(nh_ + 1) * 512], start=(f == 0), stop=(f == NFF - 1))
                                return r
                            T(BaT[f0:f1] + Bwd[f0:f1], [Bbank[db], Bbank[db + 1]], mmD)
                        for nh_ in range(2):
                            V([Bbank[db + nh_], Bhf[j]], [Byo[yi]], lambda e: e.tensor_tensor(
                                out=yo[yi][:, nh_ * 512:(nh_ + 1) * 512], in0=banks[db + nh_][:, :],
                                in1=hf_[:, j, nh_ * 512:(nh_ + 1) * 512], op=ALU.add))
                        if I.out_rows(t) is not None:
                            r0, n = I.out_rows(t)
                            S.dma("sp", I.y[r0:r0 + n, :], yo[yi][0:n, :], [Byo[yi]], [])
                S.dma("sp", I.convout, tail[:], [Btail], [])

        Ip = Inst()
        Ip.name = "p"; Ip.NW = 32; Ip.OWN0 = 15; Ip.KVC = 0; Ip.LS = 128
        Ip.x = xw; Ip.cos = cos_p; Ip.sin = sin_p; Ip.valid = valid_p; Ip.masked = True
        Ip.cache_ckv = None; Ip.cache_kpe = None; Ip.s5init = None; Ip.convh = None
        Ip.nkv = nkv_p; Ip.nkr = nkr_p; Ip.y = y_p; Ip.s5out = s5_p; Ip.convout = conv_p
        Ip.out_rows = lambda t: ((t - 16) * 128, 128) if t >= 16 else None
        Ip.tail_end = lambda ta, tb: (tb - ta) * 128
        Ip.oaT, Ip.obT = oaT_p, obT_p
        gens = [run_instance(Ip)]
        next(gens[0], None)
        if do_sample and stop_after > 4:
            Is = Inst()
            Is.name = "s"; Is.NW = 17; Is.OWN0 = 16; Is.KVC = 16; Is.LS = 16
            Is.x = xsw; Is.cos = cos_s; Is.sin = sin_s; Is.valid = valid_s; Is.masked = False
            Is.cache_ckv = cache_ckv; Is.cache_kpe = cache_kpe; Is.s5init = s5init_s; Is.convh = convh_s
            Is.nkv = nkv_s; Is.nkr = nkr_s; Is.y = y_s; Is.s5out = s5_s; Is.convout = conv_s
            Is.out_rows = lambda t: (0, DEC) if t == 16 else None
            Is.tail_end = lambda ta, tb: DEC
            Is.Bscr = Ip.Bscr
            Is.oaT, Is.obT = oaT_s, obT_s
            gens.append(run_instance(Is))
            next(gens[1], None)
        for g_ in gens:
            next(g_, None)
        S.barrier()
        if "stack" in P4:
            P4["stack"].close()
        S.barrier()
        print("ops", S.nops, "waits", S.nwait)
    return nc


def _rope_tables(pos):
    inv = 10000.0 ** (-np.arange(0, ROPE, 2, dtype=np.float32) / ROPE)
    ang = pos.astype(np.float32)[:, None] * inv[None, :]
    return np.cos(ang).astype(np.float32), np.sin(ang).astype(np.float32)


def make_in_maps(inp):
    f32 = lambda a: np.ascontiguousarray(a, dtype=np.float32)
    bcast = lambda v, n: f32(np.broadcast_to(np.asarray(v).reshape(1, -1), (128, n)))
    common = {}
    for k in ("w_in", "w_q_b", "w_kv_b", "w_s5_glu", "w_out", "w_up", "w_down"):
        common[k] = f32(inp[k][0])
    common["g_mix_b"] = bcast(inp["g_mix_norm"][0], D)
    common["g_qa_b"] = bcast(inp["g_q_a"][0], DIN_Q)
    common["g_kva_b"] = bcast(inp["g_kv_a"][0], DKV)
    common["g_ffn_b"] = bcast(inp["g_ffn_norm"][0], D)
    common["g_qh_b"] = bcast(inp["g_q_head"][0], QKD)
    common["g_kh_b"] = bcast(inp["g_k_head"][0], QKD)
    dup = lambda a: f32(np.concatenate([a.T, a.T], axis=0))
    common["s5_are"] = dup(inp["s5_a_re"][0])
    common["s5_aim"] = dup(inp["s5_a_im"][0])
    common["s5_ldt"] = bcast(inp["s5_log_dt"][0], NG)
    bre = np.transpose(inp["s5_b_re"][0], (1, 0, 2))
    bim = np.transpose(inp["s5_b_im"][0], (1, 0, 2))
    common["s5_bst"] = f32(np.concatenate([bre, bim], axis=0))
    common["s5_bsw"] = f32(np.concatenate([bim, bre], axis=0))
    cre = inp["s5_c_re"][0].reshape(4, 128, 64)
    cim = inp["s5_c_im"][0].reshape(4, 128, 64)
    common["s5_cn"] = f32(np.transpose(np.concatenate([cre, cim], axis=2), (1, 0, 2)))
    common["s5_cns"] = f32(np.transpose(np.concatenate([cim, cre], axis=2), (1, 0, 2)))
    common["s5_dl"] = f32(inp["s5_d"][0].reshape(4, 128).T)
    common["s5_bglu"] = f32(inp["b_s5_glu"][0].reshape(4, 128).T)
    common["w_dw_l"] = f32(np.transpose(inp["w_dw"][0].reshape(3, 2 * NFF, 128), (2, 1, 0)))
    common["b_dw_l"] = f32(inp["b_dw"][0].reshape(2 * NFF, 128).T)
    pidx = np.arange(128)
    common["mask8"] = f32((pidx[:, None] // 16) == np.arange(8)[None, :])
    sg = np.where(pidx < 64, -1.0, 1.0)
    common["sgn"] = f32(np.stack([sg, -sg], axis=1))
    kk = (np.arange(4)[None, :, None] * 128 + pidx[:, None, None]) // 64
    qq = np.arange(512)[None, None, :] // 64
    common["masks"] = f32(kk <= qq)
    cos_s, sin_s = _rope_tables(PAST + np.arange(128))
    common["cos_s"] = f32(cos_s[:, None, :])
    common["sin_s"] = f32(sin_s[:, None, :])
    vs = np.ones((128, 17), np.float32)
    vs[:, 16] = (pidx < DEC)
    common["valid_s"] = vs
    maps = []
    for c in range(8):
        b, h = c // 2, c % 2
        m = dict(common)
        xw = np.zeros((SEQ, D), np.float32)
        if h == 0:
            xw[HALF:] = inp["x_prompt"][b, :HALF]
            pos = np.arange(SEQ) - HALF
        else:
            xw[:] = inp["x_prompt"][b]
            pos = np.arange(SEQ)
        m["xw"] = xw
        cp, sp_ = _rope_tables(np.maximum(pos, 0))
        m["cos_p"] = f32(cp.reshape(32, 128, 16).transpose(1, 0, 2))
        m["sin_p"] = f32(sp_.reshape(32, 128, 16).transpose(1, 0, 2))
        m["valid_p"] = f32((pos >= 0).reshape(32, 128).T)
        m["hval"] = np.full((128, 1), float(h), np.float32)
        xs = np.zeros((128, D), np.float32)
        xs[:DEC] = inp["x_sample"][c]
        m["xsw"] = xs
        m["cache_ckv"] = f32(inp["cache_kv_latent"][0, c])
        m["cache_kpe"] = f32(inp["cache_k_rope"][0, c])
        m["s5init_s"] = f32(np.concatenate([inp["state_s5_re"][0, c].T, inp["state_s5_im"][0, c].T], axis=0))
        m["convh_s"] = f32(np.transpose(inp["state_ffn_conv"][0, c].reshape(2, 2 * NFF, 128), (2, 1, 0)))
        maps.append(m)
    return maps


_NC_CACHE = {}


def kernel(**inputs):
    inp = {k: np.asarray(v) for k, v in inputs.items()}
    if "nc" not in _NC_CACHE:
        _NC_CACHE["nc"] = build_program()
    nc = _NC_CACHE["nc"]
    maps = make_in_maps(inp)
    res = run_bass_kernel_spmd(nc, maps, core_ids=list(range(8)))
    R = res.results
    yp = np.zeros((4, SEQ, D), np.float32)
    nkv = np.zeros((1, 4, SEQ, DKV), np.float32)
    nkr = np.zeros((1, 4, SEQ, ROPE), np.float32)
    s5re = np.zeros((1, 4, NG, 64), np.float32); s5im = np.zeros((1, 4, NG, 64), np.float32)
    conv = np.zeros((1, 4, 2, 2 * DFF), np.float32)
    ys = np.zeros((8, DEC, D), np.float32)
    nkvs = np.zeros((1, 8, DEC, DKV), np.float32); nkrs = np.zeros((1, 8, DEC, ROPE), np.float32)
    s5res = np.zeros((1, 8, NG, 64), np.float32); s5ims = np.zeros((1, 8, NG, 64), np.float32)
    convs = np.zeros((1, 8, 2, 2 * DFF), np.float32)
    unconv = lambda a: np.transpose(a, (2, 1, 0)).reshape(2, 2 * DFF)
    for c in range(8):
        b, h = c // 2, c % 2
        r = R[c]
        yp[b, h * HALF:(h + 1) * HALF] = r["y_p"]
        nkv[0, b, h * HALF:(h + 1) * HALF] = r["nkv_p"]
        nkr[0, b, h * HALF:(h + 1) * HALF] = r["nkr_p"]
        if h == 1:
            s5re[0, b] = r["s5_p"][0:64].T
            s5im[0, b] = r["s5_p"][64:128].T
            conv[0, b] = unconv(r["conv_p"])
        ys[c] = r["y_s"]
        nkvs[0, c] = r["nkv_s"]; nkrs[0, c] = r["nkr_s"]
        s5res[0, c] = r["s5_s"][0:64].T; s5ims[0, c] = r["s5_s"][64:128].T
        convs[0, c] = unconv(r["conv_s"])
    return (yp, ys, nkv, nkr, s5re, s5im, conv, nkvs, nkrs, s5res, s5ims, convs)
```

```python
import math
import os
import numpy as np
from contextlib import ExitStack
import ml_dtypes
import concourse.bass as bass
import concourse.mybir as mybir
from concourse.bass_utils import run_bass_kernel_spmd
from concourse.alu_op_type import AluOpType as ALU

F32 = mybir.dt.float32
BF16 = mybir.dt.bfloat16
AF = mybir.ActivationFunctionType
AX = mybir.AxisListType

D = 1024
NH = 8
QKD = 96
DIN_Q = 384
DKV = 256
ROPE = 32
S5W = 512
NG = 32
DFF = 2816
NFF = DFF // 128
EPS = 1e-6
ATTN_SCALE = 1.0 / math.sqrt(96.0)
GELU_C = math.sqrt(2.0 / math.pi)
SEQ = 4096
HALF = 2048
PAST = 2048
DEC = 16


NO_SELF_RAW = set()


class Buf:
    __slots__ = ("name", "w", "r")

    def __init__(self, name):
        self.name = name
        self.w = None
        self.r = []


class Sched:
    def __init__(self, nc, es, n_dma_sems=40):
        self.nc = nc
        self.eng = {"pe": nc.tensor, "act": nc.scalar, "dve": nc.vector,
                    "pool": nc.gpsimd, "sp": nc.sync}
        self.sem = {}
        self.cnt = {}
        for k in self.eng:
            self.sem[k] = es.enter_context(nc.semaphore("s_" + k))
            self.cnt[k] = 0
        self.dsem = [es.enter_context(nc.semaphore("d%d" % i)) for i in range(n_dma_sems)]
        self.dcnt = [0] * n_dma_sems
        self.dnext = 0
        self.seen = {k: {} for k in self.eng}
        self.nwait = 0
        self.nops = 0

    def _semof(self, key):
        if isinstance(key, tuple):
            return self.dsem[key[1]]
        return self.sem[key]

    def _need(self, e, deps):
        best = {}
        for d in deps:
            k, v = d
            if v > best.get(k, 0):
                best[k] = v
        for k, v in best.items():
            if self.seen[e].get(k, 0) >= v:
                continue
            self.eng[e].wait_ge(self._semof(k), v)
            self.nwait += 1
            self.seen[e][k] = v

    def op(self, e, reads, writes, fn):
        deps = []
        for b in reads:
            if b.w is not None and (b.w[0] != e or e not in NO_SELF_RAW):
                deps.append(b.w)
        for b in writes:
            if b.w is not None and b.w[0] != e:
                deps.append(b.w)
            for r in b.r:
                if r[0] != e:
                    deps.append(r)
        self._need(e, deps)
        ins = fn(self.eng[e])
        self.cnt[e] += 1
        v = self.cnt[e]
        ins.then_inc(self.sem[e], 1)
        self.nops += 1
        for b in writes:
            b.w = (e, v)
            b.r = []
        for b in reads:
            if b not in writes:
                b.r.append((e, v))
        return ins

    def dma(self, q, out, in_, reads, writes, **kw):
        slot = self.dnext
        self.dnext = (self.dnext + 1) % len(self.dsem)
        key = ("dma", slot)
        deps = []
        if self.dcnt[slot] > 0:
            deps.append((key, self.dcnt[slot]))
        for b in reads:
            if b.w is not None:
                deps.append(b.w)
        for b in writes:
            if b.w is not None:
                deps.append(b.w)
            deps.extend(b.r)
        self._need(q, deps)
        ins = self.eng[q].dma_start(out=out, in_=in_, **kw)
        self.dcnt[slot] += 16
        v = self.dcnt[slot]
        ins.then_inc(self.dsem[slot], 16)
        for b in writes:
            b.w = (key, v)
            b.r = []
        for b in reads:
            if b not in writes:
                b.r.append((key, v))
        return ins

    def barrier(self, dma=True):
        deps = [(k, self.cnt[k]) for k in self.eng if self.cnt[k] > 0]
        if dma:
            deps += [(("dma", i), c) for i, c in enumerate(self.dcnt) if c > 0]
        for e in self.eng:
            self._need(e, [d for d in deps if d[0] != e])


class Inst:
    pass


def build_program(do_sample=True, stop_after=99, dbg=False):
    nc = bass.Bass("TRN2", target_bir_lowering=False)
    di = {}

    def inp(name, shape, dt=F32):
        di[name] = nc.dram_tensor(name, list(shape), dt, kind="ExternalInput").ap()
        return di[name]

    def outp(name, shape, dt=F32):
        di[name] = nc.dram_tensor(name, list(shape), dt, kind="ExternalOutput").ap()
        return di[name]

    xw = inp("xw", [SEQ, D])
    xsw = inp("xsw", [128, D])
    cache_ckv = inp("cache_ckv", [PAST, DKV])
    cache_kpe = inp("cache_kpe", [PAST, ROPE])
    valid_p = inp("valid_p", [128, 32])
    valid_s = inp("valid_s", [128, 17])
    cos_p = inp("cos_p", [128, 32, 16])
    sin_p = inp("sin_p", [128, 32, 16])
    cos_s = inp("cos_s", [128, 1, 16])
    sin_s = inp("sin_s", [128, 1, 16])
    masks_d = inp("masks", [128, 4, 512])
    hval_d = inp("hval", [128, 1])
    s5init_s = inp("s5init_s", [128, NG])
    convh_s = inp("convh_s", [128, 2 * NFF, 2])
    mask8_d = inp("mask8", [128, 8])
    sgn_d = inp("sgn", [128, 2])
    w_in = inp("w_in", [D, 1184])
    w_q_b = inp("w_q_b", [DIN_Q, NH * QKD])
    w_kv_b = inp("w_kv_b", [DKV, 1024])
    w_glu_d = inp("w_s5_glu", [S5W, S5W])
    w_out_d = inp("w_out", [D, D])
    w_up_d = inp("w_up", [D, 2 * DFF])
    w_down_d = inp("w_down", [DFF, D])
    g_mix_d = inp("g_mix_b", [128, D])
    g_qa_d = inp("g_qa_b", [128, DIN_Q])
    g_kva_d = inp("g_kva_b", [128, DKV])
    g_ffn_d = inp("g_ffn_b", [128, D])
    g_qh_d = inp("g_qh_b", [128, QKD])
    g_kh_d = inp("g_kh_b", [128, QKD])
    are_d = inp("s5_are", [128, NG])
    aim_d = inp("s5_aim", [128, NG])
    ldt_d = inp("s5_ldt", [128, NG])
    bst_d = inp("s5_bst", [128, NG, 16])
    bsw_d = inp("s5_bsw", [128, NG, 16])
    cn_d = inp("s5_cn", [128, 4, 128])
    cns_d = inp("s5_cns", [128, 4, 128])
    d_d = inp("s5_dl", [128, 4])
    bglu_d = inp("s5_bglu", [128, 4])
    wdw_d = inp("w_dw_l", [128, 2 * NFF, 3])
    bdw_d = inp("b_dw_l", [128, 2 * NFF])

    y_p = outp("y_p", [HALF, D])
    nkv_p = outp("nkv_p", [HALF, DKV])
    nkr_p = outp("nkr_p", [HALF, ROPE])
    s5_p = outp("s5_p", [128, NG])
    conv_p = outp("conv_p", [128, 2 * NFF, 2])
    y_s = outp("y_s", [DEC, D])
    nkv_s = outp("nkv_s", [DEC, DKV])
    nkr_s = outp("nkr_s", [DEC, ROPE])
    s5_s = outp("s5_s", [128, NG])
    conv_s = outp("conv_s", [128, 2 * NFF, 2])
    if dbg:
        dbg_ob = outp("dbg_ob", [128, 4, 17 * 128], BF16)
        dbg_oa = outp("dbg_oa", [128, 4, 17 * 128], BF16)
        dbg_q = outp("dbg_q", [96, 8, 17 * 128], BF16)
    scr_up = nc.dram_tensor("scr_up", [D, 2 * DFF], BF16, kind="Internal").ap()
    sc5 = {
        "cos": nc.dram_tensor("sc_cos", [128, NG * 128], F32, kind="Internal").ap(),
        "sin": nc.dram_tensor("sc_sin", [128, NG * 128], F32, kind="Internal").ap(),
        "bpad": nc.dram_tensor("sc_bpad", [128, NG * 128], BF16, kind="Internal").ap(),
        "bpsw": nc.dram_tensor("sc_bpsw", [128, NG * 128], BF16, kind="Internal").ap(),
        "cw1": nc.dram_tensor("sc_cw1", [128, NG * 128], BF16, kind="Internal").ap(),
        "cw2": nc.dram_tensor("sc_cw2", [128, NG * 128], BF16, kind="Internal").ap(),
        "diag": nc.dram_tensor("sc_diag", [128, 512], BF16, kind="Internal").ap(),
        "rm": nc.dram_tensor("sc_rm", [128, NG], F32, kind="Internal").ap(),
    }
    Bsc5 = {k: Buf("sc5_" + k) for k in sc5}

    es = ExitStack()
    with es:
        cnt = [0]

        def sb(shape, dt, stack=None, name=None):
            cnt[0] += 1
            return (stack or es).enter_context(
                nc.sbuf_tensor((name or "t") + "_%d" % cnt[0], list(shape), dt))

        banks = [es.enter_context(nc.psum_tensor("bank%d" % i, [128, 512], F32)) for i in range(8)]
        Bbank = [Buf("bank%d" % i) for i in range(8)]
        bankb = [b[:].bitcast(BF16) for b in banks]

        ident = sb([128, 128], BF16); Bident = Buf("ident")
        identf = sb([128, 128], F32)
        swp = sb([128, 128], F32); Bswp = Buf("swp")
        mh = sb([128, 8], F32); Bmh = Buf("mh")
        Bg = Buf("gconst")
        obT_p = sb([128, 4, 17 * 128], BF16)
        oaT_p = sb([128, 4, 17 * 128], BF16)
        obT_s = sb([128, 4, 128], BF16)
        oaT_s = sb([128, 4, 128], BF16)
        P4 = {}
        hval = sb([128, 1], F32); Bhval = Buf("hval")
        sgn = sb([128, 2], F32); Bsgn = Buf("sgn")
        junk = sb([128, 1024], F32); Bjunk = Buf("junk")

        blk = es.enter_context(nc.Block())
        S = Sched(nc, es)
        A = lambda r, w, f: S.op("act", r, w, f)
        V = lambda r, w, f: S.op("dve", r, w, f)
        G = lambda r, w, f: S.op("pool", r, w, f)
        T = lambda r, w, f: S.op("pe", r, w, f)

        def bc_mid(ap, n):
            return ap.unsqueeze(1).to_broadcast([ap.shape[0], n, ap.shape[1]])

        def bc_last(ap, n):
            return ap.unsqueeze(2).to_broadcast([ap.shape[0], ap.shape[1], n])

        G([], [Bident], lambda e: e.memset(identf[:], 0.0))
        G([Bident], [Bident], lambda e: e.affine_select(
            out=identf[:], in_=identf[:], pattern=[[-1, 128]], compare_op=ALU.not_equal,
            fill=1.0, base=0, channel_multiplier=1))
        G([Bident], [Bident], lambda e: e.tensor_copy(out=ident[:], in_=identf[:]))
        G([Bident], [Bswp], lambda e: e.tensor_copy(out=swp[:, 0:64], in_=identf[:, 64:128]))
        G([Bident, Bswp], [Bswp], lambda e: e.tensor_copy(out=swp[:, 64:128], in_=identf[:, 0:64]))
        G([], [Bmh], lambda e: e.memset(mh[:], -0.5))
        S.dma("sp", hval[:], hval_d, [], [Bhval])
        S.dma("sp", sgn[:], sgn_d, [], [Bsgn])

        def rstd_from(ssq, Bssq, n, dim, out, Bout):
            V([Bssq], [Bssq], lambda e: e.tensor_scalar(out=ssq, in0=ssq, scalar1=1.0 / dim, scalar2=EPS,
                                                        op0=ALU.mult, op1=ALU.add))
            G([Bssq, Bmh], [Bout], lambda e: e.tensor_tensor(out=out, in0=ssq, in1=mh[:, 0:n], op=ALU.pow))

        def run_instance(I):
            NW, OWN0 = I.NW, I.OWN0
            NOWN = NW - OWN0
            oaT, obT, BoaT, BobT = I.oaT, I.obT, Buf("oaT"), Buf("obT")
            pk = ExitStack()
            w_kvb = sb([128, 2, 1024], BF16, pk); Bwkvb = Buf("wkvb")
            ckvT = sb([128, 2, NW * 128], BF16, pk); BckvT = [Buf("ckvT%d" % i) for i in range(NW)]
            krotT = sb([96, NW * 128], BF16, pk); BkrotT = [Buf("krotT%d" % i) for i in range(NW)]
            kscale = sb([128, NW, NH], F32, pk); Bkscale = [Buf("kscale%d" % i) for i in range(NW)]
            g_mix = sb([128, D], F32, pk)
            S.dma("sp", g_mix[:], g_mix_d, [], [Bg])
            S.dma("pool", w_kvb[:], w_kv_b.rearrange("(k p) n -> p k n", p=128), [], [Bwkvb])
            with ExitStack() as p1:
                g_kva = sb([128, DKV], F32, p1)
                S.dma("sp", g_kva[:], g_kva_d, [], [Bg])
                w_in_kvu = sb([128, 8, 800], BF16, p1); Bw1 = Buf("w_in_kvu")
                w_nope = sb([128, 2, 512], BF16, p1); Bwn = Buf("w_nope")
                w_glu = sb([128, 4, 512], BF16, p1); Bwg = Buf("w_glu")
                dl = sb([128, 4], F32, p1); bglu = sb([128, 4], F32, p1); mask8 = sb([128, 8], F32, p1)
                Bdl = Buf("dl")
                Rm = sb([128, NG], F32, p1)
                Bpad = sb([128, NG, 128], BF16, p1); Bpsw = sb([128, NG, 128], BF16, p1)
                Cw1 = sb([128, NG, 128], BF16, p1); Cw2 = sb([128, NG, 128], BF16, p1)
                diagD = sb([128, 4, 128], BF16, p1)
                Bw5 = Buf("s5w")
                COS = sb([128, NG, 128], F32, p1); SIN = sb([128, NG, 128], F32, p1)
                Btab = Buf("tab")
                CL = sb([128, NG], F32, p1); NSL = sb([128, NG], F32, p1); carry = sb([128, NG], F32, p1)
                CLN = sb([128, 8, 8], F32, p1)
                Bcarry = Buf("carry")
                LS = I.LS
                xf = [sb([128, D], F32, p1) for _ in range(2)]; Bxf = [Buf("xf0"), Buf("xf1")]
                cs_t = [sb([128, 2, 16], F32, p1) for _ in range(2)]; Bcs = [Buf("cs0"), Buf("cs1")]
                if I.KVC == 0:
                    S.dma("sp", xf[0][:], I.x[0:128, :], [], [Bxf[0]])
                    S.dma("sp", cs_t[0][:, 0, :], I.cos[:, 0, :], [], [Bcs[0]])
                    S.dma("sp", cs_t[0][:, 1, :], I.sin[:, 0, :], [], [Bcs[0]])
                if I.name == "p":
                    p1s = ExitStack()
                    are = sb([128, NG], F32, p1s); aim = sb([128, NG], F32, p1s); ldt = sb([128, NG], F32, p1s)
                    Bs5p = Buf("s5p")
                    bst = sb([128, NG, 16], F32, p1s); bsw = sb([128, NG, 16], F32, p1s); Bbst = Buf("bst")
                    cn = sb([128, 4, 128], F32, p1s); cns = sb([128, 4, 128], F32, p1s); Bcn = Buf("cn")
                    sm = [sb([128, NG], F32, p1s) for _ in range(14)]
                    Bsm = Buf("s5small")
                    bb = sb([128, NG, 16], F32, p1s); bbs = sb([128, NG, 16], F32, p1s); btmp = sb([128, NG, 16], F32, p1s)
                    Bbb = Buf("bb")
                    ta_ = sb([128, NG, 64], F32, p1s); tb_ = sb([128, NG, 64], F32, p1s)

                    S.dma("pool", w_in_kvu[:], w_in.rearrange("(k p) n -> p k n", p=128)[:, :, 384:1184], [], [Bw1])
                    for k in range(2):
                        S.dma("pool", w_nope[:, k, :].rearrange("p (h d) -> p h d", d=64),
                              w_kv_b[k * 128:(k + 1) * 128, :].rearrange("p (h d) -> p h d", d=128)[:, :, 0:64],
                              [], [Bwn])
                    S.dma("pool", w_glu[:], w_glu_d.rearrange("(k p) n -> p k n", p=128), [], [Bwg])
                    S.dma("sp", are[:], are_d, [], [Bs5p])
                    S.dma("sp", aim[:], aim_d, [], [Bs5p])
                    S.dma("sp", ldt[:], ldt_d, [], [Bs5p])
                    S.dma("sp", bst[:], bst_d, [], [Bbst])
                    S.dma("sp", bsw[:], bsw_d, [], [Bbst])
                    S.dma("sp", cn[:], cn_d, [], [Bcn])
                    S.dma("sp", cns[:], cns_d, [], [Bcn])
                    S.dma("sp", dl[:], d_d, [], [Bdl])
                    S.dma("sp", bglu[:], bglu_d, [], [Bdl])
                    S.dma("sp", mask8[:], mask8_d, [], [Bdl])
                    V([Bdl], [Bdl], lambda e: e.tensor_scalar(out=bglu[:], in0=bglu[:], scalar1=0.5, scalar2=None, op0=ALU.mult))

                    dt_, e1, mag, ang, shh, s16, cth, sth, cc, ss, cs, fre, fim, tmp = sm
                    A([Bs5p], [Bsm], lambda e: e.activation(out=dt_[:], in_=ldt[:], func=AF.Exp))
                    V([Bs5p, Bsm], [Bsm], lambda e: e.tensor_tensor(out=e1[:], in0=are[:], in1=dt_[:], op=ALU.mult))
                    A([Bsm], [Bsm], lambda e: e.activation(out=mag[:], in_=e1[:], func=AF.Exp))
                    V([Bs5p, Bsm], [Bsm], lambda e: e.tensor_tensor(out=ang[:], in0=aim[:], in1=dt_[:], op=ALU.mult))
                    A([Bsm], [Bsm], lambda e: e.activation(out=shh[:], in_=ang[:], func=AF.Sin, scale=1.0 / 32))
                    A([Bsm], [Bsm], lambda e: e.activation(out=sth[:], in_=ang[:], func=AF.Sin, scale=1.0 / 16))
                    V([Bsm], [Bsm], lambda e: e.tensor_tensor(out=cth[:], in0=shh[:], in1=shh[:], op=ALU.mult))
                    V([Bsm], [Bsm], lambda e: e.tensor_scalar(out=cth[:], in0=cth[:], scalar1=-2.0, scalar2=1.0,
                                                              op0=ALU.mult, op1=ALU.add))
                    for _ in range(4):
                        V([Bsm], [Bsm], lambda e: e.tensor_tensor(out=cc[:], in0=cth[:], in1=cth[:], op=ALU.mult))
                        V([Bsm], [Bsm], lambda e: e.tensor_tensor(out=ss[:], in0=sth[:], in1=sth[:], op=ALU.mult))
                        V([Bsm], [Bsm], lambda e: e.tensor_tensor(out=cs[:], in0=cth[:], in1=sth[:], op=ALU.mult))
                        V([Bsm], [Bsm], lambda e: e.tensor_tensor(out=cth[:], in0=cc[:], in1=ss[:], op=ALU.subtract))
                        V([Bsm], [Bsm], lambda e: e.tensor_scalar(out=sth[:], in0=cs[:], scalar1=2.0, scalar2=None, op0=ALU.mult))
                    V([Bsm], [Bsm], lambda e: e.tensor_copy(out=Rm[:], in_=mag[:]))
                    lbr, lbi = cc, ss
                    V([Bsm], [Bsm], lambda e: e.tensor_tensor(out=lbr[:], in0=mag[:], in1=cth[:], op=ALU.mult))
                    V([Bsm], [Bsm], lambda e: e.tensor_tensor(out=lbi[:], in0=mag[:], in1=sth[:], op=ALU.mult))
                    V([Bsm], [Bsm], lambda e: e.tensor_scalar(out=lbr[:], in0=lbr[:], scalar1=-1.0, scalar2=None, op0=ALU.add))
                    den = cs
                    V([Bs5p], [Bsm], lambda e: e.tensor_tensor(out=den[:], in0=are[:], in1=are[:], op=ALU.mult))
                    V([Bs5p, Bsm], [Bsm], lambda e: e.tensor_tensor(out=tmp[:], in0=aim[:], in1=aim[:], op=ALU.mult))
                    V([Bsm], [Bsm], lambda e: e.tensor_tensor(out=den[:], in0=den[:], in1=tmp[:], op=ALU.add))
                    V([Bsm], [Bsm], lambda e: e.reciprocal(out=den[:], in_=den[:]))
                    V([Bs5p, Bsm], [Bsm], lambda e: e.tensor_tensor(out=fre[:], in0=lbr[:], in1=are[:], op=ALU.mult))
                    V([Bs5p, Bsm], [Bsm], lambda e: e.tensor_tensor(out=tmp[:], in0=lbi[:], in1=aim[:], op=ALU.mult))
                    V([Bsm], [Bsm], lambda e: e.tensor_tensor(out=fre[:], in0=fre[:], in1=tmp[:], op=ALU.add))
                    V([Bsm], [Bsm], lambda e: e.tensor_tensor(out=fre[:], in0=fre[:], in1=den[:], op=ALU.mult))
                    V([Bs5p, Bsm], [Bsm], lambda e: e.tensor_tensor(out=fim[:], in0=lbi[:], in1=are[:], op=ALU.mult))
                    V([Bs5p, Bsm], [Bsm], lambda e: e.tensor_tensor(out=tmp[:], in0=lbr[:], in1=aim[:], op=ALU.mult))
                    V([Bsm], [Bsm], lambda e: e.tensor_tensor(out=fim[:], in0=fim[:], in1=tmp[:], op=ALU.subtract))
                    V([Bsm], [Bsm], lambda e: e.tensor_tensor(out=fim[:], in0=fim[:], in1=den[:], op=ALU.mult))
                    V([Bsm, Bsgn], [Bsm], lambda e: e.tensor_scalar(out=fim[:], in0=fim[:], scalar1=sgn[:, 0:1], scalar2=None, op0=ALU.mult))
                    V([Bsm, Bbst], [Bbb], lambda e: e.tensor_tensor(out=bb[:], in0=bst[:], in1=bc_last(fre[:], 16), op=ALU.mult))
                    V([Bsm, Bbst, Bbb], [Bbb], lambda e: e.tensor_tensor(out=btmp[:], in0=bsw[:], in1=bc_last(fim[:], 16), op=ALU.mult))
                    V([Bbb], [Bbb], lambda e: e.tensor_tensor(out=bb[:], in0=bb[:], in1=btmp[:], op=ALU.add))
                    V([Bsm, Bbst, Bbb], [Bbb], lambda e: e.tensor_tensor(out=bbs[:], in0=bsw[:], in1=bc_last(fre[:], 16), op=ALU.mult))
                    V([Bsm, Bbst, Bbb], [Bbb], lambda e: e.tensor_tensor(out=btmp[:], in0=bst[:], in1=bc_last(fim[:], 16), op=ALU.mult))
                    V([Bbb], [Bbb], lambda e: e.tensor_tensor(out=bbs[:], in0=bbs[:], in1=btmp[:], op=ALU.subtract))
                    G([], [Bw5], lambda e: e.memset(Cw1[:], 0.0))
                    G([Bw5], [Bw5], lambda e: e.memset(Cw2[:], 0.0))
                    for gb in range(4):
                        for (src, dst) in ((bb, Bpad), (bbs, Bpsw)):
                            bkk = gb if src is bb else 4 + gb
                            T([Bbb, Bident], [Bbank[bkk]], lambda e: e.transpose(
                                banks[bkk][:, 0:128], src[:, gb * 8:(gb + 1) * 8, :].rearrange("p a b -> p (a b)"), identf[:]))
                            for j in range(8):
                                V([Bbank[bkk], Bdl], [Bw5], lambda e: e.tensor_scalar(
                                    out=dst[:, gb * 8 + j, :], in0=banks[bkk][:, 0:128], scalar1=mask8[:, j:j + 1],
                                    scalar2=None, op0=ALU.mult))
                        for (src, dst, sc) in ((cn, Cw1, 1), (cns, Cw2, 0)):
                            bkc = gb if sc == 1 else 4 + gb
                            T([Bcn, Bident], [Bbank[bkc]], lambda e: e.transpose(banks[bkc][:, 128:256], src[:, gb, :], identf[:]))
                            for j in range(8):
                                V([Bbank[bkc], Bsgn], [Bw5], lambda e: e.tensor_scalar(
                                    out=dst[:, gb * 8 + j, 16 * j:16 * j + 16], in0=banks[bkc][:, 128 + 16 * j:128 + 16 * j + 16],
                                    scalar1=sgn[:, sc:sc + 1], scalar2=None, op0=ALU.mult))
                        V([Bdl, Bident], [Bw5], lambda e: e.tensor_scalar(
                            out=diagD[:, gb, :], in0=identf[:], scalar1=dl[:, gb:gb + 1], scalar2=None, op0=ALU.mult))
                    V([Bsm], [Btab], lambda e: e.tensor_copy(out=COS[:, :, 0], in_=cth[:]))
                    V([Bsm, Btab], [Btab], lambda e: e.tensor_copy(out=SIN[:, :, 0], in_=sth[:]))
                    m = 1
                    while m < 128:
                        cm = COS[:, :, m - 1:m].to_broadcast([128, NG, m])
                        smm = SIN[:, :, m - 1:m].to_broadcast([128, NG, m])
                        V([Btab], [Btab], lambda e: e.tensor_tensor(out=ta_[:, :, 0:m], in0=COS[:, :, 0:m], in1=cm, op=ALU.mult))
                        V([Btab], [Btab], lambda e: e.tensor_tensor(out=tb_[:, :, 0:m], in0=SIN[:, :, 0:m], in1=smm, op=ALU.mult))
                        V([Btab], [Btab], lambda e: e.tensor_tensor(out=COS[:, :, m:2 * m], in0=ta_[:, :, 0:m], in1=tb_[:, :, 0:m], op=ALU.subtract))
                        V([Btab], [Btab], lambda e: e.tensor_tensor(out=ta_[:, :, 0:m], in0=SIN[:, :, 0:m], in1=cm, op=ALU.mult))
                        V([Btab], [Btab], lambda e: e.tensor_tensor(out=tb_[:, :, 0:m], in0=COS[:, :, 0:m], in1=smm, op=ALU.mult))
                        V([Btab], [Btab], lambda e: e.tensor_tensor(out=SIN[:, :, m:2 * m], in0=ta_[:, :, 0:m], in1=tb_[:, :, 0:m], op=ALU.add))
                        m *= 2
                    V([Btab, Bsgn], [Btab], lambda e: e.tensor_scalar(
                        out=SIN[:].rearrange("p a b -> p (a b)"), in0=SIN[:].rearrange("p a b -> p (a b)"),
                        scalar1=sgn[:, 1:2], scalar2=None, op0=ALU.mult))

                    flat = lambda ap: ap.rearrange("p a b -> p (a b)")
                    S.dma("sp", sc5["cos"], flat(COS[:]), [Btab], [Bsc5["cos"]])
                    S.dma("sp", sc5["sin"], flat(SIN[:]), [Btab], [Bsc5["sin"]])
                    S.dma("sp", sc5["bpad"], flat(Bpad[:]), [Bw5], [Bsc5["bpad"]])
                    S.dma("sp", sc5["bpsw"], flat(Bpsw[:]), [Bw5], [Bsc5["bpsw"]])
                    S.dma("sp", sc5["cw1"], flat(Cw1[:]), [Bw5], [Bsc5["cw1"]])
                    S.dma("sp", sc5["cw2"], flat(Cw2[:]), [Bw5], [Bsc5["cw2"]])
                    S.dma("sp", sc5["diag"], flat(diagD[:]), [Bw5], [Bsc5["diag"]])
                    S.dma("sp", sc5["rm"], Rm[:], [Bsm], [Bsc5["rm"]])
                else:
                    p1s = ExitStack()
                    Bsm = Buf("s5small")
                    flat = lambda ap: ap.rearrange("p a b -> p (a b)")
                    S.dma("pool", w_in_kvu[:], w_in.rearrange("(k p) n -> p k n", p=128)[:, :, 384:1184], [], [Bw1])
                    for k in range(2):
                        S.dma("pool", w_nope[:, k, :].rearrange("p (h d) -> p h d", d=64),
                              w_kv_b[k * 128:(k + 1) * 128, :].rearrange("p (h d) -> p h d", d=128)[:, :, 0:64],
                              [], [Bwn])
                    S.dma("pool", w_glu[:], w_glu_d.rearrange("(k p) n -> p k n", p=128), [], [Bwg])
                    S.dma("sp", bglu[:], bglu_d, [], [Bdl])
                    V([Bdl], [Bdl], lambda e: e.tensor_scalar(out=bglu[:], in0=bglu[:], scalar1=0.5, scalar2=None, op0=ALU.mult))
                    tb_ = [Buf("ld%d" % k) for k in range(8)]
                    S.dma("sp", flat(COS[:]), sc5["cos"], [Bsc5["cos"]], [tb_[0]])
                    S.dma("sp", flat(SIN[:]), sc5["sin"], [Bsc5["sin"]], [tb_[1]])
                    S.dma("sp", flat(Bpad[:]), sc5["bpad"], [Bsc5["bpad"]], [tb_[2]])
                    S.dma("sp", flat(Bpsw[:]), sc5["bpsw"], [Bsc5["bpsw"]], [tb_[3]])
                    S.dma("sp", flat(Cw1[:]), sc5["cw1"], [Bsc5["cw1"]], [tb_[4]])
                    S.dma("sp", flat(Cw2[:]), sc5["cw2"], [Bsc5["cw2"]], [tb_[5]])
                    S.dma("sp", flat(diagD[:]), sc5["diag"], [Bsc5["diag"]], [tb_[6]])
                    S.dma("sp", Rm[:], sc5["rm"], [Bsc5["rm"]], [tb_[7]])
                    V(tb_, [Btab, Bw5, Bsm], lambda e: e.memset(NSL[:, 0:1], 0.0))
                V([Btab], [Btab], lambda e: e.tensor_copy(out=CL[:], in_=COS[:, :, LS - 1]))
                V([Btab], [Btab], lambda e: e.tensor_scalar(out=NSL[:], in0=SIN[:, :, LS - 1], scalar1=-1.0, scalar2=None, op0=ALU.mult))
                V([Btab], [Btab], lambda e: e.tensor_copy(out=CLN[:, :, 0:4], in_=CL[:].rearrange("p (a b) -> p a b", b=4)))
                V([Btab], [Btab], lambda e: e.tensor_copy(out=CLN[:, :, 4:8], in_=NSL[:].rearrange("p (a b) -> p a b", b=4)))
                if I.s5init is None:
                    V([], [Bcarry], lambda e: e.memset(carry[:], 0.0))
                else:
                    S.dma("sp", carry[:], I.s5init, [], [Bcarry])
                S.barrier(dma=False)
                p1s.close()

                xs_b = sb([128, D], BF16, p1); Bxs = Buf("xs")
                xT = [sb([128, 8, 128], BF16, p1) for _ in range(2)]; BxT = [Buf("xT0"), Buf("xT1")]
                uT = [sb([128, 4, 128], BF16, p1) for _ in range(2)]; BuT = [Buf("uT0"), Buf("uT1")]
                st = sb([128, 16], F32, p1); Bst = Buf("st")
                ckv_f = [sb([128, DKV], F32, p1) for _ in range(2)]; Bckv = [Buf("ckvf0"), Buf("ckvf1")]
                kpe_f = [sb([128, ROPE], F32, p1) for _ in range(2)]; Bkpe = [Buf("kpef0"), Buf("kpef1")]
                ckv_b = sb([128, DKV], BF16, p1); Bckvb = Buf("ckvb")
                krin = sb([128, 96], BF16, p1); Bkrin = Buf("krin")
                rt = sb([128, 4, 16], F32, p1); Brt = Buf("rt")
                ssqn = sb([128, 8], F32, p1); Bssqn = Buf("ssqn")
                t1 = [sb([128, 512], F32, p1) for _ in range(2)]; Bt1 = [Buf("t1a"), Buf("t1b")]
                t2 = [sb([128, 512], F32, p1) for _ in range(2)]; Bt2 = [Buf("t2a"), Buf("t2b")]
                Z = [sb([128, 4, 128], F32, p1) for _ in range(2)]; BZ = [Buf("Za"), Buf("Zb")]
                Z1 = [sb([128, 4, 128], BF16, p1) for _ in range(2)]
                Z2 = [sb([128, 4, 128], BF16, p1) for _ in range(2)]; BZ12 = [Buf("Z12a"), Buf("Z12b")]
                cst = [sb([128, 8], F32, p1) for _ in range(2)]; Bcst = [Buf("csta"), Buf("cstb")]
                Bcar = [Buf("carry%d" % q) for q in range(8)]
                for q in range(8):
                    Bcar[q].w = Bcarry.w
                gl = sb([128, 4, 128], BF16, p1); gh = sb([128, 4, 128], BF16, p1); th = sb([128, 4, 128], BF16, p1)
                Bgl = Buf("gl"); Bgh = Buf("gh"); Bth = Buf("th")
                G([], [Bkrin], lambda e: e.memset(krin[:], 0.0))
                for zi in range(2):
                    V([], [BZ[zi]], lambda e: e.memset(Z[zi][:], 0.0))

                def load_tile(t):
                    i = t % 2
                    if t < I.KVC:
                        S.dma("sp", ckv_f[i][:], I.cache_ckv[t * 128:(t + 1) * 128, :], [], [Bckv[i]])
                        S.dma("sp", kpe_f[i][:], I.cache_kpe[t * 128:(t + 1) * 128, :], [], [Bkpe[i]])
                    else:
                        tt = t - I.KVC
                        S.dma("sp", xf[i][:], I.x[tt * 128:(tt + 1) * 128, :], [], [Bxf[i]])
                        S.dma("sp", cs_t[i][:, 0, :], I.cos[:, tt, :], [], [Bcs[i]])
                        S.dma("sp", cs_t[i][:, 1, :], I.sin[:, tt, :], [], [Bcs[i]])

                def stage1_steps(t):
                    i = t % 2
                    steps = [[] for _ in range(8)]

                    def add(k, fn):
                        steps[k].append(fn)
                    if t + 1 < NW:
                        add(0, lambda: load_tile(t + 1))
                    if t >= I.KVC:
                        add(0, lambda: A([Bxf[i]], [Bjunk, Bst], lambda e: e.activation(
                            out=junk[:], in_=xf[i][:], func=AF.Square, accum_out=st[:, 0:1])))
                        add(1, lambda: rstd_from(st[:, 0:1], Bst, 1, D, st[:, 1:2], Bst))
                        add(2, lambda: V([Bxf[i], Bst, Bg], [Bxs], lambda e: e.scalar_tensor_tensor(
                            out=xs_b[:], in0=xf[i][:], scalar=st[:, 1:2], in1=g_mix[:], op0=ALU.mult, op1=ALU.mult)))

                        def tr(e):
                            for k in range(8):
                                r = e.transpose(bankb[0][:, k * 128:(k + 1) * 128], xs_b[:, k * 128:(k + 1) * 128], ident[:])
                            return r
                        add(3, lambda: T([Bxs, Bident], [Bbank[0]], tr))
                        add(3, lambda: A([Bbank[0]], [BxT[i]], lambda e: e.activation(
                            out=xT[i][:].rearrange("p a b -> p (a b)"), in_=bankb[0][:, 0:1024], func=AF.Copy)))

                        def mm_kv(e):
                            for k in range(8):
                                r = e.matmul(banks[1][:, 0:288], lhsT=xT[i][:, k, :], rhs=w_in_kvu[:, k, 0:288],
                                             start=(k == 0), stop=(k == 7))
                            return r

                        def mk_mm_u(mo):
                            def mm_u(e):
                                for k in range(8):
                                    r = e.matmul(banks[0][:, mo * 128:(mo + 1) * 128],
                                                 lhsT=w_in_kvu[:, k, 288 + mo * 128:288 + (mo + 1) * 128],
                                                 rhs=xT[i][:, k, :], start=(k == 0), stop=(k == 7))
                                return r
                            return mm_u
                        add(4, lambda: T([BxT[i], Bw1], [Bbank[1]], mm_kv))
                        for mo in range(4):
                            add(4 + mo, (lambda mo=mo: T([BxT[i], Bw1], [Bbank[0]], mk_mm_u(mo))))
                        add(7, lambda: A([Bbank[0]], [BuT[i]], lambda e: e.activation(
                            out=uT[i][:].rearrange("p a b -> p (a b)"), in_=banks[0][:, 0:512], func=AF.Copy)))
                        add(4, lambda: A([Bbank[1]], [Bjunk, Bst], lambda e: e.activation(
                            out=junk[:, 0:DKV], in_=banks[1][:, 0:DKV], func=AF.Square, accum_out=st[:, 2:3])))
                        add(5, lambda: rstd_from(st[:, 2:3], Bst, 1, DKV, st[:, 3:4], Bst))
                        x1 = banks[1][:, 256:272]; x2 = banks[1][:, 272:288]
                        cs_, sn_ = cs_t[i][:, 0, :], cs_t[i][:, 1, :]
                        add(5, lambda: V([Bbank[1], Bcs[i]], [Brt], lambda e: e.tensor_tensor(out=rt[:, 0, :], in0=x1, in1=cs_, op=ALU.mult)))
                        add(5, lambda: V([Bbank[1], Bcs[i]], [Brt], lambda e: e.tensor_tensor(out=rt[:, 1, :], in0=x2, in1=sn_, op=ALU.mult)))
                        add(5, lambda: V([Bbank[1], Bcs[i]], [Brt], lambda e: e.tensor_tensor(out=rt[:, 2, :], in0=x2, in1=cs_, op=ALU.mult)))
                        add(5, lambda: V([Bbank[1], Bcs[i]], [Brt], lambda e: e.tensor_tensor(out=rt[:, 3, :], in0=x1, in1=sn_, op=ALU.mult)))
                        add(5, lambda: V([Brt], [Bkpe[i]], lambda e: e.tensor_tensor(out=kpe_f[i][:, 0:16], in0=rt[:, 0, :], in1=rt[:, 1, :], op=ALU.subtract)))
                        add(5, lambda: V([Brt], [Bkpe[i]], lambda e: e.tensor_tensor(out=kpe_f[i][:, 16:32], in0=rt[:, 2, :], in1=rt[:, 3, :], op=ALU.add)))
                        add(6, lambda: V([Bbank[1], Bst, Bg], [Bckv[i]], lambda e: e.scalar_tensor_tensor(
                            out=ckv_f[i][:], in0=banks[1][:, 0:DKV], scalar=st[:, 3:4], in1=g_kva[:],
                            op0=ALU.mult, op1=ALU.mult)))
                        if I.out_rows(t) is not None:
                            r0, n = I.out_rows(t)
                            add(7, lambda: S.dma("sp", I.nkv[r0:r0 + n, :], ckv_f[i][0:n, :], [Bckv[i]], []))
                            add(7, lambda: S.dma("sp", I.nkr[r0:r0 + n, :], kpe_f[i][0:n, :], [Bkpe[i]], []))
                    add(7, lambda: A([Bckv[i]], [Bckvb], lambda e: e.activation(out=ckv_b[:], in_=ckv_f[i][:], func=AF.Copy)))
                    add(7, lambda: A([Bkpe[i]], [Bkrin], lambda e: e.activation(out=krin[:, 64:96], in_=kpe_f[i][:], func=AF.Copy)))
                    add(7, lambda: A([Bkpe[i]], [Bjunk, Bst], lambda e: e.activation(
                        out=junk[:, 992:1024], in_=kpe_f[i][:], func=AF.Square, accum_out=st[:, 4 + t % 8:5 + t % 8])))

                    def tr2(e):
                        e.transpose(bankb[3][:, 0:128], ckv_b[:, 0:128], ident[:])
                        e.transpose(bankb[3][:, 128:256], ckv_b[:, 128:256], ident[:])
                        return e.transpose(bankb[3][0:96, 256:384], krin[:], ident[:])

                    def mm_st(e):
                        for k in range(2):
                            r = e.matmul(banks[1][:, 0:512], lhsT=ckvT[:, k, t * 128:(t + 1) * 128], rhs=w_nope[:, k, :],
                                         start=(k == 0), stop=(k == 1))
                        return r
                    post = [
                        lambda: T([Bckvb, Bkrin, Bident], [Bbank[3]], tr2),
                        lambda: A([Bbank[3]], [BckvT[t]], lambda e: e.activation(
                            out=ckvT[:, :, t * 128:(t + 1) * 128], in_=bankb[3][:, 0:256].rearrange("p (a b) -> p a b", b=128), func=AF.Copy)),
                        lambda: A([Bbank[3]], [BkrotT[t]], lambda e: e.activation(
                            out=krotT[64:96, t * 128:(t + 1) * 128], in_=bankb[3][64:96, 256:384], func=AF.Copy)),
                        lambda: T([BckvT[t], Bwn], [Bbank[1]], mm_st),
                        lambda: A([Bbank[1]], [Bjunk], lambda e: e.activation(out=junk[:, 0:512], in_=banks[1][:, 0:512], func=AF.Square)),
                        lambda: V([Bjunk], [Bssqn], lambda e: e.tensor_reduce(
                            out=ssqn[:], in_=junk[:, 0:512].rearrange("p (a b) -> p a b", b=64), axis=AX.X, op=ALU.add)),
                        lambda: V([Bssqn, Bst], [Bssqn], lambda e: e.tensor_scalar(
                            out=ssqn[:], in0=ssqn[:], scalar1=st[:, 4 + t % 8:5 + t % 8], scalar2=1.0 / QKD, op0=ALU.add, op1=ALU.mult)),
                        lambda: V([Bssqn], [Bssqn], lambda e: e.tensor_scalar(out=ssqn[:], in0=ssqn[:], scalar1=EPS, scalar2=None, op0=ALU.add)),
                        lambda: G([Bssqn, Bmh], [Bkscale[t]], lambda e: e.tensor_tensor(out=kscale[:, t, :], in0=ssqn[:], in1=mh[:, 0:8], op=ALU.pow)),
                        lambda: G([Bkscale[t]], [Bkscale[t]], lambda e: e.tensor_scalar(
                            out=kscale[:, t, :], in0=kscale[:, t, :], scalar1=ATTN_SCALE, scalar2=0.0, op0=ALU.mult, op1=ALU.add)),
                    ]
                    return steps, post

                def stage1(t):
                    steps, post = stage1_steps(t)
                    for k in range(8):
                        for fn in steps[k]:
                            fn()
                    for fn in post:
                        fn()

                def s5A(t, q, par, part):
                    i = t % 2
                    gb = q // 2
                    if part == 1:
                        V([Bbank[4 + par], Btab], [Bt1[par]], lambda e: e.tensor_tensor(
                            out=t1[par][:], in0=banks[4 + par][:, :],
                            in1=COS[:, q * 4:q * 4 + 4, :].rearrange("p a b -> p (a b)"), op=ALU.mult))
                        V([Bbank[6 + par], Btab], [Bt2[par]], lambda e: e.tensor_tensor(
                            out=t2[par][:], in0=banks[6 + par][:, :],
                            in1=SIN[:, q * 4:q * 4 + 4, :].rearrange("p a b -> p (a b)"), op=ALU.mult))
                        V([Bt1[par], Bt2[par]], [Bt1[par]], lambda e: e.tensor_tensor(out=t1[par][:], in0=t1[par][:], in1=t2[par][:], op=ALU.add))
                        return

                    def mmAB(e):
                        for j in range(4):
                            e.matmul(banks[4 + par][:, j * 128:(j + 1) * 128],
                                     lhsT=Bpad[:, q * 4 + j, :], rhs=uT[i][:, gb, :], start=True, stop=True)
                        for j in range(4):
                            r = e.matmul(banks[6 + par][:, j * 128:(j + 1) * 128],
                                         lhsT=Bpsw[:, q * 4 + j, :], rhs=uT[i][:, gb, :], start=True, stop=True)
                        return r
                    T([BuT[i], Bw5], [Bbank[4 + par], Bbank[6 + par]], mmAB)

                def s5B(t, q, par, own, part):
                    i = t % 2
                    gb = q // 2
                    gs = slice(q * 4, q * 4 + 4)
                    if part == 1:
                        A([Bbank[3]], [Bcst[par]], lambda e: e.activation(out=cst[par][:, 4:8], in_=banks[3][:, 400 + 4 * par:404 + 4 * par], func=AF.Copy))
                        A([BZ[par]], [Bcst[par]], lambda e: e.activation(out=cst[par][:, 0:4], in_=Z[par][:, :, LS - 1], func=AF.Copy))
                        G([Bcst[par], Btab], [Bcst[par]], lambda e: e.tensor_tensor(out=cst[par][:], in0=cst[par][:], in1=CLN[:, q, :], op=ALU.mult))
                        G([Bcst[par]], [Bcar[q]], lambda e: e.tensor_tensor(out=carry[:, gs], in0=cst[par][:, 0:4], in1=cst[par][:, 4:8], op=ALU.add))
                        return
                    for j in range(4):
                        g = q * 4 + j
                        V([Bt1[par], Bcar[q]], [BZ[par]], lambda e: e.tensor_tensor_scan(
                            out=Z[par][:, j, 0:LS], data0=Rm[:, g:g + 1].to_broadcast([128, LS]),
                            data1=t1[par][:, j * 128:j * 128 + LS], initial=carry[:, g:g + 1],
                            op0=ALU.mult, op1=ALU.add))
                    T([BZ[par], Bswp], [Bbank[3]], lambda e: e.matmul(
                        banks[3][:, 400 + 4 * par:404 + 4 * par], lhsT=swp[:], rhs=Z[par][:, :, LS - 1], start=True, stop=True))

                def s5B2(t, q, par, own, part):
                    i = t % 2
                    gb = q // 2
                    gs = slice(q * 4, q * 4 + 4)
                    if own and part == 0:
                        G([BZ[par], Btab], [BZ12[par]], lambda e: e.tensor_tensor(
                            out=Z1[par][:].rearrange("p a b -> p (a b)"), in0=Z[par][:].rearrange("p a b -> p (a b)"),
                            in1=COS[:, gs, :].rearrange("p a b -> p (a b)"), op=ALU.mult))
                        G([BZ[par], Btab], [BZ12[par]], lambda e: e.tensor_tensor(
                            out=Z2[par][:].rearrange("p a b -> p (a b)"), in0=Z[par][:].rearrange("p a b -> p (a b)"),
                            in1=SIN[:, gs, :].rearrange("p a b -> p (a b)"), op=ALU.mult))

                    if own and part == 1:
                        def mmY(e):
                            o = banks[2][:, gb * 128:(gb + 1) * 128]
                            if q % 2 == 0:
                                e.matmul(o, lhsT=diagD[:, gb, :], rhs=uT[i][:, gb, :], start=True, stop=False)
                            for j in range(4):
                                e.matmul(o, lhsT=Cw1[:, q * 4 + j, :], rhs=Z1[par][:, j, :], start=False, stop=False)
                                r = e.matmul(o, lhsT=Cw2[:, q * 4 + j, :], rhs=Z2[par][:, j, :], start=False,
                                             stop=(q % 2 == 1 and j == 3))
                            return r
                        T([BZ12[par], Bw5, BuT[i]], [Bbank[2]], mmY)

                def s5end(t):
                    oc = (t - OWN0) * 128
                    A([Bbank[2]], [Bgl], lambda e: e.activation(
                        out=gl[:].rearrange("p a b -> p (a b)"), in_=banks[2][:, :], func=AF.Gelu_apprx_tanh))
                    G([Bgl], [Bgh], lambda e: e.tensor_scalar(
                        out=gh[:].rearrange("p a b -> p (a b)"), in0=gl[:].rearrange("p a b -> p (a b)"),
                        scalar1=0.5, scalar2=0.0, op0=ALU.mult, op1=ALU.add))

                    def mmG(e):
                        for mo in range(4):
                            for k in range(4):
                                r = e.matmul(banks[2][:, mo * 128:(mo + 1) * 128],
                                             lhsT=w_glu[:, k, mo * 128:(mo + 1) * 128], rhs=gl[:, k, :],
                                             start=(k == 0), stop=(k == 3))
                        return r
                    T([Bgl, Bwg], [Bbank[2]], mmG)
                    for mo in range(4):
                        A([Bbank[2], Bdl], [Bth], lambda e: e.activation(
                            out=th[:, mo, :], in_=banks[2][:, mo * 128:(mo + 1) * 128], func=AF.Tanh,
                            scale=0.5, bias=bglu[:, mo:mo + 1]))
                    V([Bth, Bgh], [BobT], lambda e: e.scalar_tensor_tensor(
                        out=obT[:, :, oc:oc + 128], in0=th[:], scalar=1.0, in1=gh[:], op0=ALU.add, op1=ALU.mult))

                units = [(t, q) for t in range(I.KVC, NW) for q in range(8)]
                NU = len(units)
                if I.KVC > 0:
                    load_tile(0)
                if I.KVC > 0:
                    cst_ = []
                    for t in range(I.KVC):
                        stp, post = stage1_steps(t)
                        cst_.append([stp[0], stp[7], post[0:1], post[1:3], post[3:4], post[4:5], post[5:8], post[8:10]])
                    for step in range(I.KVC + 8):
                        for k in reversed(range(8)):
                            ti = step - k
                            if 0 <= ti < I.KVC:
                                for fn in cst_[ti][k]:
                                    fn()
                stage1(I.KVC)
                pend_post = None
                cur_steps = None
                def unit(ix):
                    return units[ix] if 0 <= ix < NU else None
                for it in range(-3, NU + 2):
                    u3, u2, u1, u0, um = unit(it + 3), unit(it + 2), unit(it + 1), unit(it), unit(it - 1)
                    if u2 is not None:
                        ta_, qa_ = u2
                        if ta_ + 1 < NW:
                            if qa_ == 0:
                                cur_steps = stage1_steps(ta_ + 1)
                            for fn in cur_steps[0][qa_]:
                                fn()
                        if qa_ < 5 and pend_post is not None:
                            for fn in pend_post[qa_ * 2:qa_ * 2 + 2]:
                                fn()
                        if qa_ == 7 and ta_ + 1 < NW:
                            pend_post = cur_steps[1]
                    if u0 is not None:
                        s5B2(u0[0], u0[1], it % 2, u0[0] >= OWN0, 0)
                    if u3 is not None and (u3[1] != 0 or u3[0] == units[0][0] or True):
                        s5A(u3[0], u3[1], (it + 3) % 2, 0)
                    if u2 is not None:
                        s5A(u2[0], u2[1], (it + 2) % 2, 1)
                    if u1 is not None:
                        s5B(u1[0], u1[1], (it + 1) % 2, u1[0] >= OWN0, 0)
                    if u0 is not None:
                        s5B(u0[0], u0[1], it % 2, u0[0] >= OWN0, 1)
                    if um is not None:
                        s5B2(um[0], um[1], (it - 1) % 2, um[0] >= OWN0, 1)
                        if um[1] == 7 and um[0] >= OWN0:
                            s5end(um[0])
                for q in range(8):
                    if Bcar[q].w is not None:
                        Bcarry.r.append(Bcar[q].w)
                S.dma("sp", I.s5out, carry[:], Bcar, [])
                S.barrier()
            if stop_after <= 1:
                pk.close()
                return
            QT = sb([96, NH, NOWN * 128], BF16, pk); BQT = Buf("QT")
            with ExitStack() as p2:
                g_qa = sb([128, DIN_Q], F32, p2); gqk = sb([128, QKD], F32, p2); gkh = sb([128, QKD], F32, p2)
                S.dma("sp", g_qa[:], g_qa_d, [], [Bg])
                S.dma("sp", gqk[:], g_qh_d, [], [Bg])
                S.dma("sp", gkh[:], g_kh_d, [], [Bg])
                V([Bg], [Bg], lambda e: e.tensor_tensor(out=gqk[:], in0=gqk[:], in1=gkh[:], op=ALU.mult))
                w_in_q = sb([128, 8, DIN_Q], BF16, p2); Bwq = Buf("w_in_q")
                w_qb = sb([128, 3, NH * QKD], BF16, p2); Bwqb = Buf("w_qb")
                S.dma("pool", w_in_q[:], w_in.rearrange("(k p) n -> p k n", p=128)[:, :, 0:DIN_Q], [], [Bwq])
                S.dma("pool", w_qb[:], w_q_b.rearrange("(k p) n -> p k n", p=128), [], [Bwqb])
                C4 = 4
                CC = 12

                def mk(n, shape, dt, nm):
                    return [sb(shape, dt, p2) for _ in range(n)], [Buf("%s%d" % (nm, k)) for k in range(n)]
                xf, Bxf = mk(C4, [128, D], F32, "xf")
                cs_t, Bcs = mk(CC, [128, 2, 16], F32, "cs")
                xs_b, Bxs = mk(C4, [128, D], BF16, "xs")
                xTq, BxTq = mk(C4, [128, 8, 128], BF16, "xTq")
                stq, Bstq = mk(C4, [128, 8], F32, "st")
                cq, Bcq = mk(C4, [128, DIN_Q], BF16, "cq")
                cqT, BcqT = mk(C4, [128, 3, 128], BF16, "cqT")
                qf, Bqf = mk(C4, [128, NH, QKD], F32, "qf")
                qb, Bqb = mk(C4, [128, NH, QKD], BF16, "qb")
                rt, Brt = mk(C4, [128, 4, NH, 16], F32, "rt")
                sq8, Bsq8 = mk(C4, [128, 8], F32, "sq8")
                junk2 = sb([128, 768], F32, p2); Bjunk2 = Buf("junk2")
                hg = ((3, 0, 5), (4, 5, 3))

                def q_stages(t):
                    c = t % C4
                    cc = t % CC
                    oc = (t - OWN0) * 128
                    tt = t - I.KVC
                    st = stq[c]; Bst = Bstq[c]
                    cs_, sn_ = cs_t[cc][:, 0, :], cs_t[cc][:, 1, :]

                    def s0():
                        S.dma("sp", xf[c][:], I.x[tt * 128:(tt + 1) * 128, :], [], [Bxf[c]])
                        S.dma("sp", cs_t[cc][:, 0, :], I.cos[:, tt, :], [], [Bcs[cc]])
                        S.dma("sp", cs_t[cc][:, 1, :], I.sin[:, tt, :], [], [Bcs[cc]])

                    def s1():
                        A([Bxf[c]], [Bjunk, Bst], lambda e: e.activation(out=junk[:], in_=xf[c][:], func=AF.Square, accum_out=st[:, 0:1]))
                        rstd_from(st[:, 0:1], Bst, 1, D, st[:, 1:2], Bst)

                    def s2():
                        V([Bxf[c], Bst, Bg], [Bxs[c]], lambda e: e.scalar_tensor_tensor(
                            out=xs_b[c][:], in0=xf[c][:], scalar=st[:, 1:2], in1=g_mix[:], op0=ALU.mult, op1=ALU.mult))

                    def s3():
                        def tr(e):
                            for k in range(8):
                                r = e.transpose(bankb[0][:, k * 128:(k + 1) * 128], xs_b[c][:, k * 128:(k + 1) * 128], ident[:])
                            return r
                        T([Bxs[c], Bident], [Bbank[0]], tr)
                        A([Bbank[0]], [BxTq[c]], lambda e: e.activation(
                            out=xTq[c][:].rearrange("p a b -> p (a b)"), in_=bankb[0][:, 0:1024], func=AF.Copy))

                    def s4():
                        def mm_q(e):
                            for k in range(8):
                                r = e.matmul(banks[1][:, 0:DIN_Q], lhsT=xTq[c][:, k, :], rhs=w_in_q[:, k, :], start=(k == 0), stop=(k == 7))
                            return r
                        T([BxTq[c], Bwq], [Bbank[1]], mm_q)
                        A([Bbank[1]], [Bjunk, Bst], lambda e: e.activation(
                            out=junk[:, 0:DIN_Q], in_=banks[1][:, 0:DIN_Q], func=AF.Square, accum_out=st[:, 2:3]))
                        rstd_from(st[:, 2:3], Bst, 1, DIN_Q, st[:, 3:4], Bst)

                    def s5():
                        V([Bbank[1], Bst, Bg], [Bcq[c]], lambda e: e.scalar_tensor_tensor(
                            out=cq[c][:], in0=banks[1][:, 0:DIN_Q], scalar=st[:, 3:4], in1=g_qa[:], op0=ALU.mult, op1=ALU.mult))

                        def tr3(e):
                            for k in range(3):
                                r = e.transpose(bankb[2][:, k * 128:(k + 1) * 128], cq[c][:, k * 128:(k + 1) * 128], ident[:])
                            return r
                        T([Bcq[c], Bident], [Bbank[2]], tr3)
                        V([Bbank[2]], [BcqT[c]], lambda e: e.tensor_copy(
                            out=cqT[c][:].rearrange("p a b -> p (a b)"), in_=bankb[2][:, 0:384]))

                    def s6():
                        def mm_qb(e):
                            for (bk, h0, nh_) in hg:
                                for k in range(3):
                                    r = e.matmul(banks[bk][:, 0:nh_ * QKD], lhsT=cqT[c][:, k, :],
                                                 rhs=w_qb[:, k, h0 * QKD:(h0 + nh_) * QKD], start=(k == 0), stop=(k == 2))
                            return r
                        T([BcqT[c], Bwqb], [Bbank[3], Bbank[4]], mm_qb)
                        for (bk, h0, nh_) in hg:
                            pv = banks[bk][:, 0:nh_ * QKD].rearrange("p (h d) -> p h d", d=QKD)
                            hs = slice(h0, h0 + nh_)
                            A([Bbank[bk]], [Bqf[c]], lambda e: e.activation(out=qf[c][:, hs, :], in_=pv, func=AF.Copy))

                    def s7():
                        x1 = qf[c][:, :, 64:80]; x2 = qf[c][:, :, 80:96]
                        cb = bc_mid(cs_, NH); sbb = bc_mid(sn_, NH)
                        V([Bqf[c], Bcs[cc]], [Brt[c]], lambda e: e.tensor_tensor(out=rt[c][:, 0, :, :], in0=x1, in1=cb, op=ALU.mult))
                        V([Bqf[c], Bcs[cc]], [Brt[c]], lambda e: e.tensor_tensor(out=rt[c][:, 1, :, :], in0=x2, in1=sbb, op=ALU.mult))
                        V([Bqf[c], Bcs[cc]], [Brt[c]], lambda e: e.tensor_tensor(out=rt[c][:, 2, :, :], in0=x2, in1=cb, op=ALU.mult))
                        V([Bqf[c], Bcs[cc]], [Brt[c]], lambda e: e.tensor_tensor(out=rt[c][:, 3, :, :], in0=x1, in1=sbb, op=ALU.mult))
                        G([Brt[c]], [Bqf[c]], lambda e: e.tensor_tensor(out=qf[c][:, :, 64:80], in0=rt[c][:, 0, :, :], in1=rt[c][:, 1, :, :], op=ALU.subtract))
                        G([Brt[c]], [Bqf[c]], lambda e: e.tensor_tensor(out=qf[c][:, :, 80:96], in0=rt[c][:, 2, :, :], in1=rt[c][:, 3, :, :], op=ALU.add))

                    def s8():
                        A([Bqf[c]], [Bjunk2], lambda e: e.activation(
                            out=junk2[:, 0:768], in_=qf[c][:].rearrange("p a b -> p (a b)"), func=AF.Square))
                        V([Bjunk2], [Bsq8[c]], lambda e: e.tensor_reduce(
                            out=sq8[c][:], in_=junk2[:, 0:768].rearrange("p (a b) -> p a b", b=QKD), axis=AX.X, op=ALU.add))
                        rstd_from(sq8[c][:], Bsq8[c], 8, QKD, sq8[c][:], Bsq8[c])

                    def s9():
                        V([Bqf[c], Bsq8[c]], [Bqf[c]], lambda e: e.tensor_tensor(out=qf[c][:], in0=qf[c][:], in1=bc_last(sq8[c][:], QKD), op=ALU.mult))
                        V([Bqf[c], Bg], [Bqb[c]], lambda e: e.tensor_tensor(out=qb[c][:], in0=qf[c][:], in1=bc_mid(gqk[:], NH), op=ALU.mult))

                    def s10():
                        def tr8(e):
                            for h in range(NH):
                                r = e.transpose(bankb[5][0:96, h * 128:(h + 1) * 128], qb[c][:, h, :], ident[:])
                            return r
                        T([Bqb[c], Bident], [Bbank[5]], tr8)
                        V([Bbank[5]], [BQT], lambda e: e.tensor_copy(
                            out=QT[:, :, oc:oc + 128], in_=bankb[5][0:96, 0:1024].rearrange("p (a b) -> p a b", b=128)))
                    return [s0, s1, s2, s3, s4, s5, s6, s7, s8, s9, s10]

                tiles = list(range(OWN0, NW))
                stg = [q_stages(t) for t in tiles]
                NSTG = 11
                for step in range(len(tiles) + NSTG):
                    for k in reversed(range(NSTG)):
                        ti = step - k
                        if 0 <= ti < len(tiles):
                            stg[ti][k]()
                S.barrier()
            if dbg and I.name == "p":
                S.dma("sp", di["dbg_ob"], obT[:], [BobT], [])
                S.dma("sp", di["dbg_q"], QT[:], [BQT], [])
            if stop_after <= 2:
                pk.close()
                return
            with ExitStack() as p3:
                NK = NW * 128
                ktb = [sb([96, NK], BF16, p3) for _ in range(2)]; Bktb = [Buf("ktb0"), Buf("ktb1")]
                vxb = [sb([128, NW, 128], BF16, p3) for _ in range(2)]; Bvxb = [Buf("vxb0"), Buf("vxb1")]
                vld = sb([128, NW], F32, p3); Bvld = Buf("vld")
                PT = [sb([128, 512], BF16, p3) for _ in range(4)]; BPT = [Buf("PT%d" % i) for i in range(4)]
                msk = sb([128, 4, 512], BF16, p3); Bmsk = Buf("msk")
                rec = sb([128, 512], F32, p3); Brec = Buf("rec")
                S.dma("sp", vld[:], I.valid, [], [Bvld])
                if I.masked:
                    S.dma("pool", msk[:], masks_d, [], [Bmsk])
                allkr = BkrotT[:NW]
                A(allkr, [Bktb[0]], lambda e: e.activation(out=ktb[0][64:96, :], in_=krotT[64:96, 0:NK], func=AF.Copy))
                V(allkr, [Bktb[1]], lambda e: e.tensor_copy(out=ktb[1][64:96, :], in_=krotT[64:96, 0:NK]))
                for b_ in range(2):
                    ooff = 64 if b_ == 0 else 0
                    V([Bvld], [Bvxb[b_]], lambda e: e.tensor_copy(
                        out=vxb[b_][:, :, ooff:ooff + 64], in_=bc_last(vld[:], 64)))
                Bscr = [Buf("scr%d" % k) for k in range(8)]
                P3S = int(os.environ.get("P3S", "99"))
                if I.name == "p" and P3S >= 1:
                    for k in range(8):
                        S.dma("pool", scr_up[k * 128:(k + 1) * 128, :], w_up_d[k * 128:(k + 1) * 128, :], [], [Bscr[k]])
                    I.Bscr = Bscr
                qblocks = []
                t = OWN0
                while t < NW:
                    t1_ = min(NW, (t // 4 + 1) * 4)
                    qblocks.append((t, t1_))
                    t = t1_
                PSB = [0, 1, 4, 5]

                def kv_tasks(h):
                    b_ = h % 2
                    voff = 0 if b_ == 0 else 64
                    tasks = []
                    c0 = 0
                    while c0 < NK:
                        n = min(512, NK - c0)

                        def tk(c0=c0, n=n):
                            def mmK(e):
                                for k in range(2):
                                    r = e.matmul(banks[6][0:64, 0:n], lhsT=w_kvb[:, k, h * 128:h * 128 + 64],
                                                 rhs=ckvT[:, k, c0:c0 + n], start=(k == 0), stop=(k == 1))
                                return r
                            T(BckvT[c0 // 128:(c0 + n) // 128] + [Bwkvb], [Bbank[6]], mmK)
                            V([Bbank[6]], [Bktb[b_]], lambda e: e.tensor_copy(out=ktb[b_][0:64, c0:c0 + n], in_=banks[6][0:64, 0:n]))
                        tasks.append(tk)
                        c0 += n
                    t0 = 0
                    while t0 < NW:
                        nt = min(4, NW - t0)

                        def tv(t0=t0, nt=nt):
                            def mmV(e):
                                for j in range(nt):
                                    for k in range(2):
                                        r = e.matmul(banks[7][:, j * 64:(j + 1) * 64],
                                                     lhsT=ckvT[:, k, (t0 + j) * 128:(t0 + j + 1) * 128],
                                                     rhs=w_kvb[:, k, h * 128 + 64:h * 128 + 128], start=(k == 0), stop=(k == 1))
                                return r
                            T(BckvT[t0:t0 + nt] + [Bwkvb], [Bbank[7]], mmV)
                            V([Bbank[7]], [Bvxb[b_]], lambda e: e.tensor_copy(
                                out=vxb[b_][:, t0:t0 + nt, voff:voff + 64],
                                in_=banks[7][:, 0:nt * 64].rearrange("p (a b) -> p a b", b=64)))
                        tasks.append(tv)
                        t0 += nt
                    return tasks

                def produce_kv(h):
                    for tk in kv_tasks(h):
                        tk()

                steps = []
                for h in range(NH):
                    for qi, (ta, tb) in enumerate(qblocks):
                        for kt in range(tb):
                            steps.append((h, qi, ta, tb, kt))
                NS = len(steps)
                LAG = 3
                gidx = {}
                for (h, qi, ta, tb, kt) in steps:
                    gidx.setdefault((h, qi), len(gidx))
                hstart = {}
                for jj, st_ in enumerate(steps):
                    hstart.setdefault(st_[0], jj)
                pending = []
                if P3S >= 2:
                    produce_kv(0)
                for j in range(NS + LAG):
                    if P3S < 3:
                        break
                    if j < NS:
                        h, qi, ta, tb, kt = steps[j]
                        b_ = h % 2
                        nq = (tb - ta) * 128
                        qc = (ta - OWN0) * 128
                        sbk = PSB[j % 4]
                        pi = j % 4
                        if j == hstart[h] + LAG and h + 1 < NH:
                            pending = kv_tasks(h + 1)
                        if pending:
                            pending.pop(0)()
                        off = (kt - ta) * 128 if (I.masked and kt > ta) else 0
                        T([Bktb[b_], BQT], [Bbank[sbk]], lambda e: e.matmul(
                            banks[sbk][:, off:nq], lhsT=ktb[b_][:, kt * 128:(kt + 1) * 128],
                            rhs=QT[:, h, qc + off:qc + nq], start=True, stop=True))
                        A([Bbank[sbk], Bkscale[kt]], [BPT[pi]], lambda e: e.activation(
                            out=PT[pi][:, off:nq], in_=banks[sbk][:, off:nq], func=AF.Exp, scale=kscale[:, kt, h:h + 1]))
                        if I.masked and kt >= ta:
                            V([BPT[pi], Bmsk], [BPT[pi]], lambda e: e.tensor_tensor(
                                out=PT[pi][:, off:nq], in0=PT[pi][:, off:nq], in1=msk[:, kt - ta, off:nq], op=ALU.mult))
                    jp = j - LAG
                    if jp >= 0:
                        h, qi, ta, tb, kt = steps[jp]
                        b_ = h % 2
                        nq = (tb - ta) * 128
                        qc = (ta - OWN0) * 128
                        ob = 2 + gidx[(h, qi)] % 2
                        pp_ = jp % 4
                        off = (kt - ta) * 128 if (I.masked and kt > ta) else 0
                        T([Bvxb[b_], BPT[pp_]], [Bbank[ob]], lambda e: e.matmul(
                            banks[ob][:, off:nq], lhsT=vxb[b_][:, kt, :], rhs=PT[pp_][:, off:nq],
                            start=(kt == 0), stop=(kt == tb - 1)))
                        if kt == tb - 1:
                            dlo = 64 if b_ == 0 else 0
                            olo = 0 if b_ == 0 else 64
                            V([Bbank[ob]], [Brec], lambda e: e.tensor_scalar(
                                out=rec[dlo:dlo + 64, 0:nq], in0=banks[ob][dlo:dlo + 64, 0:nq], scalar1=1e-30, scalar2=None, op0=ALU.max))
                            V([Brec], [Brec], lambda e: e.reciprocal(out=rec[dlo:dlo + 64, 0:nq], in_=rec[dlo:dlo + 64, 0:nq]))
                            V([Bbank[ob], Brec], [BoaT], lambda e: e.tensor_tensor(
                                out=oaT[olo:olo + 64, h // 2, qc:qc + nq], in0=banks[ob][olo:olo + 64, 0:nq],
                                in1=rec[dlo:dlo + 64, 0:nq], op=ALU.mult))
                S.barrier()
            if dbg and I.name == "p":
                S.dma("sp", di["dbg_oa"], oaT[:], [BoaT], [])
            pk.close()
            if stop_after <= 3:
                return
            yield
            if "v" not in P4:
                p4 = ExitStack(); P4["stack"] = p4
                g_ffn = sb([128, D], F32, p4)
                S.dma("sp", g_ffn[:], g_ffn_d, [], [Bg])
                w_out = sb([128, 8, D], BF16, p4); Bwo = Buf("w_out")
                w_dn = sb([128, NFF, D], BF16, p4); Bwd = [Buf("w_dn%d" % i) for i in range(NFF)]
                S.dma("pool", w_out[:], w_out_d.rearrange("(k p) n -> p k n", p=128), [], [Bwo])
                P4["wdn_pending"] = True
                wdw = sb([128, 2 * NFF, 3], F32, p4); bdw = sb([128, 2 * NFF], F32, p4); Bdw = Buf("dw")
                S.dma("sp", wdw[:], wdw_d, [], [Bdw])
                S.dma("sp", bdw[:], bdw_d, [], [Bdw])
                tail = sb([128, 2 * NFF, 2], F32, p4); Btail = Buf("tail")
                corr = sb([128, 2 * NFF, 2], F32, p4); Bcorr = Buf("corr")
                wu = [sb([128, 8, 512], BF16, p4) for _ in range(3)]; Bwu = [Buf("wu%d" % i) for i in range(3)]
                xr = [sb([128, D], F32, p4) for _ in range(2)]; Bxr = [Buf("xr0"), Buf("xr1")]
                hf_ = sb([128, 4, D], F32, p4); Bhf = [Buf("hf%d" % i) for i in range(4)]
                hn = [sb([128, D], BF16, p4) for _ in range(2)]; Bhn = [Buf("hn0"), Buf("hn1")]
                hnT = sb([128, 8, 512], BF16, p4); BhnT = Buf("hnT")
                aT = sb([128, NFF, 512], BF16, p4); BaT = [Buf("aT%d" % i) for i in range(NFF)]
                st = sb([128, 8], F32, p4); Bst = Buf("st")
                cg = [sb([128, 512], F32, p4) for _ in range(3)]; Bcg = [Buf("cg%d" % k) for k in range(3)]
                cv = [sb([128, 512], F32, p4) for _ in range(3)]; Bcv = [Buf("cv%d" % k) for k in range(3)]
                sg = sb([128, 512], F32, p4); Bsg = Buf("sg")
                yo = [sb([128, D], F32, p4) for _ in range(1)]; Byo = [Buf("yo0")]
                P4["v"] = (g_ffn, w_out, Bwo, w_dn, Bwd, wdw, bdw, Bdw, tail, Btail, wu, Bwu, xr, Bxr, hf_, Bhf,
                           hn, Bhn, hnT, BhnT, aT, BaT, st, Bst, cg, Bcg, cv, Bcv, sg, Bsg, yo, Byo, [0], corr, Bcorr)
            (g_ffn, w_out, Bwo, w_dn, Bwd, wdw, bdw, Bdw, tail, Btail, wu, Bwu, xr, Bxr, hf_, Bhf,
             hn, Bhn, hnT, BhnT, aT, BaT, st, Bst, cg, Bcg, cv, Bcv, sg, Bsg, yo, Byo, wui, corr, Bcorr) = P4["v"]
            if True:
                if I.convh is None:
                    V([], [Btail], lambda e: e.memset(tail[:], 0.0))
                else:
                    S.dma("sp", tail[:], I.convh, [], [Btail])
                qblocks = []
                t = OWN0
                while t < NW:
                    t1_ = min(NW, (t // 4 + 1) * 4)
                    qblocks.append((t, t1_))
                    t = t1_
                first_block = True
                for (ta, tb) in qblocks:
                    ntl = tb - ta
                    nq = ntl * 128
                    qc = (ta - OWN0) * 128
                    def wout_head(j):
                        t = ta + j
                        i = t % 2
                        tt = t - I.KVC
                        hb = 2 * (j % 2)
                        S.dma("sp", xr[i][:], I.x[tt * 128:(tt + 1) * 128, :], [], [Bxr[i]])

                        def mmH(e):
                            for nh_ in range(2):
                                for k in range(8):
                                    src = oaT if k < 4 else obT
                                    r = e.matmul(banks[hb + nh_][:, :], lhsT=src[:, k % 4, qc + j * 128:qc + (j + 1) * 128],
                                                 rhs=w_out[:, k, nh_ * 512:(nh_ + 1) * 512], start=(k == 0), stop=(k == 7))
                            return r
                        T([BoaT, BobT, Bwo], [Bbank[hb], Bbank[hb + 1]], mmH)
                        for nh_ in range(2):
                            V([Bbank[hb + nh_], Bxr[i]], [Bhf[j]], lambda e: e.tensor_tensor(
                                out=hf_[:, j, nh_ * 512:(nh_ + 1) * 512], in0=banks[hb + nh_][:, :],
                                in1=xr[i][:, nh_ * 512:(nh_ + 1) * 512], op=ALU.add))
                        A([Bhf[j]], [Bjunk, Bst], lambda e: e.activation(out=junk[:], in_=hf_[:, j, :], func=AF.Square, accum_out=st[:, 2 * (j % 2):2 * (j % 2) + 1]))
                        rstd_from(st[:, 2 * (j % 2):2 * (j % 2) + 1], Bst, 1, D, st[:, 2 * (j % 2) + 1:2 * (j % 2) + 2], Bst)
                        V([Bhf[j], Bst, Bg], [Bhn[j % 2]], lambda e: e.scalar_tensor_tensor(
                            out=hn[j % 2][:], in0=hf_[:, j, :], scalar=st[:, 2 * (j % 2) + 1:2 * (j % 2) + 2], in1=g_ffn[:], op0=ALU.mult, op1=ALU.mult))

                    def wout_tail(j):
                        def trh(e):
                            for k in range(8):
                                r = e.transpose(bankb[4][:, k * 128:(k + 1) * 128], hn[j % 2][:, k * 128:(k + 1) * 128], ident[:])
                            return r
                        T([Bhn[j % 2], Bident], [Bbank[4]], trh)
                        A([Bbank[4]], [BhnT], lambda e: e.activation(
                            out=hnT[:, :, j * 128:(j + 1) * 128], in_=bankb[4][:, 0:1024].rearrange("p (a b) -> p a b", b=128),
                            func=AF.Copy))
                    for j in range(ntl + 1):
                        if j < ntl:
                            wout_head(j)
                        if j >= 1:
                            wout_tail(j - 1)
                    V([Btail, Bdw], [Bcorr], lambda e: e.tensor_tensor(out=corr[:, :, 0], in0=tail[:, :, 0], in1=wdw[:, :, 0], op=ALU.mult))
                    V([Btail, Bdw], [Bcorr], lambda e: e.tensor_tensor(out=corr[:, :, 1], in0=tail[:, :, 1], in1=wdw[:, :, 1], op=ALU.mult))
                    V([Bcorr], [Bcorr], lambda e: e.tensor_tensor(out=corr[:, :, 0], in0=corr[:, :, 0], in1=corr[:, :, 1], op=ALU.add))
                    V([Btail, Bdw, Bcorr], [Bcorr], lambda e: e.tensor_tensor(out=corr[:, :, 1], in0=tail[:, :, 1], in1=wdw[:, :, 0], op=ALU.mult))
                    for f in range(NFF):
                        if P4.get("wdn_pending"):
                            S.dma("pool", w_dn[:, f, :], w_down_d[f * 128:(f + 1) * 128, :], [], [Bwd[f]])
                            if f == NFF - 1:
                                P4["wdn_pending"] = False
                        if f % 2 == 0:
                            wui[0] += 1
                            wi = wui[0] % 3
                            scv = scr_up.rearrange("(k p) c -> p k c", p=128)
                            S.dma("sp", wu[wi][:, :, 0:256], scv[:, :, f * 128:f * 128 + 256], I.Bscr, [Bwu[wi]])
                            S.dma("sp", wu[wi][:, :, 256:512], scv[:, :, DFF + f * 128:DFF + f * 128 + 256], I.Bscr, [Bwu[wi]])
                        wi = wui[0] % 3
                        bg = 2 + 2 * (f % 3)
                        bv = bg + 1
                        jo = (f % 2) * 128

                        def mmU(e):
                            for (bk, c0) in ((bg, jo), (bv, 256 + jo)):
                                for k in range(8):
                                    r = e.matmul(banks[bk][:, 0:nq], lhsT=wu[wi][:, k, c0:c0 + 128], rhs=hnT[:, k, 0:nq],
                                                 start=(k == 0), stop=(k == 7))
                            return r
                        T([Bwu[wi], BhnT], [Bbank[bg], Bbank[bv]], mmU)
                        ci = f % 3
                        for (bk, ch, dst, Bdst) in ((bg, f, cg[ci], Bcg[ci]), (bv, NFF + f, cv[ci], Bcv[ci])):
                            ps_ = banks[bk]
                            A([Bbank[bk], Bdw], [Bdst], lambda e: e.activation(
                                out=dst[:, 0:nq], in_=ps_[:, 0:nq], func=AF.Identity, scale=wdw[:, ch, 2:3], bias=bdw[:, ch:ch + 1]))
                            V([Bbank[bk], Bdw, Bdst], [Bdst], lambda e: e.scalar_tensor_tensor(
                                out=dst[:, 1:nq], in0=ps_[:, 0:nq - 1], scalar=wdw[:, ch, 1:2], in1=dst[:, 1:nq], op0=ALU.mult, op1=ALU.add))
                            V([Bbank[bk], Bdw, Bdst], [Bdst], lambda e: e.scalar_tensor_tensor(
                                out=dst[:, 2:nq], in0=ps_[:, 0:nq - 2], scalar=wdw[:, ch, 0:1], in1=dst[:, 2:nq], op0=ALU.mult, op1=ALU.add))
                            G([Bcorr, Bdst], [Bdst], lambda e: e.tensor_tensor(
                                out=dst[:, 0:2], in0=dst[:, 0:2], in1=corr[:, ch, :], op=ALU.add))
                            e0 = I.tail_end(ta, tb)
                            A([Bbank[bk], Btail], [Btail], lambda e: e.activation(out=tail[:, ch, :], in_=ps_[:, e0 - 2:e0], func=AF.Copy))
                        A([Bcg[ci]], [Bsg], lambda e: e.activation(out=sg[:, 0:nq], in_=cg[ci][:, 0:nq], func=AF.Silu))
                        G([Bsg, Bcv[ci]], [BaT[f]], lambda e: e.tensor_tensor(
                            out=aT[:, f, 0:nq], in0=sg[:, 0:nq], in1=cv[ci][:, 0:nq], op=ALU.mult))
                    if first_block and I.name == "p":
                        V([Btail, Bhval], [Btail], lambda e: e.tensor_scalar(
                            out=tail[:].rearrange("p a b -> p (a b)"), in0=tail[:].rearrange("p a b -> p (a b)"),
                            scalar1=hval[:, 0:1], scalar2=None, op0=ALU.mult))
                    first_block = False
                    for j in range(ntl):
                        t = ta + j
                        yi = 0

                        db = 2 * (j % 2)

                        for f0 in range(0, NFF, 6):
                            f1 = min(NFF, f0 + 6)

                            def mmD(e):
                                for f in range(f0, f1):
                                    for nh_ in range(2):
                                        r = e.matmul(banks[db + nh_][:, :], lhsT=aT[:, f, j * 128:(j + 1) * 128],
                                                     rhs=w_dn[:, f, nh_ * 512:(nh_ + 1) * 512], start=(f == 0), stop=(f == NFF - 1))
                                return r
                            T(BaT[f0:f1] + Bwd[f0:f1], [Bbank[db], Bbank[db + 1]], mmD)
                        for nh_ in range(2):
                            V([Bbank[db + nh_], Bhf[j]], [Byo[yi]], lambda e: e.tensor_tensor(
                                out=yo[yi][:, nh_ * 512:(nh_ + 1) * 512], in0=banks[db + nh_][:, :],
                                in1=hf_[:, j, nh_ * 512:(nh_ + 1) * 512], op=ALU.add))
                        if I.out_rows(t) is not None:
                            r0, n = I.out_rows(t)
                            S.dma("sp", I.y[r0:r0 + n, :], yo[yi][0:n, :], [Byo[yi]], [])
                S.dma("sp", I.convout, tail[:], [Btail], [])

        Ip = Inst()
        Ip.name = "p"; Ip.NW = 32; Ip.OWN0 = 15; Ip.KVC = 0; Ip.LS = 128
        Ip.x = xw; Ip.cos = cos_p; Ip.sin = sin_p; Ip.valid = valid_p; Ip.masked = True
        Ip.cache_ckv = None; Ip.cache_kpe = None; Ip.s5init = None; Ip.convh = None
        Ip.nkv = nkv_p; Ip.nkr = nkr_p; Ip.y = y_p; Ip.s5out = s5_p; Ip.convout = conv_p
        Ip.out_rows = lambda t: ((t - 16) * 128, 128) if t >= 16 else None
        Ip.tail_end = lambda ta, tb: (tb - ta) * 128
        Ip.oaT, Ip.obT = oaT_p, obT_p
        gens = [run_instance(Ip)]
        next(gens[0], None)
        if do_sample and stop_after > 4:
            Is = Inst()
            Is.name = "s"; Is.NW = 17; Is.OWN0 = 16; Is.KVC = 16; Is.LS = 16
            Is.x = xsw; Is.cos = cos_s; Is.sin = sin_s; Is.valid = valid_s; Is.masked = False
            Is.cache_ckv = cache_ckv; Is.cache_kpe = cache_kpe; Is.s5init = s5init_s; Is.convh = convh_s
            Is.nkv = nkv_s; Is.nkr = nkr_s; Is.y = y_s; Is.s5out = s5_s; Is.convout = conv_s
            Is.out_rows = lambda t: (0, DEC) if t == 16 else None
            Is.tail_end = lambda ta, tb: DEC
            Is.Bscr = Ip.Bscr
            Is.oaT, Is.obT = oaT_s, obT_s
            gens.append(run_instance(Is))
            next(gens[1], None)
        for g_ in gens:
            next(g_, None)
        S.barrier()
        if "stack" in P4:
            P4["stack"].close()
        S.barrier()
        print("ops", S.nops, "waits", S.nwait)
    return nc


def _rope_tables(pos):
    inv = 10000.0 ** (-np.arange(0, ROPE, 2, dtype=np.float32) / ROPE)
    ang = pos.astype(np.float32)[:, None] * inv[None, :]
    return np.cos(ang).astype(np.float32), np.sin(ang).astype(np.float32)


def make_in_maps(inp):
    f32 = lambda a: np.ascontiguousarray(a, dtype=np.float32)
    bcast = lambda v, n: f32(np.broadcast_to(np.asarray(v).reshape(1, -1), (128, n)))
    common = {}
    for k in ("w_in", "w_q_b", "w_kv_b", "w_s5_glu", "w_out", "w_up", "w_down"):
        common[k] = f32(inp[k][0])
    common["g_mix_b"] = bcast(inp["g_mix_norm"][0], D)
    common["g_qa_b"] = bcast(inp["g_q_a"][0], DIN_Q)
    common["g_kva_b"] = bcast(inp["g_kv_a"][0], DKV)
    common["g_ffn_b"] = bcast(inp["g_ffn_norm"][0], D)
    common["g_qh_b"] = bcast(inp["g_q_head"][0], QKD)
    common["g_kh_b"] = bcast(inp["g_k_head"][0], QKD)
    dup = lambda a: f32(np.concatenate([a.T, a.T], axis=0))
    common["s5_are"] = dup(inp["s5_a_re"][0])
    common["s5_aim"] = dup(inp["s5_a_im"][0])
    common["s5_ldt"] = bcast(inp["s5_log_dt"][0], NG)
    bre = np.transpose(inp["s5_b_re"][0], (1, 0, 2))
    bim = np.transpose(inp["s5_b_im"][0], (1, 0, 2))
    common["s5_bst"] = f32(np.concatenate([bre, bim], axis=0))
    common["s5_bsw"] = f32(np.concatenate([bim, bre], axis=0))
    cre = inp["s5_c_re"][0].reshape(4, 128, 64)
    cim = inp["s5_c_im"][0].reshape(4, 128, 64)
    common["s5_cn"] = f32(np.transpose(np.concatenate([cre, cim], axis=2), (1, 0, 2)))
    common["s5_cns"] = f32(np.transpose(np.concatenate([cim, cre], axis=2), (1, 0, 2)))
    common["s5_dl"] = f32(inp["s5_d"][0].reshape(4, 128).T)
    common["s5_bglu"] = f32(inp["b_s5_glu"][0].reshape(4, 128).T)
    common["w_dw_l"] = f32(np.transpose(inp["w_dw"][0].reshape(3, 2 * NFF, 128), (2, 1, 0)))
    common["b_dw_l"] = f32(inp["b_dw"][0].reshape(2 * NFF, 128).T)
    pidx = np.arange(128)
    common["mask8"] = f32((pidx[:, None] // 16) == np.arange(8)[None, :])
    sg = np.where(pidx < 64, -1.0, 1.0)
    common["sgn"] = f32(np.stack([sg, -sg], axis=1))
    kk = (np.arange(4)[None, :, None] * 128 + pidx[:, None, None]) // 64
    qq = np.arange(512)[None, None, :] // 64
    common["masks"] = f32(kk <= qq)
    cos_s, sin_s = _rope_tables(PAST + np.arange(128))
    common["cos_s"] = f32(cos_s[:, None, :])
    common["sin_s"] = f32(sin_s[:, None, :])
    vs = np.ones((128, 17), np.float32)
    vs[:, 16] = (pidx < DEC)
    common["valid_s"] = vs
    maps = []
    for c in range(8):
        b, h = c // 2, c % 2
        m = dict(common)
        xw = np.zeros((SEQ, D), np.float32)
        if h == 0:
            xw[HALF:] = inp["x_prompt"][b, :HALF]
            pos = np.arange(SEQ) - HALF
        else:
            xw[:] = inp["x_prompt"][b]
            pos = np.arange(SEQ)
        m["xw"] = xw
        cp, sp_ = _rope_tables(np.maximum(pos, 0))
        m["cos_p"] = f32(cp.reshape(32, 128, 16).transpose(1, 0, 2))
        m["sin_p"] = f32(sp_.reshape(32, 128, 16).transpose(1, 0, 2))
        m["valid_p"] = f32((pos >= 0).reshape(32, 128).T)
        m["hval"] = np.full((128, 1), float(h), np.float32)
        xs = np.zeros((128, D), np.float32)
        xs[:DEC] = inp["x_sample"][c]
        m["xsw"] = xs
        m["cache_ckv"] = f32(inp["cache_kv_latent"][0, c])
        m["cache_kpe"] = f32(inp["cache_k_rope"][0, c])
        m["s5init_s"] = f32(np.concatenate([inp["state_s5_re"][0, c].T, inp["state_s5_im"][0, c].T], axis=0))
        m["convh_s"] = f32(np.transpose(inp["state_ffn_conv"][0, c].reshape(2, 2 * NFF, 128), (2, 1, 0)))
        maps.append(m)
    return maps


_NC_CACHE = {}


def kernel(**inputs):
    inp = {k: np.asarray(v) for k, v in inputs.items()}
    if "nc" not in _NC_CACHE:
        _NC_CACHE["nc"] = build_program()
    nc = _NC_CACHE["nc"]
    maps = make_in_maps(inp)
    res = run_bass_kernel_spmd(nc, maps, core_ids=list(range(8)))
    R = res.results
    yp = np.zeros((4, SEQ, D), np.float32)
    nkv = np.zeros((1, 4, SEQ, DKV), np.float32)
    nkr = np.zeros((1, 4, SEQ, ROPE), np.float32)
    s5re = np.zeros((1, 4, NG, 64), np.float32); s5im = np.zeros((1, 4, NG, 64), np.float32)
    conv = np.zeros((1, 4, 2, 2 * DFF), np.float32)
    ys = np.zeros((8, DEC, D), np.float32)
    nkvs = np.zeros((1, 8, DEC, DKV), np.float32); nkrs = np.zeros((1, 8, DEC, ROPE), np.float32)
    s5res = np.zeros((1, 8, NG, 64), np.float32); s5ims = np.zeros((1, 8, NG, 64), np.float32)
    convs = np.zeros((1, 8, 2, 2 * DFF), np.float32)
    unconv = lambda a: np.transpose(a, (2, 1, 0)).reshape(2, 2 * DFF)
    for c in range(8):
        b, h = c // 2, c % 2
        r = R[c]
        yp[b, h * HALF:(h + 1) * HALF] = r["y_p"]
        nkv[0, b, h * HALF:(h + 1) * HALF] = r["nkv_p"]
        nkr[0, b, h * HALF:(h + 1) * HALF] = r["nkr_p"]
        if h == 1:
            s5re[0, b] = r["s5_p"][0:64].T
            s5im[0, b] = r["s5_p"][64:128].T
            conv[0, b] = unconv(r["conv_p"])
        ys[c] = r["y_s"]
        nkvs[0, c] = r["nkv_s"]; nkrs[0, c] = r["nkr_s"]
        s5res[0, c] = r["s5_s"][0:64].T; s5ims[0, c] = r["s5_s"][64:128].T
        convs[0, c] = unconv(r["conv_s"])
    return (yp, ys, nkv, nkr, s5re, s5im, conv, nkvs, nkrs, s5res, s5ims, convs)
```
